# Optimizing a Trainium2 kernel written in Bass

```python
import jax, jax.numpy as jnp
from jax import lax
import numpy as np

D_MODEL = 1024
BATCH = 4
SEQ = 8192
DEPTH = 4

N_META = 16
BLOCK = 128
PAD = BLOCK - N_META
MLA_HEADS = 8
MLA_NOPE = 64
MLA_ROPE = 32
MLA_QK = MLA_NOPE + MLA_ROPE
MLA_V = 64
Q_LORA = 384
KV_LORA = 256
ROPE_BASE = 10000.0
FOX_HEADS = 8
FOX_DIM = 64
CONV_WIDTH = 3
D_FF = 4 * D_MODEL
EPS = 1e-6
NEG = -1e30
N_EVEN = (DEPTH + 1) // 2
N_ODD = DEPTH // 2
ATTN_SPLITS = (Q_LORA, KV_LORA, MLA_ROPE, FOX_HEADS * FOX_DIM, FOX_HEADS * FOX_DIM, FOX_HEADS * FOX_DIM, FOX_HEADS)
ATTN_IN = Q_LORA + KV_LORA + MLA_ROPE + 3 * FOX_HEADS * FOX_DIM + FOX_HEADS
MIX_OUT = MLA_HEADS * MLA_V + FOX_HEADS * FOX_DIM

kernel_name = "hybrid_mla_fox_shortconv_trunk"


def _offsets(sizes):
    out, acc = [], 0
    for s in sizes[:-1]:
        acc += s
        out.append(acc)
    return out


def rms_norm(x, g):
    xf = x.astype(jnp.float32)
    y = xf * lax.rsqrt(jnp.mean(xf * xf, axis=-1, keepdims=True) + EPS)
    return (y * g.astype(jnp.float32)).astype(x.dtype)


def rope_tables(length):
    pos = jnp.arange(length, dtype=jnp.float32)
    inv_freq = ROPE_BASE ** (-jnp.arange(0, MLA_ROPE, 2, dtype=jnp.float32) / MLA_ROPE)
    ang = pos[:, None] * inv_freq[None, :]
    return jnp.cos(ang), jnp.sin(ang)


def rope_tail(x, cos, sin):
    x_nope = x[..., :MLA_NOPE]
    xr = x[..., MLA_NOPE:].astype(jnp.float32)
    x1, x2 = xr[..., : MLA_ROPE // 2], xr[..., MLA_ROPE // 2:]
    c, s = cos[None, :, None, :], sin[None, :, None, :]
    rot = jnp.concatenate([x1 * c - x2 * s, x2 * c + x1 * s], axis=-1).astype(x.dtype)
    return jnp.concatenate([x_nope, rot], axis=-1)


def pad_front(x):
    return jnp.pad(x, [(0, 0), (PAD, 0)] + [(0, 0)] * (x.ndim - 2))


def blocked_causal_attention(q, k, v, scale, cum_log_f=None):
    b, lp, h, dk = q.shape
    nb = lp // BLOCK
    key_pos = jnp.arange(lp)
    qb = q.reshape(b, nb, BLOCK, h, dk).transpose(1, 0, 2, 3, 4)
    use_decay = cum_log_f is not None
    if use_decay:
        f_bh = cum_log_f.transpose(0, 2, 1)
        f_q = f_bh.reshape(b, h, nb, BLOCK).transpose(2, 0, 1, 3)
        xs = (jnp.arange(nb), qb, f_q)
    else:
        xs = (jnp.arange(nb), qb)

    def one_block(args):
        if use_decay:
            i, q_blk, fq = args
        else:
            i, q_blk = args
        s = jnp.einsum('bqhd,bkhd->bhqk', q_blk, k, preferred_element_type=jnp.float32) * scale
        if use_decay:
            s = s + fq[..., :, None] - f_bh[:, :, None, :]
        q_pos = i * BLOCK + jnp.arange(BLOCK)
        mask = (key_pos[None, :] <= q_pos[:, None]) & (key_pos[None, :] >= PAD)
        s = jnp.where(mask[None, None], s, NEG)
        p = jax.nn.softmax(s, axis=-1)
        return jnp.einsum('bhqk,bkhd->bqhd', p.astype(v.dtype), v)

    out = lax.map(one_block, xs)
    return out.transpose(1, 0, 2, 3, 4).reshape(b, lp, h, v.shape[-1])


def attention_mixer(h, cos, sin, w_in, g_cq, w_uq, g_ckv, w_ukv, g_q_mla, g_k_mla,
                    g_q_fox, g_k_fox, b_forget, w_out):
    b, l, _ = h.shape
    z = h @ w_in
    c_q, c_kv, k_pe, fq, fk, fv, f_logit = jnp.split(z, _offsets(ATTN_SPLITS), axis=-1)

    q = (rms_norm(c_q, g_cq) @ w_uq).reshape(b, l, MLA_HEADS, MLA_QK)
    kv = (rms_norm(c_kv, g_ckv) @ w_ukv).reshape(b, l, MLA_HEADS, MLA_NOPE + MLA_V)
    k_nope, v_mla = kv[..., :MLA_NOPE], kv[..., MLA_NOPE:]
    k_rope = jnp.broadcast_to(k_pe[:, :, None, :], (b, l, MLA_HEADS, MLA_ROPE))
    k = jnp.concatenate([k_nope, k_rope], axis=-1)
    q = rope_tail(rms_norm(q, g_q_mla), cos, sin)
    k = rope_tail(rms_norm(k, g_k_mla), cos, sin)
    o_mla = blocked_causal_attention(pad_front(q), pad_front(k), pad_front(v_mla),
                                     MLA_QK ** -0.5)[:, PAD:]
    o_mla = o_mla.reshape(b, l, MLA_HEADS * MLA_V)

    qf = rms_norm(fq.reshape(b, l, FOX_HEADS, FOX_DIM), g_q_fox)
    kf = rms_norm(fk.reshape(b, l, FOX_HEADS, FOX_DIM), g_k_fox)
    vf = fv.reshape(b, l, FOX_HEADS, FOX_DIM)
    log_f = jax.nn.log_sigmoid(f_logit.astype(jnp.float32) + b_forget.astype(jnp.float32))
    cum_log_f = jnp.cumsum(pad_front(log_f), axis=1)
    o_fox = blocked_causal_attention(pad_front(qf), pad_front(kf), pad_front(vf),
                                     FOX_DIM ** -0.5, cum_log_f)[:, PAD:]
    o_fox = o_fox.reshape(b, l, FOX_HEADS * FOX_DIM)

    return jnp.concatenate([o_mla, o_fox], axis=-1) @ w_out


def short_conv_mixer(h, w_in, conv_w, w_out):
    z = h @ w_in
    gate_b, gate_c, u = jnp.split(z, 3, axis=-1)
    g = gate_c * u
    y = lax.conv_general_dilated(
        g, conv_w[:, None, :].astype(g.dtype), window_strides=(1,),
        padding=[(CONV_WIDTH - 1, 0)], dimension_numbers=('NWC', 'WIO', 'NWC'),
        feature_group_count=D_MODEL)
    return (gate_b * y) @ w_out


def sq_relu_mlp(h, w_up, w_down):
    return jnp.square(jax.nn.relu(h @ w_up)) @ w_down


def setup_inputs(seed: int = 0) -> dict:
    key = jax.random.key(seed)
    ks = iter(jax.random.split(key, 32))

    def nrm(shape, scale):
        return jax.random.normal(next(ks), shape, jnp.float32) * scale

    def gain(shape):
        return 1.0 + 0.02 * jax.random.normal(next(ks), shape, jnp.float32)

    out_scale = (2.0 * DEPTH) ** -0.5
    return {
        "x": nrm((BATCH, SEQ, D_MODEL), 1.0),
        "meta_tokens": nrm((N_META, D_MODEL), 1.0),
        "g_mix": gain((DEPTH, D_MODEL)),
        "g_mlp": gain((DEPTH, D_MODEL)),
        "w_in_attn": nrm((N_EVEN, D_MODEL, ATTN_IN), D_MODEL ** -0.5),
        "g_cq": gain((N_EVEN, Q_LORA)),
        "w_uq": nrm((N_EVEN, Q_LORA, MLA_HEADS * MLA_QK), Q_LORA ** -0.5),
        "g_ckv": gain((N_EVEN, KV_LORA)),
        "w_ukv": nrm((N_EVEN, KV_LORA, MLA_HEADS * (MLA_NOPE + MLA_V)), KV_LORA ** -0.5),
        "g_q_mla": gain((N_EVEN, MLA_QK)),
        "g_k_mla": gain((N_EVEN, MLA_QK)),
        "g_q_fox": gain((N_EVEN, FOX_DIM)),
        "g_k_fox": gain((N_EVEN, FOX_DIM)),
        "b_forget": 2.0 + nrm((N_EVEN, FOX_HEADS), 0.1),
        "w_out_attn": nrm((N_EVEN, MIX_OUT, D_MODEL), MIX_OUT ** -0.5 * out_scale),
        "w_in_conv": nrm((N_ODD, D_MODEL, 3 * D_MODEL), D_MODEL ** -0.5),
        "conv_w": nrm((N_ODD, CONV_WIDTH, D_MODEL), CONV_WIDTH ** -0.5),
        "w_out_conv": nrm((N_ODD, D_MODEL, D_MODEL), D_MODEL ** -0.5 * out_scale),
        "w_mlp_up": nrm((DEPTH, D_MODEL, D_FF), D_MODEL ** -0.5),
        "w_mlp_down": nrm((DEPTH, D_FF, D_MODEL), D_FF ** -0.5 * out_scale),
    }


def reference(x, meta_tokens, g_mix, g_mlp, w_in_attn, g_cq, w_uq, g_ckv, w_ukv,
              g_q_mla, g_k_mla, g_q_fox, g_k_fox, b_forget, w_out_attn,
              w_in_conv, conv_w, w_out_conv, w_mlp_up, w_mlp_down):
    b = x.shape[0]
    meta = jnp.broadcast_to(meta_tokens.astype(x.dtype)[None], (b, N_META, D_MODEL))
    h = jnp.concatenate([meta, x], axis=1)
    cos, sin = rope_tables(h.shape[1])
    for layer in range(DEPTH):
        j = layer // 2
        hn = rms_norm(h, g_mix[layer])
        if layer % 2 == 0:
            h = h + attention_mixer(hn, cos, sin, w_in_attn[j], g_cq[j], w_uq[j], g_ckv[j],
                                    w_ukv[j], g_q_mla[j], g_k_mla[j], g_q_fox[j], g_k_fox[j],
                                    b_forget[j], w_out_attn[j])
        else:
            h = h + short_conv_mixer(hn, w_in_conv[j], conv_w[j], w_out_conv[j])
        h = h + sq_relu_mlp(rms_norm(h, g_mlp[layer]), w_mlp_up[layer], w_mlp_down[layer])
    return h[:, N_META:]
```

```python
import contextlib
import numpy as np
import concourse.bass as bass
import concourse.mybir as mybir
from concourse.bass_utils import run_bass_kernel_spmd

F32 = mybir.dt.float32
BF16 = mybir.dt.bfloat16
AF = mybir.ActivationFunctionType
ALU = mybir.AluOpType
AX = mybir.AxisListType

D = 1024
DFF = 4096
NMETA = 16
PAD = 112
EPS = 1e-6
QL, KVL, ROPE = 384, 256, 32
ATTN_IN = 2216
ENGS = ("pe", "dve", "act", "pool", "sp")
SKIP = set()


class Sched:
    def __init__(self, nc, stack):
        self.nc = nc
        self.streams = {e: [] for e in ENGS}
        self.sem, self.cnt = {}, {}
        self.waited = {e: {} for e in ENGS}
        self.last_w, self.readers = {}, {}
        for e in ENGS:
            self.sem["e_" + e] = stack.enter_context(nc.semaphore("e_" + e))
            self.cnt["e_" + e] = 0
        self.dma_pool, self.dma_rr = {}, {}
        for q, n in {"sp": 12, "pool": 4, "act": 2}.items():
            names = []
            for i in range(n):
                nm = f"d_{q}{i}"
                self.sem[nm] = stack.enter_context(nc.semaphore(nm))
                self.cnt[nm] = 0
                names.append(nm)
            self.dma_pool[q], self.dma_rr[q] = names, 0
        self.cc_pool, self.cc_rr = [], 0
        for i in range(6):
            nm = f"c_{i}"
            self.sem[nm] = stack.enter_context(nc.semaphore(nm))
            self.cnt[nm] = 0
            self.cc_pool.append(nm)
        self.n_ops = 0

    def _deps(self, reads, writes):
        deps = {}

        def add(nm, val, eng):
            if deps.get(nm, (0, None))[0] < val:
                deps[nm] = (val, eng)

        for r in reads:
            if r in self.last_w:
                add(*self.last_w[r])
        for w in writes:
            if w in self.last_w:
                add(*self.last_w[w])
            for nm, (val, eng) in self.readers.get(w, {}).items():
                add(nm, val, eng)
        return deps

    def _record(self, tok, reads, writes):
        nm, val, eng = tok
        for r in reads:
            d = self.readers.setdefault(r, {})
            if d.get(nm, (0, None))[0] < val:
                d[nm] = (val, eng)
        for w in writes:
            self.last_w[w] = tok
            self.readers[w] = {}

    def _add_waits(self, eng, deps):
        for nm, (val, dep_eng) in deps.items():
            if dep_eng == eng and eng == "pe":
                continue
            if self.waited[eng].get(nm, 0) >= val:
                continue
            self.waited[eng][nm] = val
            self.streams[eng].append(("wait", nm, val))

    def op(self, eng, fn, reads=(), writes=()):
        self._add_waits(eng, self._deps(reads, writes))
        nm = "e_" + eng
        self.cnt[nm] += 1
        tok = (nm, self.cnt[nm], eng)
        self.streams[eng].append(("op", fn, nm, 1))
        self._record(tok, reads, writes)
        self.n_ops += 1

    def dma(self, q, fn, reads=(), writes=()):
        deps = self._deps(reads, writes)
        pool = self.dma_pool[q]
        nm = pool[self.dma_rr[q] % len(pool)]
        self.dma_rr[q] += 1
        if self.cnt[nm] > 0 and deps.get(nm, (0, None))[0] < self.cnt[nm]:
            deps[nm] = (self.cnt[nm], None)
        self._add_waits(q, deps)
        self.cnt[nm] += 16
        self.streams[q].append(("op", fn, nm, 16))
        self._record((nm, self.cnt[nm], None), reads, writes)
        self.n_ops += 1

    def cc(self, fn, reads=(), writes=()):
        deps = self._deps(reads, writes)
        nm = self.cc_pool[self.cc_rr % len(self.cc_pool)]
        self.cc_rr += 1
        if self.cnt[nm] > 0 and deps.get(nm, (0, None))[0] < self.cnt[nm]:
            deps[nm] = (self.cnt[nm], None)
        self._add_waits("pool", deps)
        self.cnt[nm] += 1
        self.streams["pool"].append(("op", fn, nm, 1))
        self._record((nm, self.cnt[nm], None), reads, writes)
        self.n_ops += 1

    def barrier(self):
        for e in ENGS:
            deps = {nm: (v, None) for nm, v in self.cnt.items() if v > 0 and nm != "e_" + e}
            self._add_waits(e, deps)

    def emit(self):
        nc = self.nc
        with nc.Block() as block:
            def run(e, engobj):
                for ent in self.streams[e]:
                    if ent[0] == "wait":
                        engobj.wait_ge(self.sem[ent[1]], ent[2])
                    else:
                        ent[1](engobj).then_inc(self.sem[ent[2]], ent[3])

            block.tensor(lambda t: run("pe", t))
            block.vector(lambda v: run("dve", v))
            block.scalar(lambda s: run("act", s))
            block.gpsimd(lambda g: run("pool", g))
            block.sync(lambda s: run("sp", s))


def build_program(nblk, layers=("attn", "mlp", "conv", "mlp", "attn", "mlp", "conv", "mlp"), debug=False, n_cores=8):
    T = nblk * 128
    nc = bass.Bass("TRN2", target_bir_lowering=False)
    di = lambda name, shape, dt=F32: nc.dram_tensor(name, list(shape), dt, kind="ExternalInput").ap()
    x_pad = di("x_pad", [T, D])
    g_mix = di("g_mix", [4, D]); g_mlp = di("g_mlp", [4, D])
    w_in_attn = di("w_in_attn", [2, D, ATTN_IN]); g_cq = di("g_cq", [2, QL]); w_uq = di("w_uq", [2, QL, 768])
    g_ckv = di("g_ckv", [2, KVL]); w_ukv = di("w_ukv", [2, KVL, 1024])
    g_q_mla = di("g_q_mla", [2, 96]); g_k_mla = di("g_k_mla", [2, 96])
    g_q_fox = di("g_q_fox", [2, 64]); g_k_fox = di("g_k_fox", [2, 64]); b_forget = di("b_forget", [2, 8])
    w_out_attn = di("w_out_attn", [2, D, D])
    w_in_conv = di("w_in_conv", [2, D, 3 * D]); conv_w = di("conv_w", [2, 3, D]); w_out_conv = di("w_out_conv", [2, D, D])
    w_mlp_up = di("w_mlp_up", [4, D, DFF]); w_mlp_down = di("w_mlp_down", [4, DFF, D])
    cos_d = di("cos_t", [128, nblk, 16]); sin_d = di("sin_t", [128, nblk, 16])
    ident_d = di("ident", [128, 128]); tri_d = di("tri", [128, 128]); e127_d = di("e127", [128, 128])
    a1_d = di("a1", [128, 1]); rowmask_d = di("rowmask0", [128, 1])
    y_out = nc.dram_tensor("y", [T, D], F32, kind="ExternalOutput").ap()
    h_d = nc.dram_tensor("h_scr", [T, D], F32).ap()
    skind = "ExternalOutput" if debug else "Internal"
    oT_d = nc.dram_tensor("oT_scr", [D, T], BF16, kind=skind).ap()
    qt_d = nc.dram_tensor("qt_scr", [16, 96, T], BF16).ap()
    kt2 = nc.dram_tensor("kt_scr", [16 * 96, T], BF16)
    kt_d = kt2.ap().rearrange("(h d) t -> h d t", d=96)
    v2 = nc.dram_tensor("v_scr", [16 * 128, nblk * 80], BF16)
    v_d = v2.ap().rearrange("(h p) (b c) -> h p b c", p=128, c=80)
    ktg2 = nc.dram_tensor("ktg", [2 * 16 * 96, T], BF16)
    ktg_h = lambda h: ktg2.ap()[(h // 2) * 384 + (h % 2) * 96:(h // 2) * 384 + (h % 2) * 96 + 96, :]
    vg2 = nc.dram_tensor("vg", [2 * 16 * 128, nblk * 80], BF16)
    vg_h = lambda h: vg2.ap()[(h // 2) * 512 + (h % 2) * 128:(h // 2) * 512 + (h % 2) * 128 + 128, :].rearrange("p (b c) -> p b c", c=80)
    tot_s = nc.dram_tensor("tot_s", [128, 8], F32)
    tot_g = nc.dram_tensor("tot_g", [256, 8], F32)
    gh_s = nc.dram_tensor("gh_s", [128, 16], F32)
    gh_g = nc.dram_tensor("gh_g", [256, 16], F32)
    RG = [[2 * i, 2 * i + 1] for i in range(n_cores // 2)]
    if debug:
        zdbg = nc.dram_tensor("zdbg", [nblk, 128, ATTN_IN], F32, kind="ExternalOutput").ap()
        hndbg = nc.dram_tensor("hndbg", [nblk, 128, 8, 128], BF16, kind="ExternalOutput").ap()
        windbg = nc.dram_tensor("windbg", [8, 128, 2304], BF16, kind="ExternalOutput").ap()

    with contextlib.ExitStack() as st:
        S = Sched(nc, st)
        uid = [0]

        def sbt(stack, name, shape, dt):
            uid[0] += 1
            return stack.enter_context(nc.sbuf_tensor(f"{name}_{uid[0]}", list(shape), dt))

        def pst(stack, name, shape, dt):
            uid[0] += 1
            return stack.enter_context(nc.psum_tensor(f"{name}_{uid[0]}", list(shape), dt))

        ident = sbt(st, "ident", [128, 128], BF16)
        tri_bf = sbt(st, "tri_bf", [128, 128], BF16)
        tri_f = sbt(st, "tri_f", [128, 128], F32)
        e127 = sbt(st, "e127", [128, 128], F32)
        cos_t = sbt(st, "cos", [128, nblk, 16], F32)
        sin_t = sbt(st, "sin", [128, nblk, 16], F32)
        S.dma("pool", lambda e: e.dma_start(out=ident[:], in_=ident_d), writes=["ident"])
        S.dma("pool", lambda e: e.dma_start(out=tri_bf[:], in_=tri_d), writes=["tri_bf"])
        S.dma("sp", lambda e: e.dma_start(out=tri_f[:], in_=tri_d), writes=["tri_f"])
        S.dma("sp", lambda e: e.dma_start(out=e127[:], in_=e127_d), writes=["e127"])
        S.dma("sp", lambda e: e.dma_start(out=cos_t[:], in_=cos_d), writes=["cos"])
        S.dma("sp", lambda e: e.dma_start(out=sin_t[:], in_=sin_d), writes=["sin"])
        a1 = sbt(st, "a1", [128, 1], F32)
        rowmask = sbt(st, "rowmask", [128, 1], F32)
        S.dma("sp", lambda e: e.dma_start(out=a1[:], in_=a1_d), writes=["a1"])
        S.dma("sp", lambda e: e.dma_start(out=rowmask[:], in_=rowmask_d), writes=["rowmask"])

        if debug:
            with contextlib.ExitStack() as fs:
                fill = sbt(fs, "fill", [128, 25000], F32)
                S.op("dve", lambda e: e.memset(fill[:, 0:12500], 7.0), [], ["fillA"])
                S.op("pool", lambda e: e.memset(fill[:, 12500:25000], 7.0), [], ["fillB"])
                S.barrier()
        cast_rr = [0]

        NS = 1024

        def load_weight(wst, dst, dst_name, w_ap, K, N, gain=None, gname=None, col_major=False):
            KC = K // 128
            chunks = [(kc, n0) for kc in range(KC) for n0 in range(0, N, NS)]
            if col_major:
                chunks = [(kc, n0) for n0 in range(0, N, NS) for kc in range(KC)]
            for (kc, n0) in chunks:
                if True:
                    n1 = min(N, n0 + NS)
                    i = cast_rr[0]
                    cast_rr[0] += 1
                    stg = wst[i % len(wst)]
                    sname = ("wstg", i % len(wst))
                    S.dma("sp", lambda e, stg=stg, kc=kc, n0=n0, n1=n1: e.dma_start(
                        out=stg[:, 0:n1 - n0], in_=w_ap[kc * 128:(kc + 1) * 128, n0:n1]), writes=[sname])
                    eng = ("act", "dve")[i % 2]
                    rd = [sname] + ([gname] if gain is not None else [])
                    wr = [(dst_name, kc, n0)]
                    if gain is None:
                        if eng == "act":
                            S.op("act", lambda e, stg=stg, kc=kc, n0=n0, n1=n1: e.copy(dst[:, kc, n0:n1], stg[:, 0:n1 - n0]), rd, wr)
                        else:
                            S.op(eng, lambda e, stg=stg, kc=kc, n0=n0, n1=n1: e.tensor_copy(dst[:, kc, n0:n1], stg[:, 0:n1 - n0]), rd, wr)
                    else:
                        if eng == "act":
                            S.op("act", lambda e, stg=stg, kc=kc, n0=n0, n1=n1: e.activation(
                                dst[:, kc, n0:n1], stg[:, 0:n1 - n0], AF.Copy, scale=gain[:, kc:kc + 1]), rd, wr)
                        else:
                            S.op(eng, lambda e, stg=stg, kc=kc, n0=n0, n1=n1: e.tensor_scalar(
                                dst[:, kc, n0:n1], stg[:, 0:n1 - n0], gain[:, kc:kc + 1], None, ALU.mult), rd, wr)

        def wr(dst_name, kc, c0, c1):
            return [(dst_name, kc, n0) for n0 in range((c0 // NS) * NS, c1, NS)]

        def load_gain_cols(dst, name, g_row_ap, K):
            S.dma("sp", lambda e: e.dma_start(out=dst[:, 0:K // 128], in_=g_row_ap.rearrange("(kc p) -> p kc", p=128),
                                              allow_slow_non_contiguous=True), writes=[name])

        def rmsnorm_T(h_ap, hname, hnT, hnT_name, col0, tmp, tT, tT_name, width=D, src_res=None, pre=""):
            junk, ss, rstd, hn_bf, nm = tmp
            rd = [hname] if src_res is None else src_res
            S.op("act", lambda e: e.activation(junk[:, 0:width], h_ap, AF.Square, accum_out=ss[:, 0:1]),
                 rd, [pre + "junk", pre + "ss"])
            S.op("dve", lambda e: e.tensor_scalar(rstd[:, 0:1], ss[:, 0:1], 1.0 / width, EPS, ALU.mult, ALU.add),
                 [pre + "ss"], [pre + "rstd"])
            S.op("act", lambda e: e.activation(rstd[:, 0:1], rstd[:, 0:1], AF.Ln), [pre + "rstd"], [pre + "rstd"])
            S.op("act", lambda e: e.activation(rstd[:, 0:1], rstd[:, 0:1], AF.Exp, scale=-0.5), [pre + "rstd"], [pre + "rstd"])
            S.op("act", lambda e: e.activation(hn_bf[:, 0:width], h_ap, AF.Copy, scale=rstd[:, 0:1]),
                 rd + [pre + "rstd"], [nm + "hn"])
            KC = width // 128
            for kc in range(KC):
                S.op("pe", lambda e, kc=kc: e.transpose(tT[:, kc * 128:(kc + 1) * 128], hn_bf[:, kc * 128:(kc + 1) * 128], ident[:]),
                     [nm + "hn", "ident"], [tT_name])
            S.op("dve", lambda e: e.tensor_copy(hnT[:, 0:KC, col0:col0 + 128],
                                                tT[:, 0:KC * 128].rearrange("p (k t) -> p k t", t=128)),
                 [tT_name], [hnT_name])

        first_phase = [True]

        def src_dst(is_last):
            src = x_pad if first_phase[0] else h_d
            first_phase[0] = False
            return src, (y_out if is_last else h_d)

        def mlp_phase(layer, is_last):
            src, dst = src_dst(is_last)
            CT = 256
            with contextlib.ExitStack() as ps:
                wup = sbt(ps, "wup", [128, 8, DFF], BF16)
                wdn = sbt(ps, "wdn", [128, 32, D], BF16)
                gcol = sbt(ps, "gcol", [128, 8], F32)
                load_gain_cols(gcol, "gcol", g_mlp[layer], D)
                wst = [sbt(ps, f"wstg{i}", [128, NS], F32) for i in range(3)]
                h_t = [sbt(ps, f"h{i}", [128, 2, D], F32) for i in range(2)]
                junk = sbt(ps, "junk", [128, D], BF16)
                ss = sbt(ps, "ss", [128, 1], F32)
                rstd = sbt(ps, "rstd", [128, 1], F32)
                hn_bf = [sbt(ps, f"hnbf{i}", [128, D], BF16) for i in range(2)]
                hnT = sbt(ps, "hnT", [128, 8, CT], BF16)
                r_t = [sbt(ps, f"r{i}", [128, CT], F32) for i in range(2)]
                h1T = sbt(ps, "h1T", [128, 32, CT], BF16)
                tT = [pst(ps, f"tT{i}", [128, 1024], BF16) for i in range(2)]
                pu = [pst(ps, f"pu{i}", [128, 512], F32) for i in range(3)]
                pd = [pst(ps, f"pd{i}", [128, 512], F32) for i in range(3)]
                nchunks = (T + CT - 1) // CT

                def load_chunk(c):
                    r0 = c * CT
                    nb = min(2, (T - r0) // 128)
                    ht = h_t[c % 2]
                    S.dma("sp", lambda e: e.dma_start(
                        out=ht[:, 0:nb, :], in_=src[r0:r0 + nb * 128, :].rearrange("(b p) d -> p b d", p=128)), writes=[("h", c % 2)])

                load_chunk(0)
                if nchunks > 1:
                    load_chunk(1)
                load_weight(wst, wup, "wup", w_mlp_up[layer], D, DFF, gain=gcol, gname="gcol", col_major=True)
                load_weight(wst, wdn, "wdn", w_mlp_down[layer], DFF, D)
                ti = 0
                ui = 0
                di_ = 0
                for c in range(nchunks):
                    r0 = c * CT
                    nb = min(2, (T - r0) // 128)
                    ncols = nb * 128
                    ht = h_t[c % 2]
                    hname = ("h", c % 2)
                    if c >= 1 and c + 1 < nchunks:
                        load_chunk(c + 1)
                    for b in range(nb):
                        k = ti % 2
                        ti += 1
                        rmsnorm_T(ht[:, b, :], hname, hnT, "hnT", b * 128,
                                  (junk, ss, rstd, hn_bf[k], f"n{k}"), tT[k], ("tT", k))
                    for fc in range(32):
                        k = ui % 3
                        ui += 1
                        for kc in range(8):
                            S.op("pe", lambda e, k=k, kc=kc, fc=fc, ncols=ncols: e.matmul(
                                pu[k][:, 0:ncols], wup[:, kc, fc * 128:(fc + 1) * 128], hnT[:, kc, 0:ncols],
                                start=(kc == 0), stop=(kc == 7)), ["hnT"] + wr("wup", kc, fc * 128, fc * 128 + 128), [("pu", k)])
                        rr = fc % 2
                        S.op("act", lambda e, k=k, rr=rr, ncols=ncols: e.activation(r_t[rr][:, 0:ncols], pu[k][:, 0:ncols], AF.Relu),
                             [("pu", k)], [("r", rr)])
                        S.op("dve", lambda e, rr=rr, fc=fc, ncols=ncols: e.tensor_tensor(
                            h1T[:, fc, 0:ncols], r_t[rr][:, 0:ncols], r_t[rr][:, 0:ncols], ALU.mult),
                             [("r", rr)], [("h1T", fc)])
                    for b in range(nb):
                        for dh in range(2):
                            k = di_ % 3
                            di_ += 1
                            for fc in range(32):
                                S.op("pe", lambda e, k=k, fc=fc, b=b, dh=dh: e.matmul(
                                    pd[k][:, :], h1T[:, fc, b * 128:(b + 1) * 128], wdn[:, fc, dh * 512:(dh + 1) * 512],
                                    start=(fc == 0), stop=(fc == 31)), [("h1T", fc)] + wr("wdn", fc, dh * 512, dh * 512 + 512), [("pd", k)])
                            S.op("dve", lambda e, k=k, ht=ht, b=b, dh=dh: e.tensor_tensor(
                                ht[:, b, dh * 512:(dh + 1) * 512], pd[k][:, :], ht[:, b, dh * 512:(dh + 1) * 512], ALU.add),
                                 [("pd", k), hname], [hname])
                    S.dma("sp", lambda e, ht=ht, r0=r0, nb=nb: e.dma_start(
                        out=dst[r0:r0 + nb * 128, :].rearrange("(b p) d -> p b d", p=128), in_=ht[:, 0:nb, :]),
                          reads=[hname], writes=[("hrows", c)])
                S.barrier()

        def conv_phase(j, layer, is_last):
            src, dst = src_dst(is_last)
            CT = 256
            with contextlib.ExitStack() as ps:
                win = sbt(ps, "cwin", [128, 8, 3 * D], BF16)
                wout = sbt(ps, "cwout", [128, 8, D], BF16)
                gcol = sbt(ps, "gcol", [128, 8], F32)
                cw = sbt(ps, "cw", [128, 3, 8], F32)
                load_gain_cols(gcol, "gcol", g_mix[layer], D)
                for tap in range(3):
                    S.dma("sp", lambda e, tap=tap: e.dma_start(
                        out=cw[:, tap, :], in_=conv_w[j, tap].rearrange("(kc p) -> p kc", p=128),
                        allow_slow_non_contiguous=True), writes=[("cw", tap)])
                CW = [("cw", t) for t in range(3)]
                wst = [sbt(ps, f"wstg{i}", [128, NS], F32) for i in range(3)]
                h_t = [sbt(ps, f"h{i}", [128, 2, D], F32) for i in range(2)]
                junk = sbt(ps, "junk", [128, D], BF16)
                ss = sbt(ps, "ss", [128, 1], F32)
                rstd = sbt(ps, "rstd", [128, 1], F32)
                hn_bf = [sbt(ps, f"hnbf{i}", [128, D], BF16) for i in range(2)]
                hnT = sbt(ps, "hnT", [128, 8, CT], BF16)
                gext = sbt(ps, "gext", [128, 8, CT + 2], F32)
                c_sb = [sbt(ps, f"csb{i}", [128, CT], F32) for i in range(2)]
                b_sb = [sbt(ps, f"bsb{i}", [128, CT], F32) for i in range(2)]
                acc = [sbt(ps, f"acc{i}", [128, CT], F32) for i in range(2)]
                mT = sbt(ps, "mT", [128, 8, CT], BF16)
                tT = [pst(ps, f"tT{i}", [128, 1024], BF16) for i in range(2)]
                pz = [pst(ps, f"pz{i}", [128, 512], F32) for i in range(4)]
                pd = [pst(ps, f"pd{i}", [128, 512], F32) for i in range(2)]
                ghs = sbt(ps, "ghs", [128, 8, 2], F32)
                ghl = sbt(ps, "ghl", [128, 16], F32)
                hlt = sbt(ps, "hlt", [128, D], F32)

                def conv_halo():
                    conv_halo_body()

                def conv_halo_body():
                  if True:
                    rmsnorm_T(hlt[:], "hlt", hnT, "hnT", 0, (junk, ss, rstd, hn_bf[0], "n0"), tT[0], ("tT", 0))
                    for fc in range(8):
                        for sec in (1, 2):
                            col = sec * D + fc * 128
                            for kc in range(8):
                                S.op("pe", lambda e, sec=sec, kc=kc, col=col: e.matmul(
                                    pz[sec][:, 0:128], win[:, kc, col:col + 128], hnT[:, kc, 0:128],
                                    start=(kc == 0), stop=(kc == 7)), ["hnT"] + wr("cwin", kc, col, col + 128), [("pz", sec)])
                        S.op("act", lambda e: e.copy(c_sb[0][:, 0:128], pz[1][:, 0:128]), [("pz", 1)], [("csb", 0)])
                        S.op("dve", lambda e, fc=fc: e.tensor_tensor(ghs[:, fc, :], pz[2][:, 126:128], c_sb[0][:, 126:128], ALU.mult),
                             [("pz", 2), ("csb", 0)], ["ghs"])
                    S.dma("sp", lambda e: e.dma_start(out=gh_s.ap(), in_=ghs[:].rearrange("p f c -> p (f c)")), reads=["ghs"], writes=["gh_s"])
                    S.cc(lambda e: e.collective_compute("AllGather", ALU.bypass, replica_groups=RG,
                                                        ins=[gh_s.ap().opt()], outs=[gh_g.ap().opt()]), ["gh_s"], ["gh_g"])
                    S.dma("sp", lambda e: e.dma_start(out=ghl[:], in_=gh_g.ap()[0:128, :]), reads=["gh_g"], writes=["ghl"])
                    S.op("dve", lambda e: e.tensor_scalar(gext[:, :, 0:2], ghl[:].rearrange("p (f c) -> p f c", c=2), a1[:, 0:1], None, ALU.mult),
                         ["ghl", "a1"], [("gh", fc) for fc in range(8)])
                nchunks = (T + CT - 1) // CT

                def load_chunk(c):
                    r0 = c * CT
                    nb = min(2, (T - r0) // 128)
                    ht = h_t[c % 2]
                    S.dma("sp", lambda e: e.dma_start(
                        out=ht[:, 0:nb, :], in_=src[r0:r0 + nb * 128, :].rearrange("(b p) d -> p b d", p=128)), writes=[("h", c % 2)])

                load_chunk(0)
                if nchunks > 1:
                    load_chunk(1)
                S.dma("sp", lambda e: e.dma_start(out=hlt[:], in_=src[T - 128:T, :]), writes=["hlt"])
                load_weight(wst, win, "cwin", w_in_conv[j], D, 3 * D, gain=gcol, gname="gcol")
                load_weight(wst, wout, "cwout", w_out_conv[j], D, D)
                conv_halo()
                ti = zi = di_ = 0
                for c in range(nchunks):
                    r0 = c * CT
                    nb = min(2, (T - r0) // 128)
                    ncols = nb * 128
                    ht = h_t[c % 2]
                    hname = ("h", c % 2)
                    if c >= 1 and c + 1 < nchunks:
                        load_chunk(c + 1)
                    for b in range(nb):
                        k = ti % 2
                        ti += 1
                        rmsnorm_T(ht[:, b, :], hname, hnT, "hnT", b * 128,
                                  (junk, ss, rstd, hn_bf[k], f"n{k}"), tT[k], ("tT", k))
                    for fc in range(8):
                        zk = []
                        for sec in range(3):
                            k = zi % 4
                            zi += 1
                            zk.append(k)
                            col = sec * D + fc * 128
                            for kc in range(8):
                                S.op("pe", lambda e, k=k, kc=kc, col=col, ncols=ncols: e.matmul(
                                    pz[k][:, 0:ncols], win[:, kc, col:col + 128], hnT[:, kc, 0:ncols],
                                    start=(kc == 0), stop=(kc == 7)), ["hnT"] + wr("cwin", kc, col, col + 128), [("pz", k)])
                        q = fc % 2
                        S.op("act", lambda e, q=q, k=zk[0], ncols=ncols: e.copy(b_sb[q][:, 0:ncols], pz[k][:, 0:ncols]),
                             [("pz", zk[0])], [("bsb", q)])
                        S.op("act", lambda e, q=q, k=zk[1], ncols=ncols: e.copy(c_sb[q][:, 0:ncols], pz[k][:, 0:ncols]),
                             [("pz", zk[1])], [("csb", q)])
                        S.op("dve", lambda e, q=q, k=zk[2], fc=fc, ncols=ncols: e.tensor_tensor(
                            gext[:, fc, 2:2 + ncols], pz[k][:, 0:ncols], c_sb[q][:, 0:ncols], ALU.mult),
                             [("pz", zk[2]), ("csb", q)], [("g", fc)])
                        G = [("g", fc), ("gh", fc)] + CW
                        S.op("dve", lambda e, q=q, fc=fc, ncols=ncols: e.tensor_scalar(
                            acc[q][:, 0:ncols], gext[:, fc, 0:ncols], cw[:, 0, fc:fc + 1], None, ALU.mult), G, [("acc", q)])
                        S.op("dve", lambda e, q=q, fc=fc, ncols=ncols: e.scalar_tensor_tensor(
                            acc[q][:, 0:ncols], gext[:, fc, 1:1 + ncols], cw[:, 1, fc:fc + 1], acc[q][:, 0:ncols], ALU.mult, ALU.add),
                             G + [("acc", q)], [("acc", q)])
                        S.op("dve", lambda e, q=q, fc=fc, ncols=ncols: e.scalar_tensor_tensor(
                            acc[q][:, 0:ncols], gext[:, fc, 2:2 + ncols], cw[:, 2, fc:fc + 1], acc[q][:, 0:ncols], ALU.mult, ALU.add),
                             G + [("acc", q)], [("acc", q)])
                        S.op("pool", lambda e, q=q, fc=fc, ncols=ncols: e.tensor_tensor(
                            mT[:, fc, 0:ncols], acc[q][:, 0:ncols], b_sb[q][:, 0:ncols], ALU.mult),
                             [("acc", q), ("bsb", q)], [("mT", fc)])
                        S.op("pool", lambda e, fc=fc, ncols=ncols: e.tensor_copy(gext[:, fc, 0:2], gext[:, fc, ncols:ncols + 2]),
                             [("g", fc)], [("gh", fc)])
                    for b in range(nb):
                        for dh in range(2):
                            k = di_ % 2
                            di_ += 1
                            for fc in range(8):
                                S.op("pe", lambda e, k=k, fc=fc, b=b, dh=dh: e.matmul(
                                    pd[k][:, :], mT[:, fc, b * 128:(b + 1) * 128], wout[:, fc, dh * 512:(dh + 1) * 512],
                                    start=(fc == 0), stop=(fc == 7)), [("mT", fc)] + wr("cwout", fc, dh * 512, dh * 512 + 512), [("pd", k)])
                            S.op("dve", lambda e, k=k, ht=ht, b=b, dh=dh: e.tensor_tensor(
                                ht[:, b, dh * 512:(dh + 1) * 512], pd[k][:, :], ht[:, b, dh * 512:(dh + 1) * 512], ALU.add),
                                 [("pd", k), hname], [hname])
                    S.dma("sp", lambda e, ht=ht, r0=r0, nb=nb: e.dma_start(
                        out=dst[r0:r0 + nb * 128, :].rearrange("(b p) d -> p b d", p=128), in_=ht[:, 0:nb, :]),
                          reads=[hname], writes=[("hrows", c)])
                S.barrier()

        def attn_phase(j, layer, is_last):
            src, dst = src_dst(is_last)
            with contextlib.ExitStack() as ps:
                win = sbt(ps, "awin", [128, 8, 2304], BF16)
                wuq = sbt(ps, "wuq", [128, 3, 768], BF16)
                wukv = sbt(ps, "wukv", [128, 2, 1024], BF16)
                gcol = sbt(ps, "gcol", [128, 8], F32)
                gq_c = sbt(ps, "gqc", [128, 3], F32)
                gkv_c = sbt(ps, "gkvc", [128, 2], F32)
                load_gain_cols(gcol, "gcol", g_mix[layer], D)
                load_gain_cols(gq_c, "gqc", g_cq[j], QL)
                load_gain_cols(gkv_c, "gkvc", g_ckv[j], KVL)
                gall = sbt(ps, "gall", [128, 336], F32)
                rowt = sbt(ps, "rowt", [1, 336], F32)
                ones_row = sbt(ps, "ones_row", [1, 128], F32)
                gqm, gkm, gqf, gkf, bfo = gall[:, 0:96], gall[:, 96:192], gall[:, 192:256], gall[:, 256:320], gall[:, 320:328]
                S.op("pool", lambda e: e.memset(ones_row[:], 1.0), [], ["ones_row"])
                S.op("pool", lambda e: e.memset(rowt[:], 0.0), [], ["rowt"])
                for off, n, apx in ((0, 96, g_q_mla[j:j + 1, :]), (96, 96, g_k_mla[j:j + 1, :]), (192, 64, g_q_fox[j:j + 1, :]),
                                    (256, 64, g_k_fox[j:j + 1, :]), (320, 8, b_forget[j:j + 1, :])):
                    S.dma("sp", lambda e, off=off, n=n, apx=apx: e.dma_start(out=rowt[0:1, off:off + n], in_=apx), reads=["rowt"], writes=["rowt"])
                with contextlib.ExitStack() as pbs:
                    pb = pst(pbs, "pb", [128, 512], F32)
                    S.op("pe", lambda e: e.matmul(pb[:, 0:336], ones_row[:], rowt[:], start=True, stop=True), ["ones_row", "rowt"], ["pb"])
                    S.op("dve", lambda e: e.tensor_copy(gall[:], pb[:, 0:336]), ["pb"], ["gqm", "gkm", "gqf", "gkf", "bfo"])
                    S.barrier()
                S.op("dve", lambda e: e.tensor_scalar(gqm[:], gqm[:], 96.0 ** -0.5, None, ALU.mult), ["gqm"], ["gqm"])
                S.op("dve", lambda e: e.tensor_scalar(gqf[:], gqf[:], 64.0 ** -0.5, None, ALU.mult), ["gqf"], ["gqf"])
                wst = [sbt(ps, f"wstg{i}", [128, NS], F32) for i in range(3)]
                h_t = [sbt(ps, f"h{i}", [128, D], F32) for i in range(2)]
                junk = sbt(ps, "junk", [128, D], BF16)
                ss = sbt(ps, "ss", [128, 1], F32)
                rstd = sbt(ps, "rstd", [128, 1], F32)
                hn_bf = sbt(ps, "hnbf", [128, D], BF16)
                hnT2 = [sbt(ps, f"hnT{i}", [128, 8, 128], BF16) for i in range(2)]
                z2 = [sbt(ps, f"z{i}", [128, ATTN_IN], F32) for i in range(2)]
                hn_bf2 = [hn_bf, sbt(ps, "hnbf1", [128, D], BF16)]
                ss1 = sbt(ps, "ss1", [128, 1], F32)
                rstd1 = sbt(ps, "rstd1", [128, 1], F32)
                cn_bf = sbt(ps, "cnbf", [128, 640], BF16)
                cnT = sbt(ps, "cnT", [128, 5, 128], BF16)
                ss8 = sbt(ps, "ss8", [128, 8], F32)
                sspe = sbt(ps, "sspe", [128, 1], F32)
                r8 = sbt(ps, "r8", [128, 8], F32)
                sq = sbt(ps, "sq", [128, 1024], F32)
                qn = sbt(ps, "qn", [128, 8, 96], F32)
                kpe = sbt(ps, "kpe", [128, 32], F32)
                kper = sbt(ps, "kper", [128, 32], F32)
                rtmp = sbt(ps, "rtmp", [128, 8, 16], F32)
                rtmp2 = sbt(ps, "rtmp2", [128, 8, 16], F32)
                qbf = sbt(ps, "qbf", [128, 8, 96], BF16)
                kbf = sbt(ps, "kbf", [128, 8, 96], BF16)
                fqbf = sbt(ps, "fqbf", [128, 8, 96], BF16)
                fkbf = sbt(ps, "fkbf", [128, 8, 96], BF16)
                hT_sb = [sbt(ps, f"hTsb{i}", [96, 8, 128], BF16) for i in range(4)]
                vaug = sbt(ps, "vaug", [128, 16, 80], BF16)
                lf = sbt(ps, "lf", [128, 8], F32)
                Fc = [sbt(ps, f"Fc{i}", [128, 8], F32) for i in range(2)]
                fr = sbt(ps, "fr", [128, 8], F32)
                fhi = sbt(ps, "fhi", [128, 8], BF16)
                fmid = sbt(ps, "fmid", [128, 8], BF16)
                tT = [pst(ps, f"tT{i}", [128, 1024], BF16) for i in range(2)]
                pz = [pst(ps, f"pz{i}", [128, 512], F32) for i in range(2)]
                pq = [pst(ps, f"pq{i}", [128, 512], F32) for i in range(2)]
                pkv = [pst(ps, f"pkv{i}", [128, 512], F32) for i in range(2)]
                S.op("pool", lambda e: e.memset(fqbf[:, :, 67:70], 1.0), [], ["fq1"])
                S.op("pool", lambda e: e.memset(fkbf[:, :, 64:67], 1.0), [], ["fk1"])
                S.op("pool", lambda e: e.memset(vaug[:, :, 64:80], 1.0), [], ["vones"])
                S.op("pool", lambda e: e.memset(Fc[1][:], 0.0), [], [("Fc", 1)])
                groups = [(0, 384), (384, 672), (672, 1184), (1184, 1696), (1696, 2208), (2208, 2216)]

                def bc8(ap2, n):
                    return ap2.unsqueeze(2).to_broadcast([128, 8, n])

                def bch(ap2, n):
                    return ap2.unsqueeze(1).to_broadcast([128, 8, n])

                def head_rstd(src3, n, extra=None, tag=""):
                    S.op("act", lambda e: e.activation(sq[:, 0:8 * n].rearrange("p (h d) -> p h d", d=n), src3, AF.Square),
                         [("z", 0), ("z", 1), "qn"], ["sq"])
                    S.op("dve", lambda e: e.reduce_sum(ss8[:], sq[:, 0:8 * n].rearrange("p (h d) -> p h d", d=n), axis=AX.X),
                         ["sq"], ["ss8"])
                    tot = n
                    if extra is not None:
                        tot = n + 32
                        S.op("dve", lambda e: e.tensor_scalar(ss8[:], ss8[:], sspe[:, 0:1], None, ALU.add), ["ss8", "sspe"], ["ss8"])
                    S.op("dve", lambda e: e.tensor_scalar(r8[:], ss8[:], 1.0 / tot, EPS, ALU.mult, ALU.add), ["ss8"], ["r8"])
                    S.op("act", lambda e: e.activation(r8[:], r8[:], AF.Ln), ["r8"], ["r8"])
                    S.op("act", lambda e: e.activation(r8[:], r8[:], AF.Exp, scale=-0.5), ["r8"], ["r8"])

                def transpose_heads(src_bf, sname, ncol, dstT, k, hk):
                    tt = tT[k]
                    for h in range(8):
                        S.op("pe", lambda e, h=h: e.transpose(tt[0:ncol, h * 128:(h + 1) * 128], src_bf[:, h, 0:ncol], ident[:]),
                             [sname, "ident"], [("tT", k)])
                    hs = hT_sb[hk]
                    S.op("dve", lambda e: e.tensor_copy(hs[0:ncol, :, :], tt[0:ncol, :].rearrange("p (h t) -> p h t", t=128)),
                         [("tT", k)], [("hTsb", hk)])
                    return hs

                def load_blk(bb):
                    tl = h_t[bb % 2]
                    S.dma("sp", lambda e: e.dma_start(out=tl[:], in_=src[bb * 128:(bb + 1) * 128, :]), writes=[("h", bb % 2)])

                load_blk(0)
                if nblk > 1:
                    load_blk(1)
                load_weight(wst, win, "awin", w_in_attn[j], D, ATTN_IN, gain=gcol, gname="gcol")
                load_weight(wst, wuq, "wuq", w_uq[j], QL, 768, gain=gq_c, gname="gqc")
                load_weight(wst, wukv, "wukv", w_ukv[j], KVL, 1024, gain=gkv_c, gname="gkvc")
                def stage1(blk):
                    z, zn = z2[blk % 2], ("z", blk % 2)
                    hnT, hnTn = hnT2[blk % 2], ("hnT", blk % 2)
                    r0 = blk * 128
                    ht = h_t[blk % 2]
                    hname = ("h", blk % 2)
                    if blk >= 1 and blk + 1 < nblk:
                        load_blk(blk + 1)
                    rmsnorm_T(ht[:], hname, hnT, hnTn, 0, (junk, ss1, rstd1, hn_bf2[blk % 2], f"n{blk % 2}"), tT[0], ("tT", 0), pre="s1")
                    for gi, (c0, c1) in enumerate(groups):
                        k = gi % 2
                        for kc in range(8):
                            S.op("pe", lambda e, k=k, kc=kc, c0=c0, c1=c1: e.matmul(
                                pz[k][:, 0:c1 - c0], hnT[:, kc, :], win[:, kc, c0:c1], start=(kc == 0), stop=(kc == 7)),
                                 [hnTn] + wr("awin", kc, c0, c1), [("pz", k)])
                        if gi % 2 == 0:
                            S.op("act", lambda e, k=k, c0=c0, c1=c1: e.copy(z[:, c0:c1], pz[k][:, 0:c1 - c0]), [("pz", k)], [zn])
                        else:
                            S.op("dve", lambda e, k=k, c0=c0, c1=c1: e.tensor_copy(z[:, c0:c1], pz[k][:, 0:c1 - c0]), [("pz", k)], [zn])
                    if debug:
                        S.dma("sp", lambda e, blk=blk: e.dma_start(out=zdbg[blk], in_=z[:, :]), reads=[zn], writes=[("zdbg", blk)])
                def stage2(blk):
                    r0 = blk * 128
                    z, zn = z2[blk % 2], ("z", blk % 2)
                    for (c0, w_, o0) in ((0, QL, 0), (QL, KVL, QL)):
                        S.op("act", lambda e, c0=c0, w_=w_: e.activation(sq[:, 0:w_], z[:, c0:c0 + w_], AF.Square, accum_out=ss[:, 0:1]),
                             [zn], ["sq", "ss"])
                        S.op("dve", lambda e, w_=w_: e.tensor_scalar(rstd[:, 0:1], ss[:, 0:1], 1.0 / w_, EPS, ALU.mult, ALU.add),
                             ["ss"], ["rstd"])
                        S.op("act", lambda e: e.activation(rstd[:, 0:1], rstd[:, 0:1], AF.Ln), ["rstd"], ["rstd"])
                        S.op("act", lambda e: e.activation(rstd[:, 0:1], rstd[:, 0:1], AF.Exp, scale=-0.5), ["rstd"], ["rstd"])
                        S.op("act", lambda e, c0=c0, w_=w_: e.activation(cn_bf[:, c0:c0 + w_], z[:, c0:c0 + w_], AF.Copy, scale=rstd[:, 0:1]),
                             [zn, "rstd"], ["cnbf"])
                    for kc in range(5):
                        S.op("pe", lambda e, kc=kc: e.transpose(tT[1][:, kc * 128:(kc + 1) * 128], cn_bf[:, kc * 128:(kc + 1) * 128], ident[:]),
                             ["cnbf", "ident"], [("tT", 1)])
                    S.op("dve", lambda e: e.tensor_copy(cnT[:, :, :], tT[1][:, 0:640].rearrange("p (k t) -> p k t", t=128)),
                         [("tT", 1)], ["cnT"])
                    for half, (n0, n1) in enumerate(((0, 512), (512, 768))):
                        for kc in range(3):
                            S.op("pe", lambda e, half=half, kc=kc, n0=n0, n1=n1: e.matmul(
                                pq[half][:, 0:n1 - n0], cnT[:, kc, :], wuq[:, kc, n0:n1], start=(kc == 0), stop=(kc == 2)),
                                 ["cnT"] + wr("wuq", kc, n0, n1), [("pq", half)])
                    for half in range(2):
                        for kc in range(2):
                            S.op("pe", lambda e, half=half, kc=kc: e.matmul(
                                pkv[half][:, :], cnT[:, 3 + kc, :], wukv[:, kc, half * 512:(half + 1) * 512],
                                start=(kc == 0), stop=(kc == 1)), ["cnT"] + wr("wukv", kc, half * 512, half * 512 + 512), [("pkv", half)])
                    qflat = qn[:].rearrange("p h d -> p (h d)")
                    S.op("act", lambda e: e.copy(qflat[:, 0:512], pq[0][:, :]), [("pq", 0)], ["qn"])
                    S.op("dve", lambda e: e.tensor_copy(qflat[:, 512:768], pq[1][:, 0:256]), [("pq", 1)], ["qn"])
                    head_rstd(qn[:, :, :], 96)
                    S.op("dve", lambda e: e.tensor_tensor(qn[:, :, :], qn[:, :, :], bc8(r8[:], 96), ALU.mult), ["qn", "r8"], ["qn"])
                    S.op("dve", lambda e: e.tensor_tensor(qn[:, :, :], qn[:, :, :], bch(gqm[:], 96), ALU.mult), ["qn", "gqm"], ["qn"])
                    S.op("act", lambda e: e.copy(qbf[:, :, 0:64], qn[:, :, 0:64]), ["qn"], ["qbf"])
                    cosb = bch(cos_t[:, blk, :], 16)
                    sinb = bch(sin_t[:, blk, :], 16)
                    S.op("pool", lambda e, sinb=sinb: e.tensor_tensor(rtmp[:, :, :], qn[:, :, 80:96], sinb, ALU.mult), ["qn", "sin"], ["rtmp"])
                    S.op("pool", lambda e, cosb=cosb: e.tensor_tensor(rtmp2[:, :, :], qn[:, :, 64:80], cosb, ALU.mult), ["qn", "cos"], ["rtmp2"])
                    S.op("pool", lambda e: e.tensor_tensor(qbf[:, :, 64:80], rtmp2[:, :, :], rtmp[:, :, :], ALU.subtract), ["rtmp2", "rtmp"], ["qbf"])
                    S.op("pool", lambda e, sinb=sinb: e.tensor_tensor(rtmp[:, :, :], qn[:, :, 64:80], sinb, ALU.mult), ["qn", "sin"], ["rtmp"])
                    S.op("pool", lambda e, cosb=cosb: e.tensor_tensor(rtmp2[:, :, :], qn[:, :, 80:96], cosb, ALU.mult), ["qn", "cos"], ["rtmp2"])
                    S.op("pool", lambda e: e.tensor_tensor(qbf[:, :, 80:96], rtmp2[:, :, :], rtmp[:, :, :], ALU.add), ["rtmp2", "rtmp"], ["qbf"])
                    hs = transpose_heads(qbf, "qbf", 96, qt_d, 0, 0)
                    S.dma("sp", lambda e, hs=hs, r0=r0: e.dma_start(out=qt_d[0:8, :, r0:r0 + 128].rearrange("h d t -> d h t"), in_=hs[:, :, :]),
                          reads=[("hTsb", 0)], writes=[("qt", blk)])
                    kv3 = [pkv[hf][:, :].rearrange("p (h d) -> p h d", d=128) for hf in range(2)]
                    S.op("act", lambda e: e.activation(kper[:, :], z[:, 640:672], AF.Square, accum_out=sspe[:, 0:1]), [zn], ["kper", "sspe"])
                    S.op("dve", lambda e: e.tensor_tensor(kpe[:, :], z[:, 640:672], gkm[:, 64:96], ALU.mult), [zn, "gkm"], ["kpe"])
                    c2, s2 = cos_t[:, blk, :], sin_t[:, blk, :]
                    S.op("pool", lambda e, s2=s2: e.tensor_tensor(rtmp[:, 0, :], kpe[:, 16:32], s2, ALU.mult), ["kpe", "sin"], ["rtmp"])
                    S.op("pool", lambda e, c2=c2: e.tensor_tensor(rtmp2[:, 0, :], kpe[:, 0:16], c2, ALU.mult), ["kpe", "cos"], ["rtmp2"])
                    S.op("pool", lambda e: e.tensor_tensor(kper[:, 0:16], rtmp2[:, 0, :], rtmp[:, 0, :], ALU.subtract), ["rtmp2", "rtmp", "sspe"], ["kper"])
                    S.op("pool", lambda e, s2=s2: e.tensor_tensor(rtmp[:, 0, :], kpe[:, 0:16], s2, ALU.mult), ["kpe", "sin"], ["rtmp"])
                    S.op("pool", lambda e, c2=c2: e.tensor_tensor(rtmp2[:, 0, :], kpe[:, 16:32], c2, ALU.mult), ["kpe", "cos"], ["rtmp2"])
                    S.op("pool", lambda e: e.tensor_tensor(kper[:, 16:32], rtmp2[:, 0, :], rtmp[:, 0, :], ALU.add), ["rtmp2", "rtmp"], ["kper"])
                    for hf in range(2):
                        S.op("act", lambda e, hf=hf: e.copy(qn[:, hf * 4:(hf + 1) * 4, 0:64], kv3[hf][:, :, 0:64]), [("pkv", hf)], ["qn"])
                        S.op("dve", lambda e, hf=hf: e.tensor_copy(vaug[:, hf * 4:(hf + 1) * 4, 0:64], kv3[hf][:, :, 64:128]), [("pkv", hf)], ["vaug"])
                    head_rstd(qn[:, :, 0:64], 64, extra=True)
                    S.op("dve", lambda e: e.tensor_tensor(qn[:, :, 0:64], qn[:, :, 0:64], bc8(r8[:], 64), ALU.mult), ["qn", "r8"], ["qn"])
                    S.op("dve", lambda e: e.tensor_tensor(kbf[:, :, 0:64], qn[:, :, 0:64], bch(gkm[:, 0:64], 64), ALU.mult), ["qn", "gkm"], ["kbf"])
                    S.op("dve", lambda e: e.tensor_tensor(kbf[:, :, 64:96], bch(kper[:, :], 32), bc8(r8[:], 32), ALU.mult), ["kper", "r8"], ["kbf"])
                    hs = transpose_heads(kbf, "kbf", 96, kt_d, 1, 1)
                    S.dma("sp", lambda e, hs=hs, r0=r0: e.dma_start(out=kt_d[0:8, :, r0:r0 + 128].rearrange("h d t -> d h t"), in_=hs[:, :, :]),
                          reads=[("hTsb", 1)], writes=[("kt", blk)])
                    S.op("dve", lambda e: e.tensor_tensor(lf[:], z[:, 2208:2216], bfo[:], ALU.add), [zn, "bfo"], ["lf"])
                    S.op("act", lambda e: e.activation(lf[:], lf[:], AF.Exp, scale=-1.0), ["lf"], ["lf"])
                    S.op("dve", lambda e: e.tensor_scalar(lf[:], lf[:], 1.0, None, ALU.add), ["lf"], ["lf"])
                    S.op("act", lambda e: e.activation(lf[:], lf[:], AF.Ln), ["lf"], ["lf"])
                    if blk == 0:
                        S.op("dve", lambda e: e.tensor_scalar(lf[:], lf[:], rowmask[:, 0:1], None, ALU.mult), ["lf", "rowmask"], ["lf"])
                    fcur, fprev = Fc[blk % 2], Fc[(blk + 1) % 2]
                    S.op("pe", lambda e: e.matmul(pq[1][:, 256:264], tri_f[:], lf[:], start=True, stop=False),
                         ["tri_f", "lf", ("pq", 1)], [("pq", 1)])
                    S.op("pe", lambda e, fprev=fprev: e.matmul(pq[1][:, 256:264], e127[:], fprev[:], start=False, stop=True),
                         ["e127", ("Fc", (blk + 1) % 2)], [("pq", 1)])
                    S.op("act", lambda e, fcur=fcur: e.copy(fcur[:], pq[1][:, 256:264]), [("pq", 1)], [("Fc", blk % 2)])
                    FN = ("Fc", blk % 2)
                    for (dstb, c0, sgn, nm) in ((fqbf, 64, -1.0, "fqbf"), (fkbf, 67, 1.0, "fkbf")):
                        S.op("pool", lambda e, sgn=sgn, fcur=fcur: e.tensor_scalar(fr[:], fcur[:], sgn, None, ALU.mult), [FN], ["fr"])
                        S.op("pool", lambda e: e.tensor_copy(fhi[:], fr[:]), ["fr"], ["fhi"])
                        S.op("pool", lambda e: e.tensor_tensor(fr[:], fr[:], fhi[:], ALU.subtract), ["fr", "fhi"], ["fr"])
                        S.op("pool", lambda e: e.tensor_copy(fmid[:], fr[:]), ["fr"], ["fmid"])
                        S.op("pool", lambda e: e.tensor_tensor(fr[:], fr[:], fmid[:], ALU.subtract), ["fr", "fmid"], ["fr"])
                        S.op("pool", lambda e, dstb=dstb, c0=c0: e.tensor_copy(dstb[:, :, c0:c0 + 1], fhi[:].unsqueeze(2)), ["fhi"], [nm])
                        S.op("pool", lambda e, dstb=dstb, c0=c0: e.tensor_copy(dstb[:, :, c0 + 1:c0 + 2], fmid[:].unsqueeze(2)), ["fmid"], [nm])
                        S.op("pool", lambda e, dstb=dstb, c0=c0: e.tensor_copy(dstb[:, :, c0 + 2:c0 + 3], fr[:].unsqueeze(2)), ["fr"], [nm])
                    for (c0, gt, gname, dstb, nm, dT, hk) in ((672, gqf, "gqf", fqbf, "fqbf", qt_d, 2), (1184, gkf, "gkf", fkbf, "fkbf", kt_d, 3)):
                        src3 = z[:, c0:c0 + 512].rearrange("p (h d) -> p h d", d=64)
                        head_rstd(src3, 64)
                        S.op("dve", lambda e, src3=src3: e.tensor_tensor(qn[:, :, 0:64], src3, bc8(r8[:], 64), ALU.mult), [zn, "r8"], ["qn"])
                        S.op("dve", lambda e, gt=gt, dstb=dstb: e.tensor_tensor(dstb[:, :, 0:64], qn[:, :, 0:64], bch(gt[:], 64), ALU.mult),
                             ["qn", gname], [nm])
                        hs = transpose_heads(dstb, nm, 70, dT, hk % 2, hk)
                        S.dma("sp", lambda e, hs=hs, r0=r0, dT=dT: e.dma_start(
                            out=dT[8:16, 0:70, r0:r0 + 128].rearrange("h d t -> d h t"), in_=hs[0:70, :, :]),
                              reads=[("hTsb", hk), "fq1", "fk1"], writes=[("qkt", hk, blk)])
                    S.op("act", lambda e: e.copy(vaug[:, 8:16, 0:64], z[:, 1696:2208].rearrange("p (h d) -> p h d", d=64)), [zn], ["vaug"])
                    if blk == 0:
                        S.op("dve", lambda e: e.tensor_scalar(vaug[:].rearrange("p h c -> p (h c)"), vaug[:].rearrange("p h c -> p (h c)"),
                                                              rowmask[:, 0:1], None, ALU.mult), ["vaug", "vones", "rowmask"], ["vaug", "vones"])
                    S.dma("sp", lambda e, blk=blk: e.dma_start(out=v_d[:, :, blk, :].rearrange("h p c -> p h c"), in_=vaug[:, :, :]),
                          reads=["vaug", "vones"], writes=[("vd", blk)])
                    if blk == 0:
                        S.op("pool", lambda e: e.memset(vaug[:, :, 64:80], 1.0), ["vaug", "vones"], ["vones"])
                stage1(0)
                for blk in range(nblk):
                    if blk + 1 < nblk:
                        stage1(blk + 1)
                    stage2(blk)
                flast = Fc[(nblk - 1) % 2]
                S.op("pe", lambda e: e.matmul(pq[1][:, 256:264], e127[:], flast[:], start=True, stop=True),
                     ["e127", ("Fc", (nblk - 1) % 2), ("pq", 1)], [("pq", 1)])
                S.op("dve", lambda e: e.tensor_scalar(fr[:], pq[1][:, 256:264], -1.0, None, ALU.mult), [("pq", 1)], ["fr"])
                S.dma("sp", lambda e: e.dma_start(out=tot_s.ap(), in_=fr[:]), reads=["fr"], writes=["tot_s"])
                S.barrier()
            S.cc(lambda e: e.collective_compute("AllGather", ALU.bypass, replica_groups=RG,
                                                ins=[tot_s.ap().opt()], outs=[tot_g.ap().opt()]), [], ["totg"])

            def gather_pair(g):
                S.cc(lambda e: e.collective_compute("AllGather", ALU.bypass, replica_groups=RG,
                                                    ins=[kt2.ap()[g * 192:(g + 1) * 192, :].opt()],
                                                    outs=[ktg2.ap()[g * 384:(g + 1) * 384, :].opt()]), [], [("ktg", g)])
                S.cc(lambda e: e.collective_compute("AllGather", ALU.bypass, replica_groups=RG,
                                                    ins=[v2.ap()[g * 256:(g + 1) * 256, :].opt()],
                                                    outs=[vg2.ap()[g * 512:(g + 1) * 512, :].opt()]), [], [("vg", g)])

            gather_pair(0)
            gather_pair(1)

            with contextlib.ExitStack() as ps:
                KT = [sbt(ps, f"KT{i}", [96, T], BF16) for i in range(2)]
                QT = [sbt(ps, f"QT{i}", [96, T], BF16) for i in range(2)]
                VA = [sbt(ps, f"VA{i}", [128, nblk, 80], BF16) for i in range(2)]
                KTo = [sbt(ps, f"KTo{i}", [96, T], BF16) for i in range(2)]
                VAo = [sbt(ps, f"VAo{i}", [128, nblk, 80], BF16) for i in range(2)]
                negc0 = sbt(ps, "negc0", [128, 8], F32)
                pT = [sbt(ps, f"pT{i}", [128, 1024], BF16) for i in range(4)]
                xsb = [sbt(ps, f"xsb{i}", [65, 512], F32) for i in range(2)]
                rec = sbt(ps, "rec", [64, 512], F32)
                oTs = [sbt(ps, f"oTs{i}", [64, 512], BF16) for i in range(2)]
                sel = sbt(ps, "sel", [65, 64], F32)
                S.op("pool", lambda e: e.memset(sel[:], 0.0), [], ["sel"])
                S.op("pool", lambda e: e.memset(sel[64:65, :], 1.0), ["sel"], ["sel"])
                pss = [pst(ps, f"ps{i}", [128, 1024], F32) for i in range(3)]
                po = [pst(ps, f"po{i}", [128, 512], F32) for i in range(1)]
                pbc = pst(ps, "pbc", [64, 512], F32)
                S.dma("sp", lambda e: e.dma_start(out=negc0[:], in_=tot_g.ap()[0:128, :]), reads=["totg"], writes=["negc0"])

                def load_head(h):
                    dk = 96 if h < 8 else 70
                    b = h % 2
                    if h % 2 == 0 and h // 2 + 1 < 8 and h >= 2:
                        gather_pair(h // 2 + 1)
                    S.dma("sp", lambda e: e.dma_start(out=KT[b][0:dk, :], in_=kt_d[h, 0:dk, :]), writes=[("KT", b)])
                    S.dma("sp", lambda e: e.dma_start(out=QT[b][0:dk, :], in_=qt_d[h, 0:dk, :]), writes=[("QT", b)])
                    S.dma("sp", lambda e: e.dma_start(out=VA[b][:, :, :], in_=v_d[h]), writes=[("VA", b)])
                    S.dma("sp", lambda e: e.dma_start(out=KTo[b][0:dk, :], in_=ktg_h(h)[0:dk, :]), reads=[("ktg", h // 2)], writes=[("KTo", b)])
                    S.dma("sp", lambda e: e.dma_start(out=VAo[b][:, :, :], in_=vg_h(h)), reads=[("vg", h // 2)], writes=[("VAo", b)])
                    S.op("dve", lambda e: e.tensor_scalar(KTo[b][0:dk, :], KTo[b][0:dk, :], a1[0:dk, 0:1], None, ALU.mult),
                         [("KTo", b), "a1"], [("KTo", b)])
                    S.op("dve", lambda e: e.tensor_scalar(VAo[b][:].rearrange("p b c -> p (b c)"), VAo[b][:].rearrange("p b c -> p (b c)"),
                                                           a1[:, 0:1], None, ALU.mult), [("VAo", b), "a1"], [("VAo", b)])

                groups = []
                for h in range(16):
                    for qb0 in range(0, nblk, 4):
                        nq = min(4, nblk - qb0)
                        tiles = [("o", t) for t in range(nblk)] + [("k", kt) for kt in range(qb0)]
                        for a in range(0, len(tiles) - 1, 2):
                            if tiles[a][0] == tiles[a + 1][0]:
                                groups.append((h, qb0, nq, [tiles[a], tiles[a + 1]]))
                            else:
                                groups.append((h, qb0, nq, [tiles[a]]))
                                groups.append((h, qb0, nq, [tiles[a + 1]]))
                        if len(tiles) % 2:
                            groups.append((h, qb0, nq, [tiles[-1]]))
                        for kt in range(qb0, qb0 + nq):
                            groups.append((h, qb0, nq, [("k", kt)]))
                oi = [0]

                def front(gi):
                    h, qb0, nq, tiles = groups[gi]
                    dk = 96 if h < 8 else 70
                    b = h % 2
                    sp_, pk = gi % 3, gi % 4
                    kind0, kt0 = tiles[0]
                    first = (max(kt0, qb0) - qb0) if kind0 == "k" else 0
                    c0, c1 = first * 128, nq * 128
                    for jx, (kind, kt) in enumerate(tiles):
                        src_k = KTo[b] if kind == "o" else KT[b]
                        rk = ("KTo", b) if kind == "o" else ("KT", b)
                        S.op("pe", lambda e, jx=jx, kt=kt, src_k=src_k: e.matmul(
                            pss[sp_][:, jx * 512 + c0:jx * 512 + c1], src_k[0:dk, kt * 128:(kt + 1) * 128],
                            QT[b][0:dk, qb0 * 128 + c0:qb0 * 128 + c1], start=True, stop=True), [rk, ("QT", b)], [("ps", sp_)])
                    nt = len(tiles)
                    if nt == 2:
                        o_ap = pT[pk][:, :].rearrange("p (j c) -> p j c", j=2)[:, :, c0:c1]
                        i_ap = pss[sp_][:, :].rearrange("p (j c) -> p j c", j=2)[:, :, c0:c1]
                    else:
                        o_ap, i_ap = pT[pk][:, c0:c1], pss[sp_][:, c0:c1]
                    if kind0 == "o" and h >= 8:
                        S.op("act", lambda e: e.activation(o_ap, i_ap, AF.Exp, bias=negc0[:, h - 8:h - 7]),
                             [("ps", sp_), "negc0"], [("pT", pk)])
                    else:
                        S.op("act", lambda e: e.activation(o_ap, i_ap, AF.Exp), [("ps", sp_)], [("pT", pk)])
                    if kind0 == "k" and kt0 >= qb0:
                        S.op("dve", lambda e: e.tensor_tensor(
                            pT[pk][:, c0:c0 + 128], pT[pk][:, c0:c0 + 128], tri_bf[:], ALU.mult), [("pT", pk), "tri_bf"], [("pT", pk)])

                def back(gi):
                    h, qb0, nq, tiles = groups[gi]
                    b = h % 2
                    pk = gi % 4
                    cb = 0
                    c1 = nq * 128
                    for jx, (kind, kt) in enumerate(tiles):
                        if kind == "o":
                            S.op("pe", lambda e, jx=jx, kt=kt: e.matmul(po[cb][0:65, 0:c1], VAo[b][:, kt, 0:65], pT[pk][:, jx * 512:jx * 512 + c1],
                                                                        start=(kt == 0), stop=False), [("pT", pk), ("VAo", b)], [("po", cb)])
                            continue
                        first = max(kt, qb0) - qb0
                        c0 = first * 128
                        last = kt == qb0 + nq - 1
                        S.op("pe", lambda e, jx=jx, kt=kt, c0=c0, last=last: e.matmul(
                            po[cb][0:65, c0:c1], VA[b][:, kt, 0:65], pT[pk][:, jx * 512 + c0:jx * 512 + c1],
                            start=False, stop=last), [("pT", pk), ("VA", b)], [("po", cb)])
                        if last:
                            x = oi[0] % 2
                            oi[0] += 1
                            S.op("dve", lambda e: e.tensor_copy(xsb[x][0:65, 0:c1], po[cb][0:65, 0:c1]), [("po", cb)], [("xsb", x)])
                            S.op("pe", lambda e: e.matmul(pbc[0:64, 0:c1], sel[0:65, 0:64], xsb[x][0:65, 0:c1], start=True, stop=True),
                                 ["sel", ("xsb", x)], ["pbc"])
                            S.op("dve", lambda e: e.tensor_scalar(rec[0:64, 0:c1], pbc[0:64, 0:c1], 1e-30, None, ALU.max), ["pbc"], ["rec"])
                            S.op("dve", lambda e: e.reciprocal(rec[0:64, 0:c1], rec[0:64, 0:c1]), ["rec"], ["rec"])
                            S.op("dve", lambda e: e.tensor_tensor(oTs[x][0:64, 0:c1], xsb[x][0:64, 0:c1], rec[0:64, 0:c1], ALU.mult),
                                 [("xsb", x), "rec"], [("oTs", x)])
                            S.dma("sp", lambda e: e.dma_start(out=oT_d[h * 64:(h + 1) * 64, qb0 * 128:qb0 * 128 + c1], in_=oTs[x][0:64, 0:c1]),
                                  reads=[("oTs", x)], writes=[("od", h, qb0)])
                    if gi + 1 == len(groups) or groups[gi + 1][0] != h:
                        if h + 2 < 16:
                            load_head(h + 2)

                if "a3" in SKIP:
                    groups = []
                else:
                    load_head(0)
                    load_head(1)
                LA = 2
                for gi in range(len(groups) + LA):
                    if gi < len(groups):
                        front(gi)
                    if gi - LA >= 0:
                        back(gi - LA)
                S.barrier()

            with contextlib.ExitStack() as ps:
                wout = sbt(ps, "awout", [128, 8, D], BF16)
                wst = [sbt(ps, f"wstg{i}", [128, NS], F32) for i in range(3)]
                h4_t = [sbt(ps, f"h{i}", [128, D], F32) for i in range(2)]
                o_t = [sbt(ps, f"o{i}", [128, D], BF16) for i in range(2)]
                oT = [sbt(ps, f"oT{i}", [128, 8, 128], BF16) for i in range(2)]
                tT4 = [pst(ps, f"tT4{i}", [128, 1024], BF16) for i in range(2)]
                pd = [pst(ps, f"pd{i}", [128, 512], F32) for i in range(4)]
                def load4(bb):
                    S.dma("sp", lambda e: e.dma_start(out=h4_t[bb % 2][:], in_=src[bb * 128:(bb + 1) * 128, :]), writes=[("h", bb % 2)])
                    S.dma("sp", lambda e: e.dma_start(out=oT[bb % 2][:, :, :],
                                                      in_=oT_d[:, bb * 128:(bb + 1) * 128].rearrange("(c p) t -> p c t", p=128)),
                          writes=[("oT", bb % 2)])

                load4(0)
                if nblk > 1:
                    load4(1)
                load_weight(wst, wout, "awout", w_out_attn[j], D, D)
                for blk in range(nblk):
                    r0 = blk * 128
                    k = blk % 2
                    if blk >= 1 and blk + 1 < nblk:
                        load4(blk + 1)
                    for dh in range(2):
                        pk = (blk * 2 + dh) % 4
                        for kc in range(8):
                            S.op("pe", lambda e, k=k, kc=kc, dh=dh, pk=pk: e.matmul(
                                pd[pk][:, :], oT[k][:, kc, :], wout[:, kc, dh * 512:(dh + 1) * 512], start=(kc == 0), stop=(kc == 7)),
                                 [("oT", k)] + wr("awout", kc, dh * 512, dh * 512 + 512), [("pd", pk)])
                        S.op("dve", lambda e, k=k, dh=dh, pk=pk: e.tensor_tensor(
                            h4_t[k][:, dh * 512:(dh + 1) * 512], pd[pk][:, :], h4_t[k][:, dh * 512:(dh + 1) * 512], ALU.add),
                             [("pd", pk), ("h", k)], [("h", k)])
                    S.dma("sp", lambda e, k=k, r0=r0: e.dma_start(out=dst[r0:r0 + 128, :], in_=h4_t[k][:]), reads=[("h", k)], writes=[("hrows", blk)])
                S.barrier()

        li = {"attn": 0, "conv": 0}
        for pi, ph in enumerate(layers):
            is_last = pi == len(layers) - 1
            layer = pi // 2
            if ph == "mlp":
                mlp_phase(layer, is_last)
            elif ph == "conv":
                conv_phase(li["conv"], layer, is_last)
                li["conv"] += 1
            else:
                attn_phase(li["attn"], layer, is_last)
                li["attn"] += 1
        S.barrier()
        S.emit()
        print("n_ops", S.n_ops)
    return nc


def host_consts():
    tri = np.triu(np.ones((128, 128), np.float32))
    e127 = np.zeros((128, 128), np.float32)
    e127[127, :] = 1.0
    return {"ident": np.eye(128, dtype=np.float32), "tri": tri, "e127": e127}


def rope_tables(nblk, rank):
    T = nblk * 128
    pos = (np.arange(T, dtype=np.float32) + np.float32(rank * T - PAD)).astype(np.float32)
    inv_freq = (np.float32(10000.0) ** (-np.arange(0, ROPE, 2, dtype=np.float32) / np.float32(ROPE))).astype(np.float32)
    ang = (pos[:, None] * inv_freq[None, :]).astype(np.float32)
    cos = np.cos(ang).astype(np.float32).reshape(nblk, 128, 16).transpose(1, 0, 2)
    sin = np.sin(ang).astype(np.float32).reshape(nblk, 128, 16).transpose(1, 0, 2)
    return np.ascontiguousarray(cos), np.ascontiguousarray(sin)


_WNAMES = ["g_mix", "g_mlp", "w_in_attn", "g_cq", "w_uq", "g_ckv", "w_ukv", "g_q_mla", "g_k_mla", "g_q_fox",
           "g_k_fox", "b_forget", "w_out_attn", "w_in_conv", "conv_w", "w_out_conv", "w_mlp_up", "w_mlp_down"]


def run(inputs, layers=("attn", "mlp", "conv", "mlp", "attn", "mlp", "conv", "mlp"), n_cores=8, debug=False, trace=False):
    x = np.asarray(inputs["x"], np.float32)
    B, SEQ, _ = x.shape
    L = NMETA + SEQ
    nfull = (PAD + L) // 128
    assert nfull * 128 == PAD + L
    nblk = (nfull + 1) // 2
    T = nblk * 128
    nc = build_program(nblk, layers, debug, n_cores)
    consts = host_consts()
    meta = np.asarray(inputs["meta_tokens"], np.float32)
    weights = {n: np.ascontiguousarray(np.asarray(inputs[n], np.float32)) for n in _WNAMES}
    in_maps = []
    for c in range(n_cores):
        b, rank = (c // 2) % B, c % 2
        xp = np.zeros((2 * T, D), np.float32)
        xp[PAD:PAD + NMETA] = meta
        xp[PAD + NMETA:PAD + L] = x[b]
        cos, sin = rope_tables(nblk, rank)
        rowmask = np.ones((128, 1), np.float32)
        if rank == 0:
            rowmask[:PAD] = 0.0
        m = {"x_pad": np.ascontiguousarray(xp[rank * T:(rank + 1) * T]), "cos_t": cos, "sin_t": sin,
             "a1": np.full((128, 1), float(rank), np.float32), "rowmask0": rowmask}
        m.update(weights)
        m.update(consts)
        in_maps.append(m)
    res = run_bass_kernel_spmd(nc, in_maps, core_ids=list(range(n_cores)), **({"trace": True} if trace else {}))
    if trace:
        print("EXEC_TIME_NS", res.exec_time_ns)
    out = np.zeros((B, SEQ, D), np.float32)
    for b in range(min(B, n_cores // 2)):
        full = np.concatenate([res.results[2 * b]["y"], res.results[2 * b + 1]["y"]], axis=0)
        out[b] = full[PAD + NMETA:PAD + L]
    return out


def kernel(**inputs):
    return run(inputs)
```

```python
import contextlib
import numpy as np
import concourse.bass as bass
import concourse.mybir as mybir
from concourse.bass_utils import run_bass_kernel_spmd

F32 = mybir.dt.float32
BF16 = mybir.dt.bfloat16
AF = mybir.ActivationFunctionType
ALU = mybir.AluOpType
AX = mybir.AxisListType

D = 1024
DFF = 4096
NMETA = 16
PAD = 112
EPS = 1e-6
QL, KVL, ROPE = 384, 256, 32
ATTN_IN = 2216
ENGS = ("pe", "dve", "act", "pool", "sp")
SKIP = set()


class Sched:
    def __init__(self, nc, stack):
        self.nc = nc
        self.streams = {e: [] for e in ENGS}
        self.sem, self.cnt = {}, {}
        self.waited = {e: {} for e in ENGS}
        self.last_w, self.readers = {}, {}
        for e in ENGS:
            self.sem["e_" + e] = stack.enter_context(nc.semaphore("e_" + e))
            self.cnt["e_" + e] = 0
        self.dma_pool, self.dma_rr = {}, {}
        for q, n in {"sp": 12, "pool": 4, "act": 2}.items():
            names = []
            for i in range(n):
                nm = f"d_{q}{i}"
                self.sem[nm] = stack.enter_context(nc.semaphore(nm))
                self.cnt[nm] = 0
                names.append(nm)
            self.dma_pool[q], self.dma_rr[q] = names, 0
        self.cc_pool, self.cc_rr = [], 0
        for i in range(6):
            nm = f"c_{i}"
            self.sem[nm] = stack.enter_context(nc.semaphore(nm))
            self.cnt[nm] = 0
            self.cc_pool.append(nm)
        self.n_ops = 0

    def _deps(self, reads, writes):
        deps = {}

        def add(nm, val, eng):
            if deps.get(nm, (0, None))[0] < val:
                deps[nm] = (val, eng)

        for r in reads:
            if r in self.last_w:
                add(*self.last_w[r])
        for w in writes:
            if w in self.last_w:
                add(*self.last_w[w])
            for nm, (val, eng) in self.readers.get(w, {}).items():
                add(nm, val, eng)
        return deps

    def _record(self, tok, reads, writes):
        nm, val, eng = tok
        for r in reads:
            d = self.readers.setdefault(r, {})
            if d.get(nm, (0, None))[0] < val:
                d[nm] = (val, eng)
        for w in writes:
            self.last_w[w] = tok
            self.readers[w] = {}

    def _add_waits(self, eng, deps):
        for nm, (val, dep_eng) in deps.items():
            if dep_eng == eng and eng == "pe":
                continue
            if self.waited[eng].get(nm, 0) >= val:
                continue
            self.waited[eng][nm] = val
            self.streams[eng].append(("wait", nm, val))

    def op(self, eng, fn, reads=(), writes=()):
        self._add_waits(eng, self._deps(reads, writes))
        nm = "e_" + eng
        self.cnt[nm] += 1
        tok = (nm, self.cnt[nm], eng)
        self.streams[eng].append(("op", fn, nm, 1))
        self._record(tok, reads, writes)
        self.n_ops += 1

    def dma(self, q, fn, reads=(), writes=()):
        deps = self._deps(reads, writes)
        pool = self.dma_pool[q]
        nm = pool[self.dma_rr[q] % len(pool)]
        self.dma_rr[q] += 1
        if self.cnt[nm] > 0 and deps.get(nm, (0, None))[0] < self.cnt[nm]:
            deps[nm] = (self.cnt[nm], None)
        self._add_waits(q, deps)
        self.cnt[nm] += 16
        self.streams[q].append(("op", fn, nm, 16))
        self._record((nm, self.cnt[nm], None), reads, writes)
        self.n_ops += 1

    def cc(self, fn, reads=(), writes=()):
        deps = self._deps(reads, writes)
        nm = self.cc_pool[self.cc_rr % len(self.cc_pool)]
        self.cc_rr += 1
        if self.cnt[nm] > 0 and deps.get(nm, (0, None))[0] < self.cnt[nm]:
            deps[nm] = (self.cnt[nm], None)
        self._add_waits("pool", deps)
        self.cnt[nm] += 1
        self.streams["pool"].append(("op", fn, nm, 1))
        self._record((nm, self.cnt[nm], None), reads, writes)
        self.n_ops += 1

    def barrier(self):
        for e in ENGS:
            deps = {nm: (v, None) for nm, v in self.cnt.items() if v > 0 and nm != "e_" + e}
            self._add_waits(e, deps)

    def emit(self):
        nc = self.nc
        with nc.Block() as block:
            def run(e, engobj):
                for ent in self.streams[e]:
                    if ent[0] == "wait":
                        engobj.wait_ge(self.sem[ent[1]], ent[2])
                    else:
                        ent[1](engobj).then_inc(self.sem[ent[2]], ent[3])

            block.tensor(lambda t: run("pe", t))
            block.vector(lambda v: run("dve", v))
            block.scalar(lambda s: run("act", s))
            block.gpsimd(lambda g: run("pool", g))
            block.sync(lambda s: run("sp", s))


def build_program(nblk, layers=("attn", "mlp", "conv", "mlp", "attn", "mlp", "conv", "mlp"), debug=False, n_cores=8):
    T = nblk * 128
    nc = bass.Bass("TRN2", target_bir_lowering=False)
    di = lambda name, shape, dt=F32: nc.dram_tensor(name, list(shape), dt, kind="ExternalInput").ap()
    x_pad = di("x_pad", [T, D])
    g_mix = di("g_mix", [4, D]); g_mlp = di("g_mlp", [4, D])
    w_in_attn = di("w_in_attn", [2, D, ATTN_IN]); g_cq = di("g_cq", [2, QL]); w_uq = di("w_uq", [2, QL, 768])
    g_ckv = di("g_ckv", [2, KVL]); w_ukv = di("w_ukv", [2, KVL, 1024])
    g_q_mla = di("g_q_mla", [2, 96]); g_k_mla = di("g_k_mla", [2, 96])
    g_q_fox = di("g_q_fox", [2, 64]); g_k_fox = di("g_k_fox", [2, 64]); b_forget = di("b_forget", [2, 8])
    w_out_attn = di("w_out_attn", [2, D, D])
    w_in_conv = di("w_in_conv", [2, D, 3 * D]); conv_w = di("conv_w", [2, 3, D]); w_out_conv = di("w_out_conv", [2, D, D])
    w_mlp_up = di("w_mlp_up", [4, D, DFF]); w_mlp_down = di("w_mlp_down", [4, DFF, D])
    cos_d = di("cos_t", [128, nblk, 16]); sin_d = di("sin_t", [128, nblk, 16])
    ident_d = di("ident", [128, 128]); tri_d = di("tri", [128, 128]); e127_d = di("e127", [128, 128])
    a1_d = di("a1", [128, 1]); rowmask_d = di("rowmask0", [128, 1])
    y_out = nc.dram_tensor("y", [T, D], F32, kind="ExternalOutput").ap()
    h_d = nc.dram_tensor("h_scr", [T, D], F32).ap()
    skind = "ExternalOutput" if debug else "Internal"
    oT_d = nc.dram_tensor("oT_scr", [D, T], BF16, kind=skind).ap()
    qt_d = nc.dram_tensor("qt_scr", [16, 96, T], BF16).ap()
    kt2 = nc.dram_tensor("kt_scr", [16 * 96, T], BF16)
    kt_d = kt2.ap().rearrange("(h d) t -> h d t", d=96)
    v2 = nc.dram_tensor("v_scr", [16 * 128, nblk * 80], BF16)
    v_d = v2.ap().rearrange("(h p) (b c) -> h p b c", p=128, c=80)
    ktg2 = nc.dram_tensor("ktg", [2 * 16 * 96, T], BF16)
    ktg_h = lambda h: ktg2.ap()[(h // 2) * 384 + (h % 2) * 96:(h // 2) * 384 + (h % 2) * 96 + 96, :]
    vg2 = nc.dram_tensor("vg", [2 * 16 * 128, nblk * 80], BF16)
    vg_h = lambda h: vg2.ap()[(h // 2) * 512 + (h % 2) * 128:(h // 2) * 512 + (h % 2) * 128 + 128, :].rearrange("p (b c) -> p b c", c=80)
    tot_s = nc.dram_tensor("tot_s", [128, 8], F32)
    tot_g = nc.dram_tensor("tot_g", [256, 8], F32)
    gh_s = nc.dram_tensor("gh_s", [128, 16], F32)
    gh_g = nc.dram_tensor("gh_g", [256, 16], F32)
    RG = [[2 * i, 2 * i + 1] for i in range(n_cores // 2)]
    if debug:
        zdbg = nc.dram_tensor("zdbg", [nblk, 128, ATTN_IN], F32, kind="ExternalOutput").ap()
        hndbg = nc.dram_tensor("hndbg", [nblk, 128, 8, 128], BF16, kind="ExternalOutput").ap()
        windbg = nc.dram_tensor("windbg", [8, 128, 2304], BF16, kind="ExternalOutput").ap()

    with contextlib.ExitStack() as st:
        S = Sched(nc, st)
        uid = [0]

        def sbt(stack, name, shape, dt):
            uid[0] += 1
            return stack.enter_context(nc.sbuf_tensor(f"{name}_{uid[0]}", list(shape), dt))

        def pst(stack, name, shape, dt):
            uid[0] += 1
            return stack.enter_context(nc.psum_tensor(f"{name}_{uid[0]}", list(shape), dt))

        ident = sbt(st, "ident", [128, 128], BF16)
        tri_bf = sbt(st, "tri_bf", [128, 128], BF16)
        tri_f = sbt(st, "tri_f", [128, 128], F32)
        e127 = sbt(st, "e127", [128, 128], F32)
        cos_t = sbt(st, "cos", [128, nblk, 16], F32)
        sin_t = sbt(st, "sin", [128, nblk, 16], F32)
        S.dma("pool", lambda e: e.dma_start(out=ident[:], in_=ident_d), writes=["ident"])
        S.dma("pool", lambda e: e.dma_start(out=tri_bf[:], in_=tri_d), writes=["tri_bf"])
        S.dma("sp", lambda e: e.dma_start(out=tri_f[:], in_=tri_d), writes=["tri_f"])
        S.dma("sp", lambda e: e.dma_start(out=e127[:], in_=e127_d), writes=["e127"])
        S.dma("sp", lambda e: e.dma_start(out=cos_t[:], in_=cos_d), writes=["cos"])
        S.dma("sp", lambda e: e.dma_start(out=sin_t[:], in_=sin_d), writes=["sin"])
        a1 = sbt(st, "a1", [128, 1], F32)
        rowmask = sbt(st, "rowmask", [128, 1], F32)
        S.dma("sp", lambda e: e.dma_start(out=a1[:], in_=a1_d), writes=["a1"])
        S.dma("sp", lambda e: e.dma_start(out=rowmask[:], in_=rowmask_d), writes=["rowmask"])

        if debug:
            with contextlib.ExitStack() as fs:
                fill = sbt(fs, "fill", [128, 25000], F32)
                S.op("dve", lambda e: e.memset(fill[:, 0:12500], 7.0), [], ["fillA"])
                S.op("pool", lambda e: e.memset(fill[:, 12500:25000], 7.0), [], ["fillB"])
                S.barrier()
        cast_rr = [0]

        NS = 1024

        def load_weight(wst, dst, dst_name, w_ap, K, N, gain=None, gname=None, col_major=False):
            KC = K // 128
            chunks = [(kc, n0) for kc in range(KC) for n0 in range(0, N, NS)]
            if col_major:
                chunks = [(kc, n0) for n0 in range(0, N, NS) for kc in range(KC)]
            for (kc, n0) in chunks:
                if True:
                    n1 = min(N, n0 + NS)
                    i = cast_rr[0]
                    cast_rr[0] += 1
                    stg = wst[i % len(wst)]
                    sname = ("wstg", i % len(wst))
                    S.dma("sp", lambda e, stg=stg, kc=kc, n0=n0, n1=n1: e.dma_start(
                        out=stg[:, 0:n1 - n0], in_=w_ap[kc * 128:(kc + 1) * 128, n0:n1]), writes=[sname])
                    eng = ("act", "dve")[i % 2]
                    rd = [sname] + ([gname] if gain is not None else [])
                    wr = [(dst_name, kc, n0)]
                    if gain is None:
                        if eng == "act":
                            S.op("act", lambda e, stg=stg, kc=kc, n0=n0, n1=n1: e.copy(dst[:, kc, n0:n1], stg[:, 0:n1 - n0]), rd, wr)
                        else:
                            S.op(eng, lambda e, stg=stg, kc=kc, n0=n0, n1=n1: e.tensor_copy(dst[:, kc, n0:n1], stg[:, 0:n1 - n0]), rd, wr)
                    else:
                        if eng == "act":
                            S.op("act", lambda e, stg=stg, kc=kc, n0=n0, n1=n1: e.activation(
                                dst[:, kc, n0:n1], stg[:, 0:n1 - n0], AF.Copy, scale=gain[:, kc:kc + 1]), rd, wr)
                        else:
                            S.op(eng, lambda e, stg=stg, kc=kc, n0=n0, n1=n1: e.tensor_scalar(
                                dst[:, kc, n0:n1], stg[:, 0:n1 - n0], gain[:, kc:kc + 1], None, ALU.mult), rd, wr)

        def wr(dst_name, kc, c0, c1):
            return [(dst_name, kc, n0) for n0 in range((c0 // NS) * NS, c1, NS)]

        def load_gain_cols(dst, name, g_row_ap, K):
            S.dma("sp", lambda e: e.dma_start(out=dst[:, 0:K // 128], in_=g_row_ap.rearrange("(kc p) -> p kc", p=128),
                                              allow_slow_non_contiguous=True), writes=[name])

        def rmsnorm_T(h_ap, hname, hnT, hnT_name, col0, tmp, tT, tT_name, width=D, src_res=None, pre=""):
            junk, ss, rstd, hn_bf, nm = tmp
            rd = [hname] if src_res is None else src_res
            S.op("act", lambda e: e.activation(junk[:, 0:width], h_ap, AF.Square, accum_out=ss[:, 0:1]),
                 rd, [pre + "junk", pre + "ss"])
            S.op("dve", lambda e: e.tensor_scalar(rstd[:, 0:1], ss[:, 0:1], 1.0 / width, EPS, ALU.mult, ALU.add),
                 [pre + "ss"], [pre + "rstd"])
            S.op("act", lambda e: e.activation(rstd[:, 0:1], rstd[:, 0:1], AF.Ln), [pre + "rstd"], [pre + "rstd"])
            S.op("act", lambda e: e.activation(rstd[:, 0:1], rstd[:, 0:1], AF.Exp, scale=-0.5), [pre + "rstd"], [pre + "rstd"])
            S.op("act", lambda e: e.activation(hn_bf[:, 0:width], h_ap, AF.Copy, scale=rstd[:, 0:1]),
                 rd + [pre + "rstd"], [nm + "hn"])
            KC = width // 128
            for kc in range(KC):
                S.op("pe", lambda e, kc=kc: e.transpose(tT[:, kc * 128:(kc + 1) * 128], hn_bf[:, kc * 128:(kc + 1) * 128], ident[:]),
                     [nm + "hn", "ident"], [tT_name])
            S.op("dve", lambda e: e.tensor_copy(hnT[:, 0:KC, col0:col0 + 128],
                                                tT[:, 0:KC * 128].rearrange("p (k t) -> p k t", t=128)),
                 [tT_name], [hnT_name])

        first_phase = [True]

        def src_dst(is_last):
            src = x_pad if first_phase[0] else h_d
            first_phase[0] = False
            return src, (y_out if is_last else h_d)

        def mlp_phase(layer, is_last):
            src, dst = src_dst(is_last)
            CT = 256
            with contextlib.ExitStack() as ps:
                wup = sbt(ps, "wup", [128, 8, DFF], BF16)
                wdn = sbt(ps, "wdn", [128, 32, D], BF16)
                gcol = sbt(ps, "gcol", [128, 8], F32)
                load_gain_cols(gcol, "gcol", g_mlp[layer], D)
                wst = [sbt(ps, f"wstg{i}", [128, NS], F32) for i in range(3)]
                h_t = [sbt(ps, f"h{i}", [128, 2, D], F32) for i in range(2)]
                junk = sbt(ps, "junk", [128, D], BF16)
                ss = sbt(ps, "ss", [128, 1], F32)
                rstd = sbt(ps, "rstd", [128, 1], F32)
                hn_bf = [sbt(ps, f"hnbf{i}", [128, D], BF16) for i in range(2)]
                hnT2 = [sbt(ps, f"hnT{i}", [128, 8, CT], BF16) for i in range(2)]
                r_t = [sbt(ps, f"r{i}", [128, CT], F32) for i in range(2)]
                h1T = sbt(ps, "h1T", [128, 32, CT], BF16)
                tT = [pst(ps, f"tT{i}", [128, 1024], BF16) for i in range(2)]
                pu = [pst(ps, f"pu{i}", [128, 512], F32) for i in range(3)]
                pd = [pst(ps, f"pd{i}", [128, 512], F32) for i in range(3)]
                nchunks = (T + CT - 1) // CT

                def load_chunk(c):
                    r0 = c * CT
                    nb = min(2, (T - r0) // 128)
                    ht = h_t[c % 2]
                    S.dma("sp", lambda e: e.dma_start(
                        out=ht[:, 0:nb, :], in_=src[r0:r0 + nb * 128, :].rearrange("(b p) d -> p b d", p=128)), writes=[("h", c % 2)])

                load_chunk(0)
                if nchunks > 1:
                    load_chunk(1)
                load_weight(wst, wup, "wup", w_mlp_up[layer], D, DFF, gain=gcol, gname="gcol", col_major=True)
                load_weight(wst, wdn, "wdn", w_mlp_down[layer], DFF, D)
                ti = 0
                ui = 0
                di_ = 0
                tic = [0]

                def norm_chunk(c):
                    nb_ = min(2, (T - c * CT) // 128)
                    for b in range(nb_):
                        k = tic[0] % 2
                        tic[0] += 1
                        rmsnorm_T(h_t[c % 2][:, b, :], ("h", c % 2), hnT2[c % 2], ("hnT", c % 2), b * 128,
                                  (junk, ss, rstd, hn_bf[k], f"n{k}"), tT[k], ("tT", k))

                norm_chunk(0)
                for c in range(nchunks):
                    r0 = c * CT
                    nb = min(2, (T - r0) // 128)
                    ncols = nb * 128
                    ht = h_t[c % 2]
                    hname = ("h", c % 2)
                    if c >= 1 and c + 1 < nchunks:
                        load_chunk(c + 1)
                    hnT, hnTn = hnT2[c % 2], ("hnT", c % 2)
                    for fc in range(32):
                        k = ui % 3
                        ui += 1
                        for kc in range(8):
                            S.op("pe", lambda e, k=k, kc=kc, fc=fc, ncols=ncols, hnT=hnT: e.matmul(
                                pu[k][:, 0:ncols], wup[:, kc, fc * 128:(fc + 1) * 128], hnT[:, kc, 0:ncols],
                                start=(kc == 0), stop=(kc == 7)), [hnTn] + wr("wup", kc, fc * 128, fc * 128 + 128), [("pu", k)])
                        rr = fc % 2
                        S.op("act", lambda e, k=k, rr=rr, ncols=ncols: e.activation(r_t[rr][:, 0:ncols], pu[k][:, 0:ncols], AF.Relu),
                             [("pu", k)], [("r", rr)])
                        S.op("dve", lambda e, rr=rr, fc=fc, ncols=ncols: e.tensor_tensor(
                            h1T[:, fc, 0:ncols], r_t[rr][:, 0:ncols], r_t[rr][:, 0:ncols], ALU.mult),
                             [("r", rr)], [("h1T", fc)])
                    if c + 1 < nchunks:
                        norm_chunk(c + 1)
                    for b in range(nb):
                        for dh in range(2):
                            k = di_ % 3
                            di_ += 1
                            for fc in range(32):
                                S.op("pe", lambda e, k=k, fc=fc, b=b, dh=dh: e.matmul(
                                    pd[k][:, :], h1T[:, fc, b * 128:(b + 1) * 128], wdn[:, fc, dh * 512:(dh + 1) * 512],
                                    start=(fc == 0), stop=(fc == 31)), [("h1T", fc)] + wr("wdn", fc, dh * 512, dh * 512 + 512), [("pd", k)])
                            S.op("dve", lambda e, k=k, ht=ht, b=b, dh=dh: e.tensor_tensor(
                                ht[:, b, dh * 512:(dh + 1) * 512], pd[k][:, :], ht[:, b, dh * 512:(dh + 1) * 512], ALU.add),
                                 [("pd", k), hname], [hname])
                    S.dma("sp", lambda e, ht=ht, r0=r0, nb=nb: e.dma_start(
                        out=dst[r0:r0 + nb * 128, :].rearrange("(b p) d -> p b d", p=128), in_=ht[:, 0:nb, :]),
                          reads=[hname], writes=[("hrows", c)])
                S.barrier()

        def conv_phase(j, layer, is_last):
            src, dst = src_dst(is_last)
            CT = 256
            with contextlib.ExitStack() as ps:
                win = sbt(ps, "cwin", [128, 8, 3 * D], BF16)
                wout = sbt(ps, "cwout", [128, 8, D], BF16)
                gcol = sbt(ps, "gcol", [128, 8], F32)
                cw = sbt(ps, "cw", [128, 3, 8], F32)
                load_gain_cols(gcol, "gcol", g_mix[layer], D)
                for tap in range(3):
                    S.dma("sp", lambda e, tap=tap: e.dma_start(
                        out=cw[:, tap, :], in_=conv_w[j, tap].rearrange("(kc p) -> p kc", p=128),
                        allow_slow_non_contiguous=True), writes=[("cw", tap)])
                CW = [("cw", t) for t in range(3)]
                wst = [sbt(ps, f"wstg{i}", [128, NS], F32) for i in range(3)]
                h_t = [sbt(ps, f"h{i}", [128, 2, D], F32) for i in range(2)]
                junk = sbt(ps, "junk", [128, D], BF16)
                ss = sbt(ps, "ss", [128, 1], F32)
                rstd = sbt(ps, "rstd", [128, 1], F32)
                hn_bf = [sbt(ps, f"hnbf{i}", [128, D], BF16) for i in range(2)]
                hnT = sbt(ps, "hnT", [128, 8, CT], BF16)
                gext = sbt(ps, "gext", [128, 8, CT + 2], F32)
                c_sb = [sbt(ps, f"csb{i}", [128, CT], F32) for i in range(2)]
                b_sb = [sbt(ps, f"bsb{i}", [128, CT], F32) for i in range(2)]
                acc = [sbt(ps, f"acc{i}", [128, CT], F32) for i in range(2)]
                mT = sbt(ps, "mT", [128, 8, CT], BF16)
                tT = [pst(ps, f"tT{i}", [128, 1024], BF16) for i in range(2)]
                pz = [pst(ps, f"pz{i}", [128, 512], F32) for i in range(4)]
                pd = [pst(ps, f"pd{i}", [128, 512], F32) for i in range(2)]
                ghs = sbt(ps, "ghs", [128, 8, 2], F32)
                ghl = sbt(ps, "ghl", [128, 16], F32)
                hlt = sbt(ps, "hlt", [128, D], F32)

                def conv_halo():
                    conv_halo_body()

                def conv_halo_body():
                  if True:
                    rmsnorm_T(hlt[:], "hlt", hnT, "hnT", 0, (junk, ss, rstd, hn_bf[0], "n0"), tT[0], ("tT", 0))
                    for fc in range(8):
                        for sec in (1, 2):
                            col = sec * D + fc * 128
                            for kc in range(8):
                                S.op("pe", lambda e, sec=sec, kc=kc, col=col: e.matmul(
                                    pz[sec][:, 0:128], win[:, kc, col:col + 128], hnT[:, kc, 0:128],
                                    start=(kc == 0), stop=(kc == 7)), ["hnT"] + wr("cwin", kc, col, col + 128), [("pz", sec)])
                        S.op("act", lambda e: e.copy(c_sb[0][:, 0:128], pz[1][:, 0:128]), [("pz", 1)], [("csb", 0)])
                        S.op("dve", lambda e, fc=fc: e.tensor_tensor(ghs[:, fc, :], pz[2][:, 126:128], c_sb[0][:, 126:128], ALU.mult),
                             [("pz", 2), ("csb", 0)], ["ghs"])
                    S.dma("sp", lambda e: e.dma_start(out=gh_s.ap(), in_=ghs[:].rearrange("p f c -> p (f c)")), reads=["ghs"], writes=["gh_s"])
                    S.cc(lambda e: e.collective_compute("AllGather", ALU.bypass, replica_groups=RG,
                                                        ins=[gh_s.ap().opt()], outs=[gh_g.ap().opt()]), ["gh_s"], ["gh_g"])
                    S.dma("sp", lambda e: e.dma_start(out=ghl[:], in_=gh_g.ap()[0:128, :]), reads=["gh_g"], writes=["ghl"])
                    S.op("dve", lambda e: e.tensor_scalar(gext[:, :, 0:2], ghl[:].rearrange("p (f c) -> p f c", c=2), a1[:, 0:1], None, ALU.mult),
                         ["ghl", "a1"], [("gh", fc) for fc in range(8)])
                nchunks = (T + CT - 1) // CT

                def load_chunk(c):
                    r0 = c * CT
                    nb = min(2, (T - r0) // 128)
                    ht = h_t[c % 2]
                    S.dma("sp", lambda e: e.dma_start(
                        out=ht[:, 0:nb, :], in_=src[r0:r0 + nb * 128, :].rearrange("(b p) d -> p b d", p=128)), writes=[("h", c % 2)])

                load_chunk(0)
                if nchunks > 1:
                    load_chunk(1)
                S.dma("sp", lambda e: e.dma_start(out=hlt[:], in_=src[T - 128:T, :]), writes=["hlt"])
                load_weight(wst, win, "cwin", w_in_conv[j], D, 3 * D, gain=gcol, gname="gcol")
                load_weight(wst, wout, "cwout", w_out_conv[j], D, D)
                conv_halo()
                ti = zi = di_ = 0
                for c in range(nchunks):
                    r0 = c * CT
                    nb = min(2, (T - r0) // 128)
                    ncols = nb * 128
                    ht = h_t[c % 2]
                    hname = ("h", c % 2)
                    if c >= 1 and c + 1 < nchunks:
                        load_chunk(c + 1)
                    for b in range(nb):
                        k = ti % 2
                        ti += 1
                        rmsnorm_T(ht[:, b, :], hname, hnT, "hnT", b * 128,
                                  (junk, ss, rstd, hn_bf[k], f"n{k}"), tT[k], ("tT", k))
                    for fc in range(8):
                        zk = []
                        for sec in range(3):
                            k = zi % 4
                            zi += 1
                            zk.append(k)
                            col = sec * D + fc * 128
                            for kc in range(8):
                                S.op("pe", lambda e, k=k, kc=kc, col=col, ncols=ncols: e.matmul(
                                    pz[k][:, 0:ncols], win[:, kc, col:col + 128], hnT[:, kc, 0:ncols],
                                    start=(kc == 0), stop=(kc == 7)), ["hnT"] + wr("cwin", kc, col, col + 128), [("pz", k)])
                        q = fc % 2
                        S.op("act", lambda e, q=q, k=zk[0], ncols=ncols: e.copy(b_sb[q][:, 0:ncols], pz[k][:, 0:ncols]),
                             [("pz", zk[0])], [("bsb", q)])
                        S.op("act", lambda e, q=q, k=zk[1], ncols=ncols: e.copy(c_sb[q][:, 0:ncols], pz[k][:, 0:ncols]),
                             [("pz", zk[1])], [("csb", q)])
                        S.op("dve", lambda e, q=q, k=zk[2], fc=fc, ncols=ncols: e.tensor_tensor(
                            gext[:, fc, 2:2 + ncols], pz[k][:, 0:ncols], c_sb[q][:, 0:ncols], ALU.mult),
                             [("pz", zk[2]), ("csb", q)], [("g", fc)])
                        G = [("g", fc), ("gh", fc)] + CW
                        S.op("dve", lambda e, q=q, fc=fc, ncols=ncols: e.tensor_scalar(
                            acc[q][:, 0:ncols], gext[:, fc, 0:ncols], cw[:, 0, fc:fc + 1], None, ALU.mult), G, [("acc", q)])
                        S.op("dve", lambda e, q=q, fc=fc, ncols=ncols: e.scalar_tensor_tensor(
                            acc[q][:, 0:ncols], gext[:, fc, 1:1 + ncols], cw[:, 1, fc:fc + 1], acc[q][:, 0:ncols], ALU.mult, ALU.add),
                             G + [("acc", q)], [("acc", q)])
                        S.op("dve", lambda e, q=q, fc=fc, ncols=ncols: e.scalar_tensor_tensor(
                            acc[q][:, 0:ncols], gext[:, fc, 2:2 + ncols], cw[:, 2, fc:fc + 1], acc[q][:, 0:ncols], ALU.mult, ALU.add),
                             G + [("acc", q)], [("acc", q)])
                        S.op("dve", lambda e, q=q, fc=fc, ncols=ncols: e.tensor_tensor(
                            mT[:, fc, 0:ncols], acc[q][:, 0:ncols], b_sb[q][:, 0:ncols], ALU.mult),
                             [("acc", q), ("bsb", q)], [("mT", fc)])
                        S.op("act", lambda e, fc=fc, ncols=ncols: e.copy(gext[:, fc, 0:2], gext[:, fc, ncols:ncols + 2]),
                             [("g", fc)], [("gh", fc)])
                    for b in range(nb):
                        for dh in range(2):
                            k = di_ % 2
                            di_ += 1
                            for fc in range(8):
                                S.op("pe", lambda e, k=k, fc=fc, b=b, dh=dh: e.matmul(
                                    pd[k][:, :], mT[:, fc, b * 128:(b + 1) * 128], wout[:, fc, dh * 512:(dh + 1) * 512],
                                    start=(fc == 0), stop=(fc == 7)), [("mT", fc)] + wr("cwout", fc, dh * 512, dh * 512 + 512), [("pd", k)])
                            S.op("dve", lambda e, k=k, ht=ht, b=b, dh=dh: e.tensor_tensor(
                                ht[:, b, dh * 512:(dh + 1) * 512], pd[k][:, :], ht[:, b, dh * 512:(dh + 1) * 512], ALU.add),
                                 [("pd", k), hname], [hname])
                    S.dma("sp", lambda e, ht=ht, r0=r0, nb=nb: e.dma_start(
                        out=dst[r0:r0 + nb * 128, :].rearrange("(b p) d -> p b d", p=128), in_=ht[:, 0:nb, :]),
                          reads=[hname], writes=[("hrows", c)])
                S.barrier()

        def attn_phase(j, layer, is_last):
            src, dst = src_dst(is_last)
            with contextlib.ExitStack() as ps:
                win = sbt(ps, "awin", [128, 8, 2304], BF16)
                wuq = sbt(ps, "wuq", [128, 3, 768], BF16)
                wukv = sbt(ps, "wukv", [128, 2, 1024], BF16)
                gcol = sbt(ps, "gcol", [128, 8], F32)
                gq_c = sbt(ps, "gqc", [128, 3], F32)
                gkv_c = sbt(ps, "gkvc", [128, 2], F32)
                load_gain_cols(gcol, "gcol", g_mix[layer], D)
                load_gain_cols(gq_c, "gqc", g_cq[j], QL)
                load_gain_cols(gkv_c, "gkvc", g_ckv[j], KVL)
                gall = sbt(ps, "gall", [128, 336], F32)
                rowt = sbt(ps, "rowt", [1, 336], F32)
                ones_row = sbt(ps, "ones_row", [1, 128], F32)
                gqm, gkm, gqf, gkf, bfo = gall[:, 0:96], gall[:, 96:192], gall[:, 192:256], gall[:, 256:320], gall[:, 320:328]
                S.op("pool", lambda e: e.memset(ones_row[:], 1.0), [], ["ones_row"])
                S.op("pool", lambda e: e.memset(rowt[:], 0.0), [], ["rowt"])
                for off, n, apx in ((0, 96, g_q_mla[j:j + 1, :]), (96, 96, g_k_mla[j:j + 1, :]), (192, 64, g_q_fox[j:j + 1, :]),
                                    (256, 64, g_k_fox[j:j + 1, :]), (320, 8, b_forget[j:j + 1, :])):
                    S.dma("sp", lambda e, off=off, n=n, apx=apx: e.dma_start(out=rowt[0:1, off:off + n], in_=apx), reads=["rowt"], writes=["rowt"])
                with contextlib.ExitStack() as pbs:
                    pb = pst(pbs, "pb", [128, 512], F32)
                    S.op("pe", lambda e: e.matmul(pb[:, 0:336], ones_row[:], rowt[:], start=True, stop=True), ["ones_row", "rowt"], ["pb"])
                    S.op("dve", lambda e: e.tensor_copy(gall[:], pb[:, 0:336]), ["pb"], ["gqm", "gkm", "gqf", "gkf", "bfo"])
                    S.barrier()
                S.op("dve", lambda e: e.tensor_scalar(gqm[:], gqm[:], 96.0 ** -0.5, None, ALU.mult), ["gqm"], ["gqm"])
                S.op("dve", lambda e: e.tensor_scalar(gqf[:], gqf[:], 64.0 ** -0.5, None, ALU.mult), ["gqf"], ["gqf"])
                wst = [sbt(ps, f"wstg{i}", [128, NS], F32) for i in range(3)]
                h_t = [sbt(ps, f"h{i}", [128, D], F32) for i in range(2)]
                junk = sbt(ps, "junk", [128, D], BF16)
                ss = sbt(ps, "ss", [128, 1], F32)
                rstd = sbt(ps, "rstd", [128, 1], F32)
                hn_bf = sbt(ps, "hnbf", [128, D], BF16)
                hnT2 = [sbt(ps, f"hnT{i}", [128, 8, 128], BF16) for i in range(2)]
                z2 = [sbt(ps, f"z{i}", [128, ATTN_IN], F32) for i in range(2)]
                hn_bf2 = [hn_bf, sbt(ps, "hnbf1", [128, D], BF16)]
                ss1 = sbt(ps, "ss1", [128, 1], F32)
                rstd1 = sbt(ps, "rstd1", [128, 1], F32)
                cn_bf = sbt(ps, "cnbf", [128, 640], BF16)
                cnT = sbt(ps, "cnT", [128, 5, 128], BF16)
                ss8 = sbt(ps, "ss8", [128, 8], F32)
                sspe = sbt(ps, "sspe", [128, 1], F32)
                r8 = sbt(ps, "r8", [128, 8], F32)
                sq = sbt(ps, "sq", [128, 1024], F32)
                qn = sbt(ps, "qn", [128, 8, 96], F32)
                kpe = sbt(ps, "kpe", [128, 32], F32)
                kper = sbt(ps, "kper", [128, 32], F32)
                rtmp = sbt(ps, "rtmp", [128, 8, 16], F32)
                rtmp2 = sbt(ps, "rtmp2", [128, 8, 16], F32)
                qbf = sbt(ps, "qbf", [128, 8, 96], BF16)
                kbf = sbt(ps, "kbf", [128, 8, 96], BF16)
                fqbf = sbt(ps, "fqbf", [128, 8, 96], BF16)
                fkbf = sbt(ps, "fkbf", [128, 8, 96], BF16)
                hT_sb = [sbt(ps, f"hTsb{i}", [96, 8, 128], BF16) for i in range(4)]
                vaug = sbt(ps, "vaug", [128, 16, 80], BF16)
                lf = sbt(ps, "lf", [128, 8], F32)
                Fc = [sbt(ps, f"Fc{i}", [128, 8], F32) for i in range(2)]
                fr = sbt(ps, "fr", [128, 8], F32)
                fhi = sbt(ps, "fhi", [128, 8], BF16)
                fmid = sbt(ps, "fmid", [128, 8], BF16)
                tT = [pst(ps, f"tT{i}", [128, 1024], BF16) for i in range(2)]
                pz = [pst(ps, f"pz{i}", [128, 512], F32) for i in range(2)]
                pq = [pst(ps, f"pq{i}", [128, 512], F32) for i in range(2)]
                pkv = [pst(ps, f"pkv{i}", [128, 512], F32) for i in range(2)]
                S.op("pool", lambda e: e.memset(fqbf[:, :, 67:70], 1.0), [], ["fq1"])
                S.op("pool", lambda e: e.memset(fkbf[:, :, 64:67], 1.0), [], ["fk1"])
                S.op("pool", lambda e: e.memset(vaug[:, :, 64:80], 1.0), [], ["vones"])
                S.op("pool", lambda e: e.memset(Fc[1][:], 0.0), [], [("Fc", 1)])
                groups = [(0, 384), (384, 672), (672, 1184), (1184, 1696), (1696, 2208), (2208, 2216)]

                def bc8(ap2, n):
                    return ap2.unsqueeze(2).to_broadcast([128, 8, n])

                def bch(ap2, n):
                    return ap2.unsqueeze(1).to_broadcast([128, 8, n])

                def head_rstd(src3, n, extra=None, tag=""):
                    S.op("act", lambda e: e.activation(sq[:, 0:8 * n].rearrange("p (h d) -> p h d", d=n), src3, AF.Square),
                         [("z", 0), ("z", 1), "qn"], ["sq"])
                    S.op("dve", lambda e: e.reduce_sum(ss8[:], sq[:, 0:8 * n].rearrange("p (h d) -> p h d", d=n), axis=AX.X),
                         ["sq"], ["ss8"])
                    tot = n
                    if extra is not None:
                        tot = n + 32
                        S.op("dve", lambda e: e.tensor_scalar(ss8[:], ss8[:], sspe[:, 0:1], None, ALU.add), ["ss8", "sspe"], ["ss8"])
                    S.op("dve", lambda e: e.tensor_scalar(r8[:], ss8[:], 1.0 / tot, EPS, ALU.mult, ALU.add), ["ss8"], ["r8"])
                    S.op("act", lambda e: e.activation(r8[:], r8[:], AF.Ln), ["r8"], ["r8"])
                    S.op("act", lambda e: e.activation(r8[:], r8[:], AF.Exp, scale=-0.5), ["r8"], ["r8"])

                def transpose_heads(src_bf, sname, ncol, dstT, k, hk):
                    tt = tT[k]
                    for h in range(8):
                        S.op("pe", lambda e, h=h: e.transpose(tt[0:ncol, h * 128:(h + 1) * 128], src_bf[:, h, 0:ncol], ident[:]),
                             [sname, "ident"], [("tT", k)])
                    hs = hT_sb[hk]
                    S.op("dve", lambda e: e.tensor_copy(hs[0:ncol, :, :], tt[0:ncol, :].rearrange("p (h t) -> p h t", t=128)),
                         [("tT", k)], [("hTsb", hk)])
                    return hs

                def load_blk(bb):
                    tl = h_t[bb % 2]
                    S.dma("sp", lambda e: e.dma_start(out=tl[:], in_=src[bb * 128:(bb + 1) * 128, :]), writes=[("h", bb % 2)])

                load_blk(0)
                if nblk > 1:
                    load_blk(1)
                load_weight(wst, win, "awin", w_in_attn[j], D, ATTN_IN, gain=gcol, gname="gcol")
                load_weight(wst, wuq, "wuq", w_uq[j], QL, 768, gain=gq_c, gname="gqc")
                load_weight(wst, wukv, "wukv", w_ukv[j], KVL, 1024, gain=gkv_c, gname="gkvc")
                def stage1(blk):
                    z, zn = z2[blk % 2], ("z", blk % 2)
                    hnT, hnTn = hnT2[blk % 2], ("hnT", blk % 2)
                    r0 = blk * 128
                    ht = h_t[blk % 2]
                    hname = ("h", blk % 2)
                    if blk >= 1 and blk + 1 < nblk:
                        load_blk(blk + 1)
                    rmsnorm_T(ht[:], hname, hnT, hnTn, 0, (junk, ss1, rstd1, hn_bf2[blk % 2], f"n{blk % 2}"), tT[0], ("tT", 0), pre="s1")
                    for gi, (c0, c1) in enumerate(groups):
                        k = gi % 2
                        for kc in range(8):
                            S.op("pe", lambda e, k=k, kc=kc, c0=c0, c1=c1: e.matmul(
                                pz[k][:, 0:c1 - c0], hnT[:, kc, :], win[:, kc, c0:c1], start=(kc == 0), stop=(kc == 7)),
                                 [hnTn] + wr("awin", kc, c0, c1), [("pz", k)])
                        if gi % 2 == 0:
                            S.op("act", lambda e, k=k, c0=c0, c1=c1: e.copy(z[:, c0:c1], pz[k][:, 0:c1 - c0]), [("pz", k)], [zn])
                        else:
                            S.op("dve", lambda e, k=k, c0=c0, c1=c1: e.tensor_copy(z[:, c0:c1], pz[k][:, 0:c1 - c0]), [("pz", k)], [zn])
                    if debug:
                        S.dma("sp", lambda e, blk=blk: e.dma_start(out=zdbg[blk], in_=z[:, :]), reads=[zn], writes=[("zdbg", blk)])
                def stage2(blk):
                    r0 = blk * 128
                    z, zn = z2[blk % 2], ("z", blk % 2)
                    for (c0, w_, o0) in ((0, QL, 0), (QL, KVL, QL)):
                        S.op("act", lambda e, c0=c0, w_=w_: e.activation(sq[:, 0:w_], z[:, c0:c0 + w_], AF.Square, accum_out=ss[:, 0:1]),
                             [zn], ["sq", "ss"])
                        S.op("dve", lambda e, w_=w_: e.tensor_scalar(rstd[:, 0:1], ss[:, 0:1], 1.0 / w_, EPS, ALU.mult, ALU.add),
                             ["ss"], ["rstd"])
                        S.op("act", lambda e: e.activation(rstd[:, 0:1], rstd[:, 0:1], AF.Ln), ["rstd"], ["rstd"])
                        S.op("act", lambda e: e.activation(rstd[:, 0:1], rstd[:, 0:1], AF.Exp, scale=-0.5), ["rstd"], ["rstd"])
                        S.op("act", lambda e, c0=c0, w_=w_: e.activation(cn_bf[:, c0:c0 + w_], z[:, c0:c0 + w_], AF.Copy, scale=rstd[:, 0:1]),
                             [zn, "rstd"], ["cnbf"])
                    for kc in range(5):
                        S.op("pe", lambda e, kc=kc: e.transpose(tT[1][:, kc * 128:(kc + 1) * 128], cn_bf[:, kc * 128:(kc + 1) * 128], ident[:]),
                             ["cnbf", "ident"], [("tT", 1)])
                    S.op("dve", lambda e: e.tensor_copy(cnT[:, :, :], tT[1][:, 0:640].rearrange("p (k t) -> p k t", t=128)),
                         [("tT", 1)], ["cnT"])
                    for half, (n0, n1) in enumerate(((0, 512), (512, 768))):
                        for kc in range(3):
                            S.op("pe", lambda e, half=half, kc=kc, n0=n0, n1=n1: e.matmul(
                                pq[half][:, 0:n1 - n0], cnT[:, kc, :], wuq[:, kc, n0:n1], start=(kc == 0), stop=(kc == 2)),
                                 ["cnT"] + wr("wuq", kc, n0, n1), [("pq", half)])
                    for half in range(2):
                        for kc in range(2):
                            S.op("pe", lambda e, half=half, kc=kc: e.matmul(
                                pkv[half][:, :], cnT[:, 3 + kc, :], wukv[:, kc, half * 512:(half + 1) * 512],
                                start=(kc == 0), stop=(kc == 1)), ["cnT"] + wr("wukv", kc, half * 512, half * 512 + 512), [("pkv", half)])
                    qflat = qn[:].rearrange("p h d -> p (h d)")
                    S.op("act", lambda e: e.copy(qflat[:, 0:512], pq[0][:, :]), [("pq", 0)], ["qn"])
                    S.op("dve", lambda e: e.tensor_copy(qflat[:, 512:768], pq[1][:, 0:256]), [("pq", 1)], ["qn"])
                    head_rstd(qn[:, :, :], 96)
                    S.op("dve", lambda e: e.tensor_tensor(qn[:, :, :], qn[:, :, :], bc8(r8[:], 96), ALU.mult), ["qn", "r8"], ["qn"])
                    S.op("dve", lambda e: e.tensor_tensor(qn[:, :, :], qn[:, :, :], bch(gqm[:], 96), ALU.mult), ["qn", "gqm"], ["qn"])
                    S.op("act", lambda e: e.copy(qbf[:, :, 0:64], qn[:, :, 0:64]), ["qn"], ["qbf"])
                    cosb = bch(cos_t[:, blk, :], 16)
                    sinb = bch(sin_t[:, blk, :], 16)
                    S.op("pool", lambda e, sinb=sinb: e.tensor_tensor(rtmp[:, :, :], qn[:, :, 80:96], sinb, ALU.mult), ["qn", "sin"], ["rtmp"])
                    S.op("pool", lambda e, cosb=cosb: e.tensor_tensor(rtmp2[:, :, :], qn[:, :, 64:80], cosb, ALU.mult), ["qn", "cos"], ["rtmp2"])
                    S.op("pool", lambda e: e.tensor_tensor(qbf[:, :, 64:80], rtmp2[:, :, :], rtmp[:, :, :], ALU.subtract), ["rtmp2", "rtmp"], ["qbf"])
                    S.op("pool", lambda e, sinb=sinb: e.tensor_tensor(rtmp[:, :, :], qn[:, :, 64:80], sinb, ALU.mult), ["qn", "sin"], ["rtmp"])
                    S.op("pool", lambda e, cosb=cosb: e.tensor_tensor(rtmp2[:, :, :], qn[:, :, 80:96], cosb, ALU.mult), ["qn", "cos"], ["rtmp2"])
                    S.op("pool", lambda e: e.tensor_tensor(qbf[:, :, 80:96], rtmp2[:, :, :], rtmp[:, :, :], ALU.add), ["rtmp2", "rtmp"], ["qbf"])
                    hs = transpose_heads(qbf, "qbf", 96, qt_d, 0, 0)
                    S.dma("sp", lambda e, hs=hs, r0=r0: e.dma_start(out=qt_d[0:8, :, r0:r0 + 128].rearrange("h d t -> d h t"), in_=hs[:, :, :]),
                          reads=[("hTsb", 0)], writes=[("qt", blk)])
                    kv3 = [pkv[hf][:, :].rearrange("p (h d) -> p h d", d=128) for hf in range(2)]
                    S.op("act", lambda e: e.activation(kper[:, :], z[:, 640:672], AF.Square, accum_out=sspe[:, 0:1]), [zn], ["kper", "sspe"])
                    S.op("dve", lambda e: e.tensor_tensor(kpe[:, :], z[:, 640:672], gkm[:, 64:96], ALU.mult), [zn, "gkm"], ["kpe"])
                    c2, s2 = cos_t[:, blk, :], sin_t[:, blk, :]
                    S.op("pool", lambda e, s2=s2: e.tensor_tensor(rtmp[:, 0, :], kpe[:, 16:32], s2, ALU.mult), ["kpe", "sin"], ["rtmp"])
                    S.op("pool", lambda e, c2=c2: e.tensor_tensor(rtmp2[:, 0, :], kpe[:, 0:16], c2, ALU.mult), ["kpe", "cos"], ["rtmp2"])
                    S.op("pool", lambda e: e.tensor_tensor(kper[:, 0:16], rtmp2[:, 0, :], rtmp[:, 0, :], ALU.subtract), ["rtmp2", "rtmp", "sspe"], ["kper"])
                    S.op("pool", lambda e, s2=s2: e.tensor_tensor(rtmp[:, 0, :], kpe[:, 0:16], s2, ALU.mult), ["kpe", "sin"], ["rtmp"])
                    S.op("pool", lambda e, c2=c2: e.tensor_tensor(rtmp2[:, 0, :], kpe[:, 16:32], c2, ALU.mult), ["kpe", "cos"], ["rtmp2"])
                    S.op("pool", lambda e: e.tensor_tensor(kper[:, 16:32], rtmp2[:, 0, :], rtmp[:, 0, :], ALU.add), ["rtmp2", "rtmp"], ["kper"])
                    for hf in range(2):
                        S.op("act", lambda e, hf=hf: e.copy(qn[:, hf * 4:(hf + 1) * 4, 0:64], kv3[hf][:, :, 0:64]), [("pkv", hf)], ["qn"])
                        S.op("dve", lambda e, hf=hf: e.tensor_copy(vaug[:, hf * 4:(hf + 1) * 4, 0:64], kv3[hf][:, :, 64:128]), [("pkv", hf)], ["vaug"])
                    head_rstd(qn[:, :, 0:64], 64, extra=True)
                    S.op("dve", lambda e: e.tensor_tensor(qn[:, :, 0:64], qn[:, :, 0:64], bc8(r8[:], 64), ALU.mult), ["qn", "r8"], ["qn"])
                    S.op("dve", lambda e: e.tensor_tensor(kbf[:, :, 0:64], qn[:, :, 0:64], bch(gkm[:, 0:64], 64), ALU.mult), ["qn", "gkm"], ["kbf"])
                    S.op("dve", lambda e: e.tensor_tensor(kbf[:, :, 64:96], bch(kper[:, :], 32), bc8(r8[:], 32), ALU.mult), ["kper", "r8"], ["kbf"])
                    hs = transpose_heads(kbf, "kbf", 96, kt_d, 1, 1)
                    S.dma("sp", lambda e, hs=hs, r0=r0: e.dma_start(out=kt_d[0:8, :, r0:r0 + 128].rearrange("h d t -> d h t"), in_=hs[:, :, :]),
                          reads=[("hTsb", 1)], writes=[("kt", blk)])
                    S.op("dve", lambda e: e.tensor_tensor(lf[:], z[:, 2208:2216], bfo[:], ALU.add), [zn, "bfo"], ["lf"])
                    S.op("act", lambda e: e.activation(lf[:], lf[:], AF.Exp, scale=-1.0), ["lf"], ["lf"])
                    S.op("dve", lambda e: e.tensor_scalar(lf[:], lf[:], 1.0, None, ALU.add), ["lf"], ["lf"])
                    S.op("act", lambda e: e.activation(lf[:], lf[:], AF.Ln), ["lf"], ["lf"])
                    if blk == 0:
                        S.op("dve", lambda e: e.tensor_scalar(lf[:], lf[:], rowmask[:, 0:1], None, ALU.mult), ["lf", "rowmask"], ["lf"])
                    fcur, fprev = Fc[blk % 2], Fc[(blk + 1) % 2]
                    S.op("pe", lambda e: e.matmul(pq[1][:, 256:264], tri_f[:], lf[:], start=True, stop=False),
                         ["tri_f", "lf", ("pq", 1)], [("pq", 1)])
                    S.op("pe", lambda e, fprev=fprev: e.matmul(pq[1][:, 256:264], e127[:], fprev[:], start=False, stop=True),
                         ["e127", ("Fc", (blk + 1) % 2)], [("pq", 1)])
                    S.op("act", lambda e, fcur=fcur: e.copy(fcur[:], pq[1][:, 256:264]), [("pq", 1)], [("Fc", blk % 2)])
                    FN = ("Fc", blk % 2)
                    for (dstb, c0, sgn, nm) in ((fqbf, 64, -1.0, "fqbf"), (fkbf, 67, 1.0, "fkbf")):
                        S.op("pool", lambda e, sgn=sgn, fcur=fcur: e.tensor_scalar(fr[:], fcur[:], sgn, None, ALU.mult), [FN], ["fr"])
                        S.op("pool", lambda e: e.tensor_copy(fhi[:], fr[:]), ["fr"], ["fhi"])
                        S.op("pool", lambda e: e.tensor_tensor(fr[:], fr[:], fhi[:], ALU.subtract), ["fr", "fhi"], ["fr"])
                        S.op("pool", lambda e: e.tensor_copy(fmid[:], fr[:]), ["fr"], ["fmid"])
                        S.op("pool", lambda e: e.tensor_tensor(fr[:], fr[:], fmid[:], ALU.subtract), ["fr", "fmid"], ["fr"])
                        S.op("pool", lambda e, dstb=dstb, c0=c0: e.tensor_copy(dstb[:, :, c0:c0 + 1], fhi[:].unsqueeze(2)), ["fhi"], [nm])
                        S.op("pool", lambda e, dstb=dstb, c0=c0: e.tensor_copy(dstb[:, :, c0 + 1:c0 + 2], fmid[:].unsqueeze(2)), ["fmid"], [nm])
                        S.op("pool", lambda e, dstb=dstb, c0=c0: e.tensor_copy(dstb[:, :, c0 + 2:c0 + 3], fr[:].unsqueeze(2)), ["fr"], [nm])
                    for (c0, gt, gname, dstb, nm, dT, hk) in ((672, gqf, "gqf", fqbf, "fqbf", qt_d, 2), (1184, gkf, "gkf", fkbf, "fkbf", kt_d, 3)):
                        src3 = z[:, c0:c0 + 512].rearrange("p (h d) -> p h d", d=64)
                        head_rstd(src3, 64)
                        S.op("dve", lambda e, src3=src3: e.tensor_tensor(qn[:, :, 0:64], src3, bc8(r8[:], 64), ALU.mult), [zn, "r8"], ["qn"])
                        S.op("dve", lambda e, gt=gt, dstb=dstb: e.tensor_tensor(dstb[:, :, 0:64], qn[:, :, 0:64], bch(gt[:], 64), ALU.mult),
                             ["qn", gname], [nm])
                        hs = transpose_heads(dstb, nm, 70, dT, hk % 2, hk)
                        S.dma("sp", lambda e, hs=hs, r0=r0, dT=dT: e.dma_start(
                            out=dT[8:16, 0:70, r0:r0 + 128].rearrange("h d t -> d h t"), in_=hs[0:70, :, :]),
                              reads=[("hTsb", hk), "fq1", "fk1"], writes=[("qkt", hk, blk)])
                    S.op("act", lambda e: e.copy(vaug[:, 8:16, 0:64], z[:, 1696:2208].rearrange("p (h d) -> p h d", d=64)), [zn], ["vaug"])
                    if blk == 0:
                        S.op("dve", lambda e: e.tensor_scalar(vaug[:].rearrange("p h c -> p (h c)"), vaug[:].rearrange("p h c -> p (h c)"),
                                                              rowmask[:, 0:1], None, ALU.mult), ["vaug", "vones", "rowmask"], ["vaug", "vones"])
                    S.dma("sp", lambda e, blk=blk: e.dma_start(out=v_d[:, :, blk, :].rearrange("h p c -> p h c"), in_=vaug[:, :, :]),
                          reads=["vaug", "vones"], writes=[("vd", blk)])
                    if blk == 0:
                        S.op("pool", lambda e: e.memset(vaug[:, :, 64:80], 1.0), ["vaug", "vones"], ["vones"])
                stage1(0)
                for blk in range(nblk):
                    if blk + 1 < nblk:
                        stage1(blk + 1)
                    stage2(blk)
                flast = Fc[(nblk - 1) % 2]
                S.op("pe", lambda e: e.matmul(pq[1][:, 256:264], e127[:], flast[:], start=True, stop=True),
                     ["e127", ("Fc", (nblk - 1) % 2), ("pq", 1)], [("pq", 1)])
                S.op("dve", lambda e: e.tensor_scalar(fr[:], pq[1][:, 256:264], -1.0, None, ALU.mult), [("pq", 1)], ["fr"])
                S.dma("sp", lambda e: e.dma_start(out=tot_s.ap(), in_=fr[:]), reads=["fr"], writes=["tot_s"])
                S.barrier()
            S.cc(lambda e: e.collective_compute("AllGather", ALU.bypass, replica_groups=RG,
                                                ins=[tot_s.ap().opt()], outs=[tot_g.ap().opt()]), [], ["totg"])

            def gather_pair(g):
                S.cc(lambda e: e.collective_compute("AllGather", ALU.bypass, replica_groups=RG,
                                                    ins=[kt2.ap()[g * 192:(g + 1) * 192, :].opt()],
                                                    outs=[ktg2.ap()[g * 384:(g + 1) * 384, :].opt()]), [], [("ktg", g)])
                S.cc(lambda e: e.collective_compute("AllGather", ALU.bypass, replica_groups=RG,
                                                    ins=[v2.ap()[g * 256:(g + 1) * 256, :].opt()],
                                                    outs=[vg2.ap()[g * 512:(g + 1) * 512, :].opt()]), [], [("vg", g)])

            gather_pair(0)
            gather_pair(1)

            with contextlib.ExitStack() as ps:
                KT = [sbt(ps, f"KT{i}", [96, T], BF16) for i in range(2)]
                QT = [sbt(ps, f"QT{i}", [96, T], BF16) for i in range(2)]
                VA = [sbt(ps, f"VA{i}", [128, nblk, 80], BF16) for i in range(2)]
                KTo = [sbt(ps, f"KTo{i}", [96, T], BF16) for i in range(2)]
                VAo = [sbt(ps, f"VAo{i}", [128, nblk, 80], BF16) for i in range(2)]
                negc0 = sbt(ps, "negc0", [128, 8], F32)
                pT = [sbt(ps, f"pT{i}", [128, 1024], BF16) for i in range(4)]
                xsb = [sbt(ps, f"xsb{i}", [65, 512], F32) for i in range(2)]
                rec = sbt(ps, "rec", [64, 512], F32)
                oTs = [sbt(ps, f"oTs{i}", [64, 512], BF16) for i in range(2)]
                sel = sbt(ps, "sel", [65, 64], F32)
                S.op("pool", lambda e: e.memset(sel[:], 0.0), [], ["sel"])
                S.op("pool", lambda e: e.memset(sel[64:65, :], 1.0), ["sel"], ["sel"])
                pss = [pst(ps, f"ps{i}", [128, 1024], F32) for i in range(3)]
                po = [pst(ps, f"po{i}", [128, 512], F32) for i in range(1)]
                pbc = pst(ps, "pbc", [64, 512], F32)
                S.dma("sp", lambda e: e.dma_start(out=negc0[:], in_=tot_g.ap()[0:128, :]), reads=["totg"], writes=["negc0"])

                def load_head(h):
                    dk = 96 if h < 8 else 70
                    b = h % 2
                    if h % 2 == 0 and h // 2 + 1 < 8 and h >= 2:
                        gather_pair(h // 2 + 1)
                    S.dma("sp", lambda e: e.dma_start(out=KT[b][0:dk, :], in_=kt_d[h, 0:dk, :]), writes=[("KT", b)])
                    S.dma("sp", lambda e: e.dma_start(out=QT[b][0:dk, :], in_=qt_d[h, 0:dk, :]), writes=[("QT", b)])
                    S.dma("sp", lambda e: e.dma_start(out=VA[b][:, :, :], in_=v_d[h]), writes=[("VA", b)])
                    S.dma("sp", lambda e: e.dma_start(out=KTo[b][0:dk, :], in_=ktg_h(h)[0:dk, :]), reads=[("ktg", h // 2)], writes=[("KTo", b)])
                    S.dma("sp", lambda e: e.dma_start(out=VAo[b][:, :, :], in_=vg_h(h)), reads=[("vg", h // 2)], writes=[("VAo", b)])
                    S.op("dve", lambda e: e.tensor_scalar(KTo[b][0:dk, :], KTo[b][0:dk, :], a1[0:dk, 0:1], None, ALU.mult),
                         [("KTo", b), "a1"], [("KTo", b)])
                    S.op("dve", lambda e: e.tensor_scalar(VAo[b][:].rearrange("p b c -> p (b c)"), VAo[b][:].rearrange("p b c -> p (b c)"),
                                                           a1[:, 0:1], None, ALU.mult), [("VAo", b), "a1"], [("VAo", b)])

                groups = []
                for h in range(16):
                    for qb0 in range(0, nblk, 4):
                        nq = min(4, nblk - qb0)
                        tiles = [("o", t) for t in range(nblk)] + [("k", kt) for kt in range(qb0)]
                        for a in range(0, len(tiles) - 1, 2):
                            if tiles[a][0] == tiles[a + 1][0]:
                                groups.append((h, qb0, nq, [tiles[a], tiles[a + 1]]))
                            else:
                                groups.append((h, qb0, nq, [tiles[a]]))
                                groups.append((h, qb0, nq, [tiles[a + 1]]))
                        if len(tiles) % 2:
                            groups.append((h, qb0, nq, [tiles[-1]]))
                        for kt in range(qb0, qb0 + nq):
                            groups.append((h, qb0, nq, [("k", kt)]))
                oi = [0]

                def front(gi):
                    h, qb0, nq, tiles = groups[gi]
                    dk = 96 if h < 8 else 70
                    b = h % 2
                    sp_, pk = gi % 3, gi % 4
                    kind0, kt0 = tiles[0]
                    first = (max(kt0, qb0) - qb0) if kind0 == "k" else 0
                    c0, c1 = first * 128, nq * 128
                    for jx, (kind, kt) in enumerate(tiles):
                        src_k = KTo[b] if kind == "o" else KT[b]
                        rk = ("KTo", b) if kind == "o" else ("KT", b)
                        S.op("pe", lambda e, jx=jx, kt=kt, src_k=src_k: e.matmul(
                            pss[sp_][:, jx * 512 + c0:jx * 512 + c1], src_k[0:dk, kt * 128:(kt + 1) * 128],
                            QT[b][0:dk, qb0 * 128 + c0:qb0 * 128 + c1], start=True, stop=True), [rk, ("QT", b)], [("ps", sp_)])
                    nt = len(tiles)
                    if nt == 2:
                        o_ap = pT[pk][:, :].rearrange("p (j c) -> p j c", j=2)[:, :, c0:c1]
                        i_ap = pss[sp_][:, :].rearrange("p (j c) -> p j c", j=2)[:, :, c0:c1]
                    else:
                        o_ap, i_ap = pT[pk][:, c0:c1], pss[sp_][:, c0:c1]
                    if kind0 == "o" and h >= 8:
                        S.op("act", lambda e: e.activation(o_ap, i_ap, AF.Exp, bias=negc0[:, h - 8:h - 7]),
                             [("ps", sp_), "negc0"], [("pT", pk)])
                    else:
                        S.op("act", lambda e: e.activation(o_ap, i_ap, AF.Exp), [("ps", sp_)], [("pT", pk)])
                    if kind0 == "k" and kt0 >= qb0:
                        S.op("dve", lambda e: e.tensor_tensor(
                            pT[pk][:, c0:c0 + 128], pT[pk][:, c0:c0 + 128], tri_bf[:], ALU.mult), [("pT", pk), "tri_bf"], [("pT", pk)])

                def back(gi):
                    h, qb0, nq, tiles = groups[gi]
                    b = h % 2
                    pk = gi % 4
                    cb = 0
                    c1 = nq * 128
                    for jx, (kind, kt) in enumerate(tiles):
                        if kind == "o":
                            S.op("pe", lambda e, jx=jx, kt=kt: e.matmul(po[cb][0:65, 0:c1], VAo[b][:, kt, 0:65], pT[pk][:, jx * 512:jx * 512 + c1],
                                                                        start=(kt == 0), stop=False), [("pT", pk), ("VAo", b)], [("po", cb)])
                            continue
                        first = max(kt, qb0) - qb0
                        c0 = first * 128
                        last = kt == qb0 + nq - 1
                        S.op("pe", lambda e, jx=jx, kt=kt, c0=c0, last=last: e.matmul(
                            po[cb][0:65, c0:c1], VA[b][:, kt, 0:65], pT[pk][:, jx * 512 + c0:jx * 512 + c1],
                            start=False, stop=last), [("pT", pk), ("VA", b)], [("po", cb)])
                        if last:
                            x = oi[0] % 2
                            oi[0] += 1
                            S.op("dve", lambda e: e.tensor_copy(xsb[x][0:65, 0:c1], po[cb][0:65, 0:c1]), [("po", cb)], [("xsb", x)])
                            S.op("pe", lambda e: e.matmul(pbc[0:64, 0:c1], sel[0:65, 0:64], xsb[x][0:65, 0:c1], start=True, stop=True),
                                 ["sel", ("xsb", x)], ["pbc"])
                            S.op("dve", lambda e: e.tensor_scalar(rec[0:64, 0:c1], pbc[0:64, 0:c1], 1e-30, None, ALU.max), ["pbc"], ["rec"])
                            S.op("dve", lambda e: e.reciprocal(rec[0:64, 0:c1], rec[0:64, 0:c1]), ["rec"], ["rec"])
                            S.op("dve", lambda e: e.tensor_tensor(oTs[x][0:64, 0:c1], xsb[x][0:64, 0:c1], rec[0:64, 0:c1], ALU.mult),
                                 [("xsb", x), "rec"], [("oTs", x)])
                            S.dma("sp", lambda e: e.dma_start(out=oT_d[h * 64:(h + 1) * 64, qb0 * 128:qb0 * 128 + c1], in_=oTs[x][0:64, 0:c1]),
                                  reads=[("oTs", x)], writes=[("od", h, qb0)])
                    if gi + 1 == len(groups) or groups[gi + 1][0] != h:
                        if h + 2 < 16:
                            load_head(h + 2)

                if "a3" in SKIP:
                    groups = []
                else:
                    load_head(0)
                    load_head(1)
                LA = 2
                for gi in range(len(groups) + LA):
                    if gi < len(groups):
                        front(gi)
                    if gi - LA >= 0:
                        back(gi - LA)
                S.barrier()

            with contextlib.ExitStack() as ps:
                wout = sbt(ps, "awout", [128, 8, D], BF16)
                wst = [sbt(ps, f"wstg{i}", [128, NS], F32) for i in range(3)]
                h4_t = [sbt(ps, f"h{i}", [128, D], F32) for i in range(2)]
                o_t = [sbt(ps, f"o{i}", [128, D], BF16) for i in range(2)]
                oT = [sbt(ps, f"oT{i}", [128, 8, 128], BF16) for i in range(2)]
                tT4 = [pst(ps, f"tT4{i}", [128, 1024], BF16) for i in range(2)]
                pd = [pst(ps, f"pd{i}", [128, 512], F32) for i in range(4)]
                def load4(bb):
                    S.dma("sp", lambda e: e.dma_start(out=h4_t[bb % 2][:], in_=src[bb * 128:(bb + 1) * 128, :]), writes=[("h", bb % 2)])
                    S.dma("sp", lambda e: e.dma_start(out=oT[bb % 2][:, :, :],
                                                      in_=oT_d[:, bb * 128:(bb + 1) * 128].rearrange("(c p) t -> p c t", p=128)),
                          writes=[("oT", bb % 2)])

                load4(0)
                if nblk > 1:
                    load4(1)
                load_weight(wst, wout, "awout", w_out_attn[j], D, D)
                for blk in range(nblk):
                    r0 = blk * 128
                    k = blk % 2
                    if blk >= 1 and blk + 1 < nblk:
                        load4(blk + 1)
                    for dh in range(2):
                        pk = (blk * 2 + dh) % 4
                        for kc in range(8):
                            S.op("pe", lambda e, k=k, kc=kc, dh=dh, pk=pk: e.matmul(
                                pd[pk][:, :], oT[k][:, kc, :], wout[:, kc, dh * 512:(dh + 1) * 512], start=(kc == 0), stop=(kc == 7)),
                                 [("oT", k)] + wr("awout", kc, dh * 512, dh * 512 + 512), [("pd", pk)])
                        S.op("dve", lambda e, k=k, dh=dh, pk=pk: e.tensor_tensor(
                            h4_t[k][:, dh * 512:(dh + 1) * 512], pd[pk][:, :], h4_t[k][:, dh * 512:(dh + 1) * 512], ALU.add),
                             [("pd", pk), ("h", k)], [("h", k)])
                    S.dma("sp", lambda e, k=k, r0=r0: e.dma_start(out=dst[r0:r0 + 128, :], in_=h4_t[k][:]), reads=[("h", k)], writes=[("hrows", blk)])
                S.barrier()

        li = {"attn": 0, "conv": 0}
        for pi, ph in enumerate(layers):
            is_last = pi == len(layers) - 1
            layer = pi // 2
            if ph == "mlp":
                mlp_phase(layer, is_last)
            elif ph == "conv":
                conv_phase(li["conv"], layer, is_last)
                li["conv"] += 1
            else:
                attn_phase(li["attn"], layer, is_last)
                li["attn"] += 1
        S.barrier()
        S.emit()
        print("n_ops", S.n_ops)
    return nc


def host_consts():
    tri = np.triu(np.ones((128, 128), np.float32))
    e127 = np.zeros((128, 128), np.float32)
    e127[127, :] = 1.0
    return {"ident": np.eye(128, dtype=np.float32), "tri": tri, "e127": e127}


def rope_tables(nblk, rank):
    T = nblk * 128
    pos = (np.arange(T, dtype=np.float32) + np.float32(rank * T - PAD)).astype(np.float32)
    inv_freq = (np.float32(10000.0) ** (-np.arange(0, ROPE, 2, dtype=np.float32) / np.float32(ROPE))).astype(np.float32)
    ang = (pos[:, None] * inv_freq[None, :]).astype(np.float32)
    cos = np.cos(ang).astype(np.float32).reshape(nblk, 128, 16).transpose(1, 0, 2)
    sin = np.sin(ang).astype(np.float32).reshape(nblk, 128, 16).transpose(1, 0, 2)
    return np.ascontiguousarray(cos), np.ascontiguousarray(sin)


_WNAMES = ["g_mix", "g_mlp", "w_in_attn", "g_cq", "w_uq", "g_ckv", "w_ukv", "g_q_mla", "g_k_mla", "g_q_fox",
           "g_k_fox", "b_forget", "w_out_attn", "w_in_conv", "conv_w", "w_out_conv", "w_mlp_up", "w_mlp_down"]


def run(inputs, layers=("attn", "mlp", "conv", "mlp", "attn", "mlp", "conv", "mlp"), n_cores=8, debug=False, trace=False):
    x = np.asarray(inputs["x"], np.float32)
    B, SEQ, _ = x.shape
    L = NMETA + SEQ
    nfull = (PAD + L) // 128
    assert nfull * 128 == PAD + L
    nblk = (nfull + 1) // 2
    T = nblk * 128
    nc = build_program(nblk, layers, debug, n_cores)
    consts = host_consts()
    meta = np.asarray(inputs["meta_tokens"], np.float32)
    weights = {n: np.ascontiguousarray(np.asarray(inputs[n], np.float32)) for n in _WNAMES}
    in_maps = []
    for c in range(n_cores):
        b, rank = (c // 2) % B, c % 2
        xp = np.zeros((2 * T, D), np.float32)
        xp[PAD:PAD + NMETA] = meta
        xp[PAD + NMETA:PAD + L] = x[b]
        cos, sin = rope_tables(nblk, rank)
        rowmask = np.ones((128, 1), np.float32)
        if rank == 0:
            rowmask[:PAD] = 0.0
        m = {"x_pad": np.ascontiguousarray(xp[rank * T:(rank + 1) * T]), "cos_t": cos, "sin_t": sin,
             "a1": np.full((128, 1), float(rank), np.float32), "rowmask0": rowmask}
        m.update(weights)
        m.update(consts)
        in_maps.append(m)
    res = run_bass_kernel_spmd(nc, in_maps, core_ids=list(range(n_cores)), **({"trace": True} if trace else {}))
    if trace:
        print("EXEC_TIME_NS", res.exec_time_ns)
    out = np.zeros((B, SEQ, D), np.float32)
    for b in range(min(B, n_cores // 2)):
        full = np.concatenate([res.results[2 * b]["y"], res.results[2 * b + 1]["y"]], axis=0)
        out[b] = full[PAD + NMETA:PAD + L]
    return out


def kernel(**inputs):
    return run(inputs)
```

```python
import contextlib
import numpy as np
import concourse.bass as bass
import concourse.mybir as mybir
from concourse.bass_utils import run_bass_kernel_spmd

F32 = mybir.dt.float32
BF16 = mybir.dt.bfloat16
AF = mybir.ActivationFunctionType
ALU = mybir.AluOpType
AX = mybir.AxisListType

D = 1024
DFF = 4096
NMETA = 16
PAD = 112
EPS = 1e-6
QL, KVL, ROPE = 384, 256, 32
ATTN_IN = 2216
ENGS = ("pe", "dve", "act", "pool", "sp")
SKIP = set()


class Sched:
    def __init__(self, nc, stack):
        self.nc = nc
        self.streams = {e: [] for e in ENGS}
        self.sem, self.cnt = {}, {}
        self.waited = {e: {} for e in ENGS}
        self.last_w, self.readers = {}, {}
        for e in ENGS:
            self.sem["e_" + e] = stack.enter_context(nc.semaphore("e_" + e))
            self.cnt["e_" + e] = 0
        self.dma_pool, self.dma_rr = {}, {}
        for q, n in {"sp": 12, "pool": 4, "act": 2}.items():
            names = []
            for i in range(n):
                nm = f"d_{q}{i}"
                self.sem[nm] = stack.enter_context(nc.semaphore(nm))
                self.cnt[nm] = 0
                names.append(nm)
            self.dma_pool[q], self.dma_rr[q] = names, 0
        self.cc_pool, self.cc_rr = [], 0
        for i in range(6):
            nm = f"c_{i}"
            self.sem[nm] = stack.enter_context(nc.semaphore(nm))
            self.cnt[nm] = 0
            self.cc_pool.append(nm)
        self.n_ops = 0

    def _deps(self, reads, writes):
        deps = {}

        def add(nm, val, eng):
            if deps.get(nm, (0, None))[0] < val:
                deps[nm] = (val, eng)

        for r in reads:
            if r in self.last_w:
                add(*self.last_w[r])
        for w in writes:
            if w in self.last_w:
                add(*self.last_w[w])
            for nm, (val, eng) in self.readers.get(w, {}).items():
                add(nm, val, eng)
        return deps

    def _record(self, tok, reads, writes):
        nm, val, eng = tok
        for r in reads:
            d = self.readers.setdefault(r, {})
            if d.get(nm, (0, None))[0] < val:
                d[nm] = (val, eng)
        for w in writes:
            self.last_w[w] = tok
            self.readers[w] = {}

    def _add_waits(self, eng, deps):
        for nm, (val, dep_eng) in deps.items():
            if dep_eng == eng and eng == "pe":
                continue
            if self.waited[eng].get(nm, 0) >= val:
                continue
            self.waited[eng][nm] = val
            self.streams[eng].append(("wait", nm, val))

    def begin_defer(self):
        self._buf = []

    def end_defer(self):
        buf, self._buf = self._buf, None
        lastw, readers, level = {}, {}, []
        for i, (kind, eng, fn, reads, writes) in enumerate(buf):
            lv = 0
            for r in reads:
                if r in lastw:
                    lv = max(lv, level[lastw[r]] + 1)
            for w in writes:
                if w in lastw:
                    lv = max(lv, level[lastw[w]] + 1)
                for j in readers.get(w, ()):
                    lv = max(lv, level[j] + 1)
            level.append(lv)
            for r in reads:
                readers.setdefault(r, []).append(i)
            for w in writes:
                lastw[w] = i
                readers[w] = []
        for i in sorted(range(len(buf)), key=lambda i: (level[i], i)):
            kind, eng, fn, reads, writes = buf[i]
            (self.op if kind == "op" else self.dma)(eng, fn, reads, writes)

    def op(self, eng, fn, reads=(), writes=()):
        if getattr(self, "_buf", None) is not None:
            self._buf.append(("op", eng, fn, tuple(reads), tuple(writes)))
            return
        self._add_waits(eng, self._deps(reads, writes))
        nm = "e_" + eng
        self.cnt[nm] += 1
        tok = (nm, self.cnt[nm], eng)
        self.streams[eng].append(("op", fn, nm, 1))
        self._record(tok, reads, writes)
        self.n_ops += 1

    def dma(self, q, fn, reads=(), writes=()):
        if getattr(self, "_buf", None) is not None:
            self._buf.append(("dma", q, fn, tuple(reads), tuple(writes)))
            return
        deps = self._deps(reads, writes)
        pool = self.dma_pool[q]
        nm = pool[self.dma_rr[q] % len(pool)]
        self.dma_rr[q] += 1
        if self.cnt[nm] > 0 and deps.get(nm, (0, None))[0] < self.cnt[nm]:
            deps[nm] = (self.cnt[nm], None)
        self._add_waits(q, deps)
        self.cnt[nm] += 16
        self.streams[q].append(("op", fn, nm, 16))
        self._record((nm, self.cnt[nm], None), reads, writes)
        self.n_ops += 1

    def cc(self, fn, reads=(), writes=()):
        deps = self._deps(reads, writes)
        nm = self.cc_pool[self.cc_rr % len(self.cc_pool)]
        self.cc_rr += 1
        if self.cnt[nm] > 0 and deps.get(nm, (0, None))[0] < self.cnt[nm]:
            deps[nm] = (self.cnt[nm], None)
        self._add_waits("pool", deps)
        self.cnt[nm] += 1
        self.streams["pool"].append(("op", fn, nm, 1))
        self._record((nm, self.cnt[nm], None), reads, writes)
        self.n_ops += 1

    def barrier(self):
        for e in ENGS:
            deps = {nm: (v, None) for nm, v in self.cnt.items() if v > 0 and nm != "e_" + e}
            self._add_waits(e, deps)

    def emit(self):
        nc = self.nc
        with nc.Block() as block:
            def run(e, engobj):
                for ent in self.streams[e]:
                    if ent[0] == "wait":
                        engobj.wait_ge(self.sem[ent[1]], ent[2])
                    else:
                        ent[1](engobj).then_inc(self.sem[ent[2]], ent[3])

            block.tensor(lambda t: run("pe", t))
            block.vector(lambda v: run("dve", v))
            block.scalar(lambda s: run("act", s))
            block.gpsimd(lambda g: run("pool", g))
            block.sync(lambda s: run("sp", s))


def build_program(nblk, layers=("attn", "mlp", "conv", "mlp", "attn", "mlp", "conv", "mlp"), debug=False, n_cores=8):
    T = nblk * 128
    nc = bass.Bass("TRN2", target_bir_lowering=False)
    di = lambda name, shape, dt=F32: nc.dram_tensor(name, list(shape), dt, kind="ExternalInput").ap()
    x_pad = di("x_pad", [T, D])
    g_mix = di("g_mix", [4, D]); g_mlp = di("g_mlp", [4, D])
    w_in_attn = di("w_in_attn", [2, D, ATTN_IN]); g_cq = di("g_cq", [2, QL]); w_uq = di("w_uq", [2, QL, 768])
    g_ckv = di("g_ckv", [2, KVL]); w_ukv = di("w_ukv", [2, KVL, 1024])
    g_q_mla = di("g_q_mla", [2, 96]); g_k_mla = di("g_k_mla", [2, 96])
    g_q_fox = di("g_q_fox", [2, 64]); g_k_fox = di("g_k_fox", [2, 64]); b_forget = di("b_forget", [2, 8])
    w_out_attn = di("w_out_attn", [2, D, D])
    w_in_conv = di("w_in_conv", [2, D, 3 * D]); conv_w = di("conv_w", [2, 3, D]); w_out_conv = di("w_out_conv", [2, D, D])
    w_mlp_up = di("w_mlp_up", [4, D, DFF]); w_mlp_down = di("w_mlp_down", [4, DFF, D])
    cos_d = di("cos_t", [128, nblk, 16]); sin_d = di("sin_t", [128, nblk, 16])
    ident_d = di("ident", [128, 128]); tri_d = di("tri", [128, 128]); e127_d = di("e127", [128, 128])
    a1_d = di("a1", [128, 1]); rowmask_d = di("rowmask0", [128, 1])
    y_out = nc.dram_tensor("y", [T, D], F32, kind="ExternalOutput").ap()
    h_d = nc.dram_tensor("h_scr", [T, D], F32).ap()
    skind = "ExternalOutput" if debug else "Internal"
    oT_d = nc.dram_tensor("oT_scr", [D, T], BF16, kind=skind).ap()
    qt_d = nc.dram_tensor("qt_scr", [16, 96, T], BF16).ap()
    kt2 = nc.dram_tensor("kt_scr", [16 * 96, T], BF16)
    kt_d = kt2.ap().rearrange("(h d) t -> h d t", d=96)
    v2 = nc.dram_tensor("v_scr", [16 * 128, nblk * 80], BF16)
    v_d = v2.ap().rearrange("(h p) (b c) -> h p b c", p=128, c=80)
    ktg2 = nc.dram_tensor("ktg", [2 * 16 * 96, T], BF16)
    ktg_h = lambda h: ktg2.ap()[(h // 2) * 384 + (h % 2) * 96:(h // 2) * 384 + (h % 2) * 96 + 96, :]
    vg2 = nc.dram_tensor("vg", [2 * 16 * 128, nblk * 80], BF16)
    vg_h = lambda h: vg2.ap()[(h // 2) * 512 + (h % 2) * 128:(h // 2) * 512 + (h % 2) * 128 + 128, :].rearrange("p (b c) -> p b c", c=80)
    tot_s = nc.dram_tensor("tot_s", [128, 8], F32)
    tot_g = nc.dram_tensor("tot_g", [256, 8], F32)
    gh_s = nc.dram_tensor("gh_s", [128, 16], F32)
    gh_g = nc.dram_tensor("gh_g", [256, 16], F32)
    RG = [[2 * i, 2 * i + 1] for i in range(n_cores // 2)]
    if debug:
        zdbg = nc.dram_tensor("zdbg", [nblk, 128, ATTN_IN], F32, kind="ExternalOutput").ap()
        hndbg = nc.dram_tensor("hndbg", [nblk, 128, 8, 128], BF16, kind="ExternalOutput").ap()
        windbg = nc.dram_tensor("windbg", [8, 128, 2304], BF16, kind="ExternalOutput").ap()

    with contextlib.ExitStack() as st:
        S = Sched(nc, st)
        uid = [0]

        def sbt(stack, name, shape, dt):
            uid[0] += 1
            return stack.enter_context(nc.sbuf_tensor(f"{name}_{uid[0]}", list(shape), dt))

        def pst(stack, name, shape, dt):
            uid[0] += 1
            return stack.enter_context(nc.psum_tensor(f"{name}_{uid[0]}", list(shape), dt))

        ident = sbt(st, "ident", [128, 128], BF16)
        tri_bf = sbt(st, "tri_bf", [128, 128], BF16)
        tri_f = sbt(st, "tri_f", [128, 128], F32)
        e127 = sbt(st, "e127", [128, 128], F32)
        cos_t = sbt(st, "cos", [128, nblk, 16], F32)
        sin_t = sbt(st, "sin", [128, nblk, 16], F32)
        S.dma("pool", lambda e: e.dma_start(out=ident[:], in_=ident_d), writes=["ident"])
        S.dma("pool", lambda e: e.dma_start(out=tri_bf[:], in_=tri_d), writes=["tri_bf"])
        S.dma("sp", lambda e: e.dma_start(out=tri_f[:], in_=tri_d), writes=["tri_f"])
        S.dma("sp", lambda e: e.dma_start(out=e127[:], in_=e127_d), writes=["e127"])
        S.dma("sp", lambda e: e.dma_start(out=cos_t[:], in_=cos_d), writes=["cos"])
        S.dma("sp", lambda e: e.dma_start(out=sin_t[:], in_=sin_d), writes=["sin"])
        a1 = sbt(st, "a1", [128, 1], F32)
        rowmask = sbt(st, "rowmask", [128, 1], F32)
        S.dma("sp", lambda e: e.dma_start(out=a1[:], in_=a1_d), writes=["a1"])
        S.dma("sp", lambda e: e.dma_start(out=rowmask[:], in_=rowmask_d), writes=["rowmask"])

        if debug:
            with contextlib.ExitStack() as fs:
                fill = sbt(fs, "fill", [128, 25000], F32)
                S.op("dve", lambda e: e.memset(fill[:, 0:12500], 7.0), [], ["fillA"])
                S.op("pool", lambda e: e.memset(fill[:, 12500:25000], 7.0), [], ["fillB"])
                S.barrier()
        cast_rr = [0]

        NS = 1024

        def load_weight(wst, dst, dst_name, w_ap, K, N, gain=None, gname=None, col_major=False):
            KC = K // 128
            chunks = [(kc, n0) for kc in range(KC) for n0 in range(0, N, NS)]
            if col_major:
                chunks = [(kc, n0) for n0 in range(0, N, NS) for kc in range(KC)]
            for (kc, n0) in chunks:
                if True:
                    n1 = min(N, n0 + NS)
                    i = cast_rr[0]
                    cast_rr[0] += 1
                    stg = wst[i % len(wst)]
                    sname = ("wstg", i % len(wst))
                    S.dma("sp", lambda e, stg=stg, kc=kc, n0=n0, n1=n1: e.dma_start(
                        out=stg[:, 0:n1 - n0], in_=w_ap[kc * 128:(kc + 1) * 128, n0:n1]), writes=[sname])
                    eng = ("act", "dve")[i % 2]
                    rd = [sname] + ([gname] if gain is not None else [])
                    wr = [(dst_name, kc, n0)]
                    if gain is None:
                        if eng == "act":
                            S.op("act", lambda e, stg=stg, kc=kc, n0=n0, n1=n1: e.copy(dst[:, kc, n0:n1], stg[:, 0:n1 - n0]), rd, wr)
                        else:
                            S.op(eng, lambda e, stg=stg, kc=kc, n0=n0, n1=n1: e.tensor_copy(dst[:, kc, n0:n1], stg[:, 0:n1 - n0]), rd, wr)
                    else:
                        if eng == "act":
                            S.op("act", lambda e, stg=stg, kc=kc, n0=n0, n1=n1: e.activation(
                                dst[:, kc, n0:n1], stg[:, 0:n1 - n0], AF.Copy, scale=gain[:, kc:kc + 1]), rd, wr)
                        else:
                            S.op(eng, lambda e, stg=stg, kc=kc, n0=n0, n1=n1: e.tensor_scalar(
                                dst[:, kc, n0:n1], stg[:, 0:n1 - n0], gain[:, kc:kc + 1], None, ALU.mult), rd, wr)

        def wr(dst_name, kc, c0, c1):
            return [(dst_name, kc, n0) for n0 in range((c0 // NS) * NS, c1, NS)]

        def load_gain_cols(dst, name, g_row_ap, K):
            S.dma("sp", lambda e: e.dma_start(out=dst[:, 0:K // 128], in_=g_row_ap.rearrange("(kc p) -> p kc", p=128),
                                              allow_slow_non_contiguous=True), writes=[name])

        def rmsnorm_T(h_ap, hname, hnT, hnT_name, col0, tmp, tT, tT_name, width=D, src_res=None, pre=""):
            junk, ss, rstd, hn_bf, nm = tmp
            rd = [hname] if src_res is None else src_res
            S.op("act", lambda e: e.activation(junk[:, 0:width], h_ap, AF.Square, accum_out=ss[:, 0:1]),
                 rd, [pre + "junk", pre + "ss"])
            S.op("dve", lambda e: e.tensor_scalar(rstd[:, 0:1], ss[:, 0:1], 1.0 / width, EPS, ALU.mult, ALU.add),
                 [pre + "ss"], [pre + "rstd"])
            S.op("act", lambda e: e.activation(rstd[:, 0:1], rstd[:, 0:1], AF.Ln), [pre + "rstd"], [pre + "rstd"])
            S.op("act", lambda e: e.activation(rstd[:, 0:1], rstd[:, 0:1], AF.Exp, scale=-0.5), [pre + "rstd"], [pre + "rstd"])
            S.op("act", lambda e: e.activation(hn_bf[:, 0:width], h_ap, AF.Copy, scale=rstd[:, 0:1]),
                 rd + [pre + "rstd"], [nm + "hn"])
            KC = width // 128
            for kc in range(KC):
                S.op("pe", lambda e, kc=kc: e.transpose(tT[:, kc * 128:(kc + 1) * 128], hn_bf[:, kc * 128:(kc + 1) * 128], ident[:]),
                     [nm + "hn", "ident"], [tT_name])
            S.op("dve", lambda e: e.tensor_copy(hnT[:, 0:KC, col0:col0 + 128],
                                                tT[:, 0:KC * 128].rearrange("p (k t) -> p k t", t=128)),
                 [tT_name], [hnT_name])

        first_phase = [True]

        def src_dst(is_last):
            src = x_pad if first_phase[0] else h_d
            first_phase[0] = False
            return src, (y_out if is_last else h_d)

        def mlp_phase(layer, is_last):
            src, dst = src_dst(is_last)
            CT = 256
            with contextlib.ExitStack() as ps:
                wup = sbt(ps, "wup", [128, 8, DFF], BF16)
                wdn = sbt(ps, "wdn", [128, 32, D], BF16)
                gcol = sbt(ps, "gcol", [128, 8], F32)
                load_gain_cols(gcol, "gcol", g_mlp[layer], D)
                wst = [sbt(ps, f"wstg{i}", [128, NS], F32) for i in range(3)]
                h_t = [sbt(ps, f"h{i}", [128, 2, D], F32) for i in range(2)]
                junk = sbt(ps, "junk", [128, D], BF16)
                ss = sbt(ps, "ss", [128, 1], F32)
                rstd = sbt(ps, "rstd", [128, 1], F32)
                hn_bf = [sbt(ps, f"hnbf{i}", [128, D], BF16) for i in range(2)]
                hnT2 = [sbt(ps, f"hnT{i}", [128, 8, CT], BF16) for i in range(2)]
                r_t = [sbt(ps, f"r{i}", [128, CT], F32) for i in range(2)]
                h1T = sbt(ps, "h1T", [128, 32, CT], BF16)
                tT = [pst(ps, f"tT{i}", [128, 1024], BF16) for i in range(2)]
                pu = [pst(ps, f"pu{i}", [128, 512], F32) for i in range(3)]
                pd = [pst(ps, f"pd{i}", [128, 512], F32) for i in range(3)]
                nchunks = (T + CT - 1) // CT

                def load_chunk(c):
                    r0 = c * CT
                    nb = min(2, (T - r0) // 128)
                    ht = h_t[c % 2]
                    S.dma("sp", lambda e: e.dma_start(
                        out=ht[:, 0:nb, :], in_=src[r0:r0 + nb * 128, :].rearrange("(b p) d -> p b d", p=128)), writes=[("h", c % 2)])

                load_chunk(0)
                if nchunks > 1:
                    load_chunk(1)
                load_weight(wst, wup, "wup", w_mlp_up[layer], D, DFF, gain=gcol, gname="gcol", col_major=True)
                load_weight(wst, wdn, "wdn", w_mlp_down[layer], DFF, D)
                ti = 0
                ui = 0
                di_ = 0
                tic = [0]

                def norm_chunk(c):
                    nb_ = min(2, (T - c * CT) // 128)
                    for b in range(nb_):
                        k = tic[0] % 2
                        tic[0] += 1
                        rmsnorm_T(h_t[c % 2][:, b, :], ("h", c % 2), hnT2[c % 2], ("hnT", c % 2), b * 128,
                                  (junk, ss, rstd, hn_bf[k], f"n{k}"), tT[k], ("tT", k))

                norm_chunk(0)
                for c in range(nchunks):
                    r0 = c * CT
                    nb = min(2, (T - r0) // 128)
                    ncols = nb * 128
                    ht = h_t[c % 2]
                    hname = ("h", c % 2)
                    if c >= 1 and c + 1 < nchunks:
                        load_chunk(c + 1)
                    hnT, hnTn = hnT2[c % 2], ("hnT", c % 2)
                    for fc in range(32):
                        k = ui % 3
                        ui += 1
                        for kc in range(8):
                            S.op("pe", lambda e, k=k, kc=kc, fc=fc, ncols=ncols, hnT=hnT: e.matmul(
                                pu[k][:, 0:ncols], wup[:, kc, fc * 128:(fc + 1) * 128], hnT[:, kc, 0:ncols],
                                start=(kc == 0), stop=(kc == 7)), [hnTn] + wr("wup", kc, fc * 128, fc * 128 + 128), [("pu", k)])
                        rr = fc % 2
                        S.op("act", lambda e, k=k, rr=rr, ncols=ncols: e.activation(r_t[rr][:, 0:ncols], pu[k][:, 0:ncols], AF.Relu),
                             [("pu", k)], [("r", rr)])
                        S.op("dve", lambda e, rr=rr, fc=fc, ncols=ncols: e.tensor_tensor(
                            h1T[:, fc, 0:ncols], r_t[rr][:, 0:ncols], r_t[rr][:, 0:ncols], ALU.mult),
                             [("r", rr)], [("h1T", fc)])
                    if c + 1 < nchunks:
                        norm_chunk(c + 1)
                    for b in range(nb):
                        for dh in range(2):
                            k = di_ % 3
                            di_ += 1
                            for fc in range(32):
                                S.op("pe", lambda e, k=k, fc=fc, b=b, dh=dh: e.matmul(
                                    pd[k][:, :], h1T[:, fc, b * 128:(b + 1) * 128], wdn[:, fc, dh * 512:(dh + 1) * 512],
                                    start=(fc == 0), stop=(fc == 31)), [("h1T", fc)] + wr("wdn", fc, dh * 512, dh * 512 + 512), [("pd", k)])
                            S.op("dve", lambda e, k=k, ht=ht, b=b, dh=dh: e.tensor_tensor(
                                ht[:, b, dh * 512:(dh + 1) * 512], pd[k][:, :], ht[:, b, dh * 512:(dh + 1) * 512], ALU.add),
                                 [("pd", k), hname], [hname])
                    S.dma("sp", lambda e, ht=ht, r0=r0, nb=nb: e.dma_start(
                        out=dst[r0:r0 + nb * 128, :].rearrange("(b p) d -> p b d", p=128), in_=ht[:, 0:nb, :]),
                          reads=[hname], writes=[("hrows", c)])
                S.barrier()

        def conv_phase(j, layer, is_last):
            src, dst = src_dst(is_last)
            CT = 256
            with contextlib.ExitStack() as ps:
                win = sbt(ps, "cwin", [128, 8, 3 * D], BF16)
                wout = sbt(ps, "cwout", [128, 8, D], BF16)
                gcol = sbt(ps, "gcol", [128, 8], F32)
                cw = sbt(ps, "cw", [128, 3, 8], F32)
                load_gain_cols(gcol, "gcol", g_mix[layer], D)
                for tap in range(3):
                    S.dma("sp", lambda e, tap=tap: e.dma_start(
                        out=cw[:, tap, :], in_=conv_w[j, tap].rearrange("(kc p) -> p kc", p=128),
                        allow_slow_non_contiguous=True), writes=[("cw", tap)])
                CW = [("cw", t) for t in range(3)]
                wst = [sbt(ps, f"wstg{i}", [128, NS], F32) for i in range(3)]
                h_t = [sbt(ps, f"h{i}", [128, 2, D], F32) for i in range(2)]
                junk = sbt(ps, "junk", [128, D], BF16)
                ss = sbt(ps, "ss", [128, 1], F32)
                rstd = sbt(ps, "rstd", [128, 1], F32)
                hn_bf = [sbt(ps, f"hnbf{i}", [128, D], BF16) for i in range(2)]
                hnT = sbt(ps, "hnT", [128, 8, CT], BF16)
                gext = sbt(ps, "gext", [128, 8, CT + 2], F32)
                c_sb = [sbt(ps, f"csb{i}", [128, CT], F32) for i in range(2)]
                b_sb = [sbt(ps, f"bsb{i}", [128, CT], F32) for i in range(2)]
                acc = [sbt(ps, f"acc{i}", [128, CT], F32) for i in range(2)]
                mT = sbt(ps, "mT", [128, 8, CT], BF16)
                tT = [pst(ps, f"tT{i}", [128, 1024], BF16) for i in range(2)]
                pz = [pst(ps, f"pz{i}", [128, 512], F32) for i in range(4)]
                pd = [pst(ps, f"pd{i}", [128, 512], F32) for i in range(2)]
                ghs = sbt(ps, "ghs", [128, 8, 2], F32)
                ghl = sbt(ps, "ghl", [128, 16], F32)
                hlt = sbt(ps, "hlt", [128, D], F32)

                def conv_halo():
                    conv_halo_body()

                def conv_halo_body():
                  if True:
                    rmsnorm_T(hlt[:], "hlt", hnT, "hnT", 0, (junk, ss, rstd, hn_bf[0], "n0"), tT[0], ("tT", 0))
                    for fc in range(8):
                        for sec in (1, 2):
                            col = sec * D + fc * 128
                            for kc in range(8):
                                S.op("pe", lambda e, sec=sec, kc=kc, col=col: e.matmul(
                                    pz[sec][:, 0:128], win[:, kc, col:col + 128], hnT[:, kc, 0:128],
                                    start=(kc == 0), stop=(kc == 7)), ["hnT"] + wr("cwin", kc, col, col + 128), [("pz", sec)])
                        S.op("act", lambda e: e.copy(c_sb[0][:, 0:128], pz[1][:, 0:128]), [("pz", 1)], [("csb", 0)])
                        S.op("dve", lambda e, fc=fc: e.tensor_tensor(ghs[:, fc, :], pz[2][:, 126:128], c_sb[0][:, 126:128], ALU.mult),
                             [("pz", 2), ("csb", 0)], ["ghs"])
                    S.dma("sp", lambda e: e.dma_start(out=gh_s.ap(), in_=ghs[:].rearrange("p f c -> p (f c)")), reads=["ghs"], writes=["gh_s"])
                    S.cc(lambda e: e.collective_compute("AllGather", ALU.bypass, replica_groups=RG,
                                                        ins=[gh_s.ap().opt()], outs=[gh_g.ap().opt()]), ["gh_s"], ["gh_g"])
                    S.dma("sp", lambda e: e.dma_start(out=ghl[:], in_=gh_g.ap()[0:128, :]), reads=["gh_g"], writes=["ghl"])
                    S.op("dve", lambda e: e.tensor_scalar(gext[:, :, 0:2], ghl[:].rearrange("p (f c) -> p f c", c=2), a1[:, 0:1], None, ALU.mult),
                         ["ghl", "a1"], [("gh", fc) for fc in range(8)])
                nchunks = (T + CT - 1) // CT

                def load_chunk(c):
                    r0 = c * CT
                    nb = min(2, (T - r0) // 128)
                    ht = h_t[c % 2]
                    S.dma("sp", lambda e: e.dma_start(
                        out=ht[:, 0:nb, :], in_=src[r0:r0 + nb * 128, :].rearrange("(b p) d -> p b d", p=128)), writes=[("h", c % 2)])

                load_chunk(0)
                if nchunks > 1:
                    load_chunk(1)
                S.dma("sp", lambda e: e.dma_start(out=hlt[:], in_=src[T - 128:T, :]), writes=["hlt"])
                load_weight(wst, win, "cwin", w_in_conv[j], D, 3 * D, gain=gcol, gname="gcol")
                load_weight(wst, wout, "cwout", w_out_conv[j], D, D)
                conv_halo()
                ti = zi = di_ = 0
                for c in range(nchunks):
                    r0 = c * CT
                    nb = min(2, (T - r0) // 128)
                    ncols = nb * 128
                    ht = h_t[c % 2]
                    hname = ("h", c % 2)
                    if c >= 1 and c + 1 < nchunks:
                        load_chunk(c + 1)
                    for b in range(nb):
                        k = ti % 2
                        ti += 1
                        rmsnorm_T(ht[:, b, :], hname, hnT, "hnT", b * 128,
                                  (junk, ss, rstd, hn_bf[k], f"n{k}"), tT[k], ("tT", k))
                    for fc in range(8):
                        zk = []
                        for sec in range(3):
                            k = zi % 4
                            zi += 1
                            zk.append(k)
                            col = sec * D + fc * 128
                            for kc in range(8):
                                S.op("pe", lambda e, k=k, kc=kc, col=col, ncols=ncols: e.matmul(
                                    pz[k][:, 0:ncols], win[:, kc, col:col + 128], hnT[:, kc, 0:ncols],
                                    start=(kc == 0), stop=(kc == 7)), ["hnT"] + wr("cwin", kc, col, col + 128), [("pz", k)])
                        q = fc % 2
                        S.op("act", lambda e, q=q, k=zk[0], ncols=ncols: e.copy(b_sb[q][:, 0:ncols], pz[k][:, 0:ncols]),
                             [("pz", zk[0])], [("bsb", q)])
                        S.op("act", lambda e, q=q, k=zk[1], ncols=ncols: e.copy(c_sb[q][:, 0:ncols], pz[k][:, 0:ncols]),
                             [("pz", zk[1])], [("csb", q)])
                        S.op("dve", lambda e, q=q, k=zk[2], fc=fc, ncols=ncols: e.tensor_tensor(
                            gext[:, fc, 2:2 + ncols], pz[k][:, 0:ncols], c_sb[q][:, 0:ncols], ALU.mult),
                             [("pz", zk[2]), ("csb", q)], [("g", fc)])
                        G = [("g", fc), ("gh", fc)] + CW
                        S.op("dve", lambda e, q=q, fc=fc, ncols=ncols: e.tensor_scalar(
                            acc[q][:, 0:ncols], gext[:, fc, 0:ncols], cw[:, 0, fc:fc + 1], None, ALU.mult), G, [("acc", q)])
                        S.op("dve", lambda e, q=q, fc=fc, ncols=ncols: e.scalar_tensor_tensor(
                            acc[q][:, 0:ncols], gext[:, fc, 1:1 + ncols], cw[:, 1, fc:fc + 1], acc[q][:, 0:ncols], ALU.mult, ALU.add),
                             G + [("acc", q)], [("acc", q)])
                        S.op("dve", lambda e, q=q, fc=fc, ncols=ncols: e.scalar_tensor_tensor(
                            acc[q][:, 0:ncols], gext[:, fc, 2:2 + ncols], cw[:, 2, fc:fc + 1], acc[q][:, 0:ncols], ALU.mult, ALU.add),
                             G + [("acc", q)], [("acc", q)])
                        S.op("dve", lambda e, q=q, fc=fc, ncols=ncols: e.tensor_tensor(
                            mT[:, fc, 0:ncols], acc[q][:, 0:ncols], b_sb[q][:, 0:ncols], ALU.mult),
                             [("acc", q), ("bsb", q)], [("mT", fc)])
                        S.op("act", lambda e, fc=fc, ncols=ncols: e.copy(gext[:, fc, 0:2], gext[:, fc, ncols:ncols + 2]),
                             [("g", fc)], [("gh", fc)])
                    for b in range(nb):
                        for dh in range(2):
                            k = di_ % 2
                            di_ += 1
                            for fc in range(8):
                                S.op("pe", lambda e, k=k, fc=fc, b=b, dh=dh: e.matmul(
                                    pd[k][:, :], mT[:, fc, b * 128:(b + 1) * 128], wout[:, fc, dh * 512:(dh + 1) * 512],
                                    start=(fc == 0), stop=(fc == 7)), [("mT", fc)] + wr("cwout", fc, dh * 512, dh * 512 + 512), [("pd", k)])
                            S.op("dve", lambda e, k=k, ht=ht, b=b, dh=dh: e.tensor_tensor(
                                ht[:, b, dh * 512:(dh + 1) * 512], pd[k][:, :], ht[:, b, dh * 512:(dh + 1) * 512], ALU.add),
                                 [("pd", k), hname], [hname])
                    S.dma("sp", lambda e, ht=ht, r0=r0, nb=nb: e.dma_start(
                        out=dst[r0:r0 + nb * 128, :].rearrange("(b p) d -> p b d", p=128), in_=ht[:, 0:nb, :]),
                          reads=[hname], writes=[("hrows", c)])
                S.barrier()

        def attn_phase(j, layer, is_last):
            src, dst = src_dst(is_last)
            with contextlib.ExitStack() as ps:
                win = sbt(ps, "awin", [128, 8, 2304], BF16)
                wuq = sbt(ps, "wuq", [128, 3, 768], BF16)
                wukv = sbt(ps, "wukv", [128, 2, 1024], BF16)
                gcol = sbt(ps, "gcol", [128, 8], F32)
                gq_c = sbt(ps, "gqc", [128, 3], F32)
                gkv_c = sbt(ps, "gkvc", [128, 2], F32)
                load_gain_cols(gcol, "gcol", g_mix[layer], D)
                load_gain_cols(gq_c, "gqc", g_cq[j], QL)
                load_gain_cols(gkv_c, "gkvc", g_ckv[j], KVL)
                gall = sbt(ps, "gall", [128, 336], F32)
                rowt = sbt(ps, "rowt", [1, 336], F32)
                ones_row = sbt(ps, "ones_row", [1, 128], F32)
                gqm, gkm, gqf, gkf, bfo = gall[:, 0:96], gall[:, 96:192], gall[:, 192:256], gall[:, 256:320], gall[:, 320:328]
                S.op("pool", lambda e: e.memset(ones_row[:], 1.0), [], ["ones_row"])
                S.op("pool", lambda e: e.memset(rowt[:], 0.0), [], ["rowt"])
                for off, n, apx in ((0, 96, g_q_mla[j:j + 1, :]), (96, 96, g_k_mla[j:j + 1, :]), (192, 64, g_q_fox[j:j + 1, :]),
                                    (256, 64, g_k_fox[j:j + 1, :]), (320, 8, b_forget[j:j + 1, :])):
                    S.dma("sp", lambda e, off=off, n=n, apx=apx: e.dma_start(out=rowt[0:1, off:off + n], in_=apx), reads=["rowt"], writes=["rowt"])
                with contextlib.ExitStack() as pbs:
                    pb = pst(pbs, "pb", [128, 512], F32)
                    S.op("pe", lambda e: e.matmul(pb[:, 0:336], ones_row[:], rowt[:], start=True, stop=True), ["ones_row", "rowt"], ["pb"])
                    S.op("dve", lambda e: e.tensor_copy(gall[:], pb[:, 0:336]), ["pb"], ["gqm", "gkm", "gqf", "gkf", "bfo"])
                    S.barrier()
                S.op("dve", lambda e: e.tensor_scalar(gqm[:], gqm[:], 96.0 ** -0.5, None, ALU.mult), ["gqm"], ["gqm"])
                S.op("dve", lambda e: e.tensor_scalar(gqf[:], gqf[:], 64.0 ** -0.5, None, ALU.mult), ["gqf"], ["gqf"])
                wst = [sbt(ps, f"wstg{i}", [128, NS], F32) for i in range(3)]
                h_t = [sbt(ps, f"h{i}", [128, D], F32) for i in range(2)]
                junk = sbt(ps, "junk", [128, D], BF16)
                ss = sbt(ps, "ss", [128, 1], F32)
                rstd = sbt(ps, "rstd", [128, 1], F32)
                hn_bf = sbt(ps, "hnbf", [128, D], BF16)
                hnT2 = [sbt(ps, f"hnT{i}", [128, 8, 128], BF16) for i in range(2)]
                z2 = [sbt(ps, f"z{i}", [128, ATTN_IN], F32) for i in range(2)]
                hn_bf2 = [hn_bf, sbt(ps, "hnbf1", [128, D], BF16)]
                ss1 = sbt(ps, "ss1", [128, 1], F32)
                rstd1 = sbt(ps, "rstd1", [128, 1], F32)
                cn_bf = sbt(ps, "cnbf", [128, 640], BF16)
                cnT = sbt(ps, "cnT", [128, 5, 128], BF16)
                ss8 = sbt(ps, "ss8", [128, 8], F32)
                sspe = sbt(ps, "sspe", [128, 1], F32)
                r8 = sbt(ps, "r8", [128, 8], F32)
                sq = sbt(ps, "sq", [128, 1024], F32)
                qn = sbt(ps, "qn", [128, 8, 96], F32)
                kpe = sbt(ps, "kpe", [128, 32], F32)
                kper = sbt(ps, "kper", [128, 32], F32)
                rtmp = sbt(ps, "rtmp", [128, 8, 16], F32)
                rtmp2 = sbt(ps, "rtmp2", [128, 8, 16], F32)
                qbf = sbt(ps, "qbf", [128, 8, 96], BF16)
                kbf = sbt(ps, "kbf", [128, 8, 96], BF16)
                fqbf = sbt(ps, "fqbf", [128, 8, 96], BF16)
                fkbf = sbt(ps, "fkbf", [128, 8, 96], BF16)
                hT_sb = [sbt(ps, f"hTsb{i}", [96, 8, 128], BF16) for i in range(4)]
                vaug = sbt(ps, "vaug", [128, 16, 80], BF16)
                lf = sbt(ps, "lf", [128, 8], F32)
                Fc = [sbt(ps, f"Fc{i}", [128, 8], F32) for i in range(2)]
                fr = sbt(ps, "fr", [128, 8], F32)
                fhi = sbt(ps, "fhi", [128, 8], BF16)
                fmid = sbt(ps, "fmid", [128, 8], BF16)
                tT = [pst(ps, f"tT{i}", [128, 1024], BF16) for i in range(2)]
                pz = [pst(ps, f"pz{i}", [128, 512], F32) for i in range(2)]
                pq = [pst(ps, f"pq{i}", [128, 512], F32) for i in range(2)]
                pkv = [pst(ps, f"pkv{i}", [128, 512], F32) for i in range(2)]
                S.op("pool", lambda e: e.memset(fqbf[:, :, 67:70], 1.0), [], ["fq1"])
                S.op("pool", lambda e: e.memset(fkbf[:, :, 64:67], 1.0), [], ["fk1"])
                S.op("pool", lambda e: e.memset(vaug[:, :, 64:80], 1.0), [], ["vones"])
                S.op("pool", lambda e: e.memset(Fc[1][:], 0.0), [], [("Fc", 1)])
                groups = [(0, 384), (384, 672), (672, 1184), (1184, 1696), (1696, 2208), (2208, 2216)]

                def bc8(ap2, n):
                    return ap2.unsqueeze(2).to_broadcast([128, 8, n])

                def bch(ap2, n):
                    return ap2.unsqueeze(1).to_broadcast([128, 8, n])

                def head_rstd(src3, n, extra=None, tag=""):
                    S.op("act", lambda e: e.activation(sq[:, 0:8 * n].rearrange("p (h d) -> p h d", d=n), src3, AF.Square),
                         [("z", 0), ("z", 1), "qn"], ["sq"])
                    S.op("dve", lambda e: e.reduce_sum(ss8[:], sq[:, 0:8 * n].rearrange("p (h d) -> p h d", d=n), axis=AX.X),
                         ["sq"], ["ss8"])
                    tot = n
                    if extra is not None:
                        tot = n + 32
                        S.op("dve", lambda e: e.tensor_scalar(ss8[:], ss8[:], sspe[:, 0:1], None, ALU.add), ["ss8", "sspe"], ["ss8"])
                    S.op("dve", lambda e: e.tensor_scalar(r8[:], ss8[:], 1.0 / tot, EPS, ALU.mult, ALU.add), ["ss8"], ["r8"])
                    S.op("act", lambda e: e.activation(r8[:], r8[:], AF.Ln), ["r8"], ["r8"])
                    S.op("act", lambda e: e.activation(r8[:], r8[:], AF.Exp, scale=-0.5), ["r8"], ["r8"])

                def transpose_heads(src_bf, sname, ncol, dstT, k, hk):
                    tt = tT[k]
                    for h in range(8):
                        S.op("pe", lambda e, h=h: e.transpose(tt[0:ncol, h * 128:(h + 1) * 128], src_bf[:, h, 0:ncol], ident[:]),
                             [sname, "ident"], [("tT", k)])
                    hs = hT_sb[hk]
                    S.op("dve", lambda e: e.tensor_copy(hs[0:ncol, :, :], tt[0:ncol, :].rearrange("p (h t) -> p h t", t=128)),
                         [("tT", k)], [("hTsb", hk)])
                    return hs

                def load_blk(bb):
                    tl = h_t[bb % 2]
                    S.dma("sp", lambda e: e.dma_start(out=tl[:], in_=src[bb * 128:(bb + 1) * 128, :]), writes=[("h", bb % 2)])

                load_blk(0)
                if nblk > 1:
                    load_blk(1)
                load_weight(wst, win, "awin", w_in_attn[j], D, ATTN_IN, gain=gcol, gname="gcol")
                load_weight(wst, wuq, "wuq", w_uq[j], QL, 768, gain=gq_c, gname="gqc")
                load_weight(wst, wukv, "wukv", w_ukv[j], KVL, 1024, gain=gkv_c, gname="gkvc")
                def stage1(blk):
                    z, zn = z2[blk % 2], ("z", blk % 2)
                    hnT, hnTn = hnT2[blk % 2], ("hnT", blk % 2)
                    r0 = blk * 128
                    ht = h_t[blk % 2]
                    hname = ("h", blk % 2)
                    if blk >= 1 and blk + 1 < nblk:
                        load_blk(blk + 1)
                    rmsnorm_T(ht[:], hname, hnT, hnTn, 0, (junk, ss1, rstd1, hn_bf2[blk % 2], f"n{blk % 2}"), tT[0], ("tT", 0), pre="s1")
                    for gi, (c0, c1) in enumerate(groups):
                        k = gi % 2
                        for kc in range(8):
                            S.op("pe", lambda e, k=k, kc=kc, c0=c0, c1=c1: e.matmul(
                                pz[k][:, 0:c1 - c0], hnT[:, kc, :], win[:, kc, c0:c1], start=(kc == 0), stop=(kc == 7)),
                                 [hnTn] + wr("awin", kc, c0, c1), [("pz", k)])
                        if gi % 2 == 0:
                            S.op("act", lambda e, k=k, c0=c0, c1=c1: e.copy(z[:, c0:c1], pz[k][:, 0:c1 - c0]), [("pz", k)], [zn])
                        else:
                            S.op("dve", lambda e, k=k, c0=c0, c1=c1: e.tensor_copy(z[:, c0:c1], pz[k][:, 0:c1 - c0]), [("pz", k)], [zn])
                    if debug:
                        S.dma("sp", lambda e, blk=blk: e.dma_start(out=zdbg[blk], in_=z[:, :]), reads=[zn], writes=[("zdbg", blk)])
                def stage2(blk):
                    r0 = blk * 128
                    z, zn = z2[blk % 2], ("z", blk % 2)
                    for (c0, w_, o0) in ((0, QL, 0), (QL, KVL, QL)):
                        S.op("act", lambda e, c0=c0, w_=w_: e.activation(sq[:, 0:w_], z[:, c0:c0 + w_], AF.Square, accum_out=ss[:, 0:1]),
                             [zn], ["sq", "ss"])
                        S.op("dve", lambda e, w_=w_: e.tensor_scalar(rstd[:, 0:1], ss[:, 0:1], 1.0 / w_, EPS, ALU.mult, ALU.add),
                             ["ss"], ["rstd"])
                        S.op("act", lambda e: e.activation(rstd[:, 0:1], rstd[:, 0:1], AF.Ln), ["rstd"], ["rstd"])
                        S.op("act", lambda e: e.activation(rstd[:, 0:1], rstd[:, 0:1], AF.Exp, scale=-0.5), ["rstd"], ["rstd"])
                        S.op("act", lambda e, c0=c0, w_=w_: e.activation(cn_bf[:, c0:c0 + w_], z[:, c0:c0 + w_], AF.Copy, scale=rstd[:, 0:1]),
                             [zn, "rstd"], ["cnbf"])
                    for kc in range(5):
                        S.op("pe", lambda e, kc=kc: e.transpose(tT[1][:, kc * 128:(kc + 1) * 128], cn_bf[:, kc * 128:(kc + 1) * 128], ident[:]),
                             ["cnbf", "ident"], [("tT", 1)])
                    S.op("dve", lambda e: e.tensor_copy(cnT[:, :, :], tT[1][:, 0:640].rearrange("p (k t) -> p k t", t=128)),
                         [("tT", 1)], ["cnT"])
                    for half, (n0, n1) in enumerate(((0, 512), (512, 768))):
                        for kc in range(3):
                            S.op("pe", lambda e, half=half, kc=kc, n0=n0, n1=n1: e.matmul(
                                pq[half][:, 0:n1 - n0], cnT[:, kc, :], wuq[:, kc, n0:n1], start=(kc == 0), stop=(kc == 2)),
                                 ["cnT"] + wr("wuq", kc, n0, n1), [("pq", half)])
                    for half in range(2):
                        for kc in range(2):
                            S.op("pe", lambda e, half=half, kc=kc: e.matmul(
                                pkv[half][:, :], cnT[:, 3 + kc, :], wukv[:, kc, half * 512:(half + 1) * 512],
                                start=(kc == 0), stop=(kc == 1)), ["cnT"] + wr("wukv", kc, half * 512, half * 512 + 512), [("pkv", half)])
                    qflat = qn[:].rearrange("p h d -> p (h d)")
                    S.op("act", lambda e: e.copy(qflat[:, 0:512], pq[0][:, :]), [("pq", 0)], ["qn"])
                    S.op("dve", lambda e: e.tensor_copy(qflat[:, 512:768], pq[1][:, 0:256]), [("pq", 1)], ["qn"])
                    head_rstd(qn[:, :, :], 96)
                    S.op("dve", lambda e: e.tensor_tensor(qn[:, :, :], qn[:, :, :], bc8(r8[:], 96), ALU.mult), ["qn", "r8"], ["qn"])
                    S.op("dve", lambda e: e.tensor_tensor(qn[:, :, :], qn[:, :, :], bch(gqm[:], 96), ALU.mult), ["qn", "gqm"], ["qn"])
                    S.op("act", lambda e: e.copy(qbf[:, :, 0:64], qn[:, :, 0:64]), ["qn"], ["qbf"])
                    cosb = bch(cos_t[:, blk, :], 16)
                    sinb = bch(sin_t[:, blk, :], 16)
                    S.op("pool", lambda e, sinb=sinb: e.tensor_tensor(rtmp[:, :, :], qn[:, :, 80:96], sinb, ALU.mult), ["qn", "sin"], ["rtmp"])
                    S.op("pool", lambda e, cosb=cosb: e.tensor_tensor(rtmp2[:, :, :], qn[:, :, 64:80], cosb, ALU.mult), ["qn", "cos"], ["rtmp2"])
                    S.op("pool", lambda e: e.tensor_tensor(qbf[:, :, 64:80], rtmp2[:, :, :], rtmp[:, :, :], ALU.subtract), ["rtmp2", "rtmp"], ["qbf"])
                    S.op("pool", lambda e, sinb=sinb: e.tensor_tensor(rtmp[:, :, :], qn[:, :, 64:80], sinb, ALU.mult), ["qn", "sin"], ["rtmp"])
                    S.op("pool", lambda e, cosb=cosb: e.tensor_tensor(rtmp2[:, :, :], qn[:, :, 80:96], cosb, ALU.mult), ["qn", "cos"], ["rtmp2"])
                    S.op("pool", lambda e: e.tensor_tensor(qbf[:, :, 80:96], rtmp2[:, :, :], rtmp[:, :, :], ALU.add), ["rtmp2", "rtmp"], ["qbf"])
                    hs = transpose_heads(qbf, "qbf", 96, qt_d, 0, 0)
                    S.dma("sp", lambda e, hs=hs, r0=r0: e.dma_start(out=qt_d[0:8, :, r0:r0 + 128].rearrange("h d t -> d h t"), in_=hs[:, :, :]),
                          reads=[("hTsb", 0)], writes=[("qt", blk)])
                    kv3 = [pkv[hf][:, :].rearrange("p (h d) -> p h d", d=128) for hf in range(2)]
                    S.op("act", lambda e: e.activation(kper[:, :], z[:, 640:672], AF.Square, accum_out=sspe[:, 0:1]), [zn], ["kper", "sspe"])
                    S.op("dve", lambda e: e.tensor_tensor(kpe[:, :], z[:, 640:672], gkm[:, 64:96], ALU.mult), [zn, "gkm"], ["kpe"])
                    c2, s2 = cos_t[:, blk, :], sin_t[:, blk, :]
                    S.op("pool", lambda e, s2=s2: e.tensor_tensor(rtmp[:, 0, :], kpe[:, 16:32], s2, ALU.mult), ["kpe", "sin"], ["rtmp"])
                    S.op("pool", lambda e, c2=c2: e.tensor_tensor(rtmp2[:, 0, :], kpe[:, 0:16], c2, ALU.mult), ["kpe", "cos"], ["rtmp2"])
                    S.op("pool", lambda e: e.tensor_tensor(kper[:, 0:16], rtmp2[:, 0, :], rtmp[:, 0, :], ALU.subtract), ["rtmp2", "rtmp", "sspe"], ["kper"])
                    S.op("pool", lambda e, s2=s2: e.tensor_tensor(rtmp[:, 0, :], kpe[:, 0:16], s2, ALU.mult), ["kpe", "sin"], ["rtmp"])
                    S.op("pool", lambda e, c2=c2: e.tensor_tensor(rtmp2[:, 0, :], kpe[:, 16:32], c2, ALU.mult), ["kpe", "cos"], ["rtmp2"])
                    S.op("pool", lambda e: e.tensor_tensor(kper[:, 16:32], rtmp2[:, 0, :], rtmp[:, 0, :], ALU.add), ["rtmp2", "rtmp"], ["kper"])
                    for hf in range(2):
                        S.op("act", lambda e, hf=hf: e.copy(qn[:, hf * 4:(hf + 1) * 4, 0:64], kv3[hf][:, :, 0:64]), [("pkv", hf)], ["qn"])
                        S.op("dve", lambda e, hf=hf: e.tensor_copy(vaug[:, hf * 4:(hf + 1) * 4, 0:64], kv3[hf][:, :, 64:128]), [("pkv", hf)], ["vaug"])
                    head_rstd(qn[:, :, 0:64], 64, extra=True)
                    S.op("dve", lambda e: e.tensor_tensor(qn[:, :, 0:64], qn[:, :, 0:64], bc8(r8[:], 64), ALU.mult), ["qn", "r8"], ["qn"])
                    S.op("dve", lambda e: e.tensor_tensor(kbf[:, :, 0:64], qn[:, :, 0:64], bch(gkm[:, 0:64], 64), ALU.mult), ["qn", "gkm"], ["kbf"])
                    S.op("dve", lambda e: e.tensor_tensor(kbf[:, :, 64:96], bch(kper[:, :], 32), bc8(r8[:], 32), ALU.mult), ["kper", "r8"], ["kbf"])
                    hs = transpose_heads(kbf, "kbf", 96, kt_d, 1, 1)
                    S.dma("sp", lambda e, hs=hs, r0=r0: e.dma_start(out=kt_d[0:8, :, r0:r0 + 128].rearrange("h d t -> d h t"), in_=hs[:, :, :]),
                          reads=[("hTsb", 1)], writes=[("kt", blk)])
                    S.op("dve", lambda e: e.tensor_tensor(lf[:], z[:, 2208:2216], bfo[:], ALU.add), [zn, "bfo"], ["lf"])
                    S.op("act", lambda e: e.activation(lf[:], lf[:], AF.Exp, scale=-1.0), ["lf"], ["lf"])
                    S.op("dve", lambda e: e.tensor_scalar(lf[:], lf[:], 1.0, None, ALU.add), ["lf"], ["lf"])
                    S.op("act", lambda e: e.activation(lf[:], lf[:], AF.Ln), ["lf"], ["lf"])
                    if blk == 0:
                        S.op("dve", lambda e: e.tensor_scalar(lf[:], lf[:], rowmask[:, 0:1], None, ALU.mult), ["lf", "rowmask"], ["lf"])
                    fcur, fprev = Fc[blk % 2], Fc[(blk + 1) % 2]
                    S.op("pe", lambda e: e.matmul(pq[1][:, 256:264], tri_f[:], lf[:], start=True, stop=False),
                         ["tri_f", "lf", ("pq", 1)], [("pq", 1)])
                    S.op("pe", lambda e, fprev=fprev: e.matmul(pq[1][:, 256:264], e127[:], fprev[:], start=False, stop=True),
                         ["e127", ("Fc", (blk + 1) % 2)], [("pq", 1)])
                    S.op("act", lambda e, fcur=fcur: e.copy(fcur[:], pq[1][:, 256:264]), [("pq", 1)], [("Fc", blk % 2)])
                    FN = ("Fc", blk % 2)
                    for (dstb, c0, sgn, nm) in ((fqbf, 64, -1.0, "fqbf"), (fkbf, 67, 1.0, "fkbf")):
                        S.op("pool", lambda e, sgn=sgn, fcur=fcur: e.tensor_scalar(fr[:], fcur[:], sgn, None, ALU.mult), [FN], ["fr"])
                        S.op("pool", lambda e: e.tensor_copy(fhi[:], fr[:]), ["fr"], ["fhi"])
                        S.op("pool", lambda e: e.tensor_tensor(fr[:], fr[:], fhi[:], ALU.subtract), ["fr", "fhi"], ["fr"])
                        S.op("pool", lambda e: e.tensor_copy(fmid[:], fr[:]), ["fr"], ["fmid"])
                        S.op("pool", lambda e: e.tensor_tensor(fr[:], fr[:], fmid[:], ALU.subtract), ["fr", "fmid"], ["fr"])
                        S.op("pool", lambda e, dstb=dstb, c0=c0: e.tensor_copy(dstb[:, :, c0:c0 + 1], fhi[:].unsqueeze(2)), ["fhi"], [nm])
                        S.op("pool", lambda e, dstb=dstb, c0=c0: e.tensor_copy(dstb[:, :, c0 + 1:c0 + 2], fmid[:].unsqueeze(2)), ["fmid"], [nm])
                        S.op("pool", lambda e, dstb=dstb, c0=c0: e.tensor_copy(dstb[:, :, c0 + 2:c0 + 3], fr[:].unsqueeze(2)), ["fr"], [nm])
                    for (c0, gt, gname, dstb, nm, dT, hk) in ((672, gqf, "gqf", fqbf, "fqbf", qt_d, 2), (1184, gkf, "gkf", fkbf, "fkbf", kt_d, 3)):
                        src3 = z[:, c0:c0 + 512].rearrange("p (h d) -> p h d", d=64)
                        head_rstd(src3, 64)
                        S.op("dve", lambda e, src3=src3: e.tensor_tensor(qn[:, :, 0:64], src3, bc8(r8[:], 64), ALU.mult), [zn, "r8"], ["qn"])
                        S.op("dve", lambda e, gt=gt, dstb=dstb: e.tensor_tensor(dstb[:, :, 0:64], qn[:, :, 0:64], bch(gt[:], 64), ALU.mult),
                             ["qn", gname], [nm])
                        hs = transpose_heads(dstb, nm, 70, dT, hk % 2, hk)
                        S.dma("sp", lambda e, hs=hs, r0=r0, dT=dT: e.dma_start(
                            out=dT[8:16, 0:70, r0:r0 + 128].rearrange("h d t -> d h t"), in_=hs[0:70, :, :]),
                              reads=[("hTsb", hk), "fq1", "fk1"], writes=[("qkt", hk, blk)])
                    S.op("act", lambda e: e.copy(vaug[:, 8:16, 0:64], z[:, 1696:2208].rearrange("p (h d) -> p h d", d=64)), [zn], ["vaug"])
                    if blk == 0:
                        S.op("dve", lambda e: e.tensor_scalar(vaug[:].rearrange("p h c -> p (h c)"), vaug[:].rearrange("p h c -> p (h c)"),
                                                              rowmask[:, 0:1], None, ALU.mult), ["vaug", "vones", "rowmask"], ["vaug", "vones"])
                    S.dma("sp", lambda e, blk=blk: e.dma_start(out=v_d[:, :, blk, :].rearrange("h p c -> p h c"), in_=vaug[:, :, :]),
                          reads=["vaug", "vones"], writes=[("vd", blk)])
                    if blk == 0:
                        S.op("pool", lambda e: e.memset(vaug[:, :, 64:80], 1.0), ["vaug", "vones"], ["vones"])
                stage1(0)
                for blk in range(nblk):
                    if "nodefer" not in SKIP:
                        S.begin_defer()
                    if blk + 1 < nblk:
                        stage1(blk + 1)
                    stage2(blk)
                    if "nodefer" not in SKIP:
                        S.end_defer()
                flast = Fc[(nblk - 1) % 2]
                S.op("pe", lambda e: e.matmul(pq[1][:, 256:264], e127[:], flast[:], start=True, stop=True),
                     ["e127", ("Fc", (nblk - 1) % 2), ("pq", 1)], [("pq", 1)])
                S.op("dve", lambda e: e.tensor_scalar(fr[:], pq[1][:, 256:264], -1.0, None, ALU.mult), [("pq", 1)], ["fr"])
                S.dma("sp", lambda e: e.dma_start(out=tot_s.ap(), in_=fr[:]), reads=["fr"], writes=["tot_s"])
                S.barrier()
            S.cc(lambda e: e.collective_compute("AllGather", ALU.bypass, replica_groups=RG,
                                                ins=[tot_s.ap().opt()], outs=[tot_g.ap().opt()]), [], ["totg"])

            def gather_pair(g):
                S.cc(lambda e: e.collective_compute("AllGather", ALU.bypass, replica_groups=RG,
                                                    ins=[kt2.ap()[g * 192:(g + 1) * 192, :].opt()],
                                                    outs=[ktg2.ap()[g * 384:(g + 1) * 384, :].opt()]), [], [("ktg", g)])
                S.cc(lambda e: e.collective_compute("AllGather", ALU.bypass, replica_groups=RG,
                                                    ins=[v2.ap()[g * 256:(g + 1) * 256, :].opt()],
                                                    outs=[vg2.ap()[g * 512:(g + 1) * 512, :].opt()]), [], [("vg", g)])

            gather_pair(0)
            gather_pair(1)

            with contextlib.ExitStack() as ps:
                KT = [sbt(ps, f"KT{i}", [96, T], BF16) for i in range(2)]
                QT = [sbt(ps, f"QT{i}", [96, T], BF16) for i in range(2)]
                VA = [sbt(ps, f"VA{i}", [128, nblk, 80], BF16) for i in range(2)]
                KTo = [sbt(ps, f"KTo{i}", [96, T], BF16) for i in range(2)]
                VAo = [sbt(ps, f"VAo{i}", [128, nblk, 80], BF16) for i in range(2)]
                negc0 = sbt(ps, "negc0", [128, 8], F32)
                pT = [sbt(ps, f"pT{i}", [128, 1024], BF16) for i in range(4)]
                xsb = [sbt(ps, f"xsb{i}", [65, 512], F32) for i in range(2)]
                rec = sbt(ps, "rec", [64, 512], F32)
                oTs = [sbt(ps, f"oTs{i}", [64, 512], BF16) for i in range(2)]
                sel = sbt(ps, "sel", [65, 64], F32)
                S.op("pool", lambda e: e.memset(sel[:], 0.0), [], ["sel"])
                S.op("pool", lambda e: e.memset(sel[64:65, :], 1.0), ["sel"], ["sel"])
                pss = [pst(ps, f"ps{i}", [128, 1024], F32) for i in range(3)]
                po = [pst(ps, f"po{i}", [128, 512], F32) for i in range(1)]
                pbc = pst(ps, "pbc", [64, 512], F32)
                S.dma("sp", lambda e: e.dma_start(out=negc0[:], in_=tot_g.ap()[0:128, :]), reads=["totg"], writes=["negc0"])

                def load_head(h):
                    dk = 96 if h < 8 else 70
                    b = h % 2
                    if h % 2 == 0 and h // 2 + 1 < 8 and h >= 2:
                        gather_pair(h // 2 + 1)
                    S.dma("sp", lambda e: e.dma_start(out=KT[b][0:dk, :], in_=kt_d[h, 0:dk, :]), writes=[("KT", b)])
                    S.dma("sp", lambda e: e.dma_start(out=QT[b][0:dk, :], in_=qt_d[h, 0:dk, :]), writes=[("QT", b)])
                    S.dma("sp", lambda e: e.dma_start(out=VA[b][:, :, :], in_=v_d[h]), writes=[("VA", b)])
                    S.dma("sp", lambda e: e.dma_start(out=KTo[b][0:dk, :], in_=ktg_h(h)[0:dk, :]), reads=[("ktg", h // 2)], writes=[("KTo", b)])
                    S.dma("sp", lambda e: e.dma_start(out=VAo[b][:, :, :], in_=vg_h(h)), reads=[("vg", h // 2)], writes=[("VAo", b)])
                    S.op("dve", lambda e: e.tensor_scalar(KTo[b][0:dk, :], KTo[b][0:dk, :], a1[0:dk, 0:1], None, ALU.mult),
                         [("KTo", b), "a1"], [("KTo", b)])
                    S.op("dve", lambda e: e.tensor_scalar(VAo[b][:].rearrange("p b c -> p (b c)"), VAo[b][:].rearrange("p b c -> p (b c)"),
                                                           a1[:, 0:1], None, ALU.mult), [("VAo", b), "a1"], [("VAo", b)])

                groups = []
                for h in range(16):
                    for qb0 in range(0, nblk, 4):
                        nq = min(4, nblk - qb0)
                        tiles = [("o", t) for t in range(nblk)] + [("k", kt) for kt in range(qb0)]
                        for a in range(0, len(tiles) - 1, 2):
                            if tiles[a][0] == tiles[a + 1][0]:
                                groups.append((h, qb0, nq, [tiles[a], tiles[a + 1]]))
                            else:
                                groups.append((h, qb0, nq, [tiles[a]]))
                                groups.append((h, qb0, nq, [tiles[a + 1]]))
                        if len(tiles) % 2:
                            groups.append((h, qb0, nq, [tiles[-1]]))
                        for kt in range(qb0, qb0 + nq):
                            groups.append((h, qb0, nq, [("k", kt)]))
                oi = [0]

                def front(gi):
                    h, qb0, nq, tiles = groups[gi]
                    dk = 96 if h < 8 else 70
                    b = h % 2
                    sp_, pk = gi % 3, gi % 4
                    kind0, kt0 = tiles[0]
                    first = (max(kt0, qb0) - qb0) if kind0 == "k" else 0
                    c0, c1 = first * 128, nq * 128
                    for jx, (kind, kt) in enumerate(tiles):
                        src_k = KTo[b] if kind == "o" else KT[b]
                        rk = ("KTo", b) if kind == "o" else ("KT", b)
                        S.op("pe", lambda e, jx=jx, kt=kt, src_k=src_k: e.matmul(
                            pss[sp_][:, jx * 512 + c0:jx * 512 + c1], src_k[0:dk, kt * 128:(kt + 1) * 128],
                            QT[b][0:dk, qb0 * 128 + c0:qb0 * 128 + c1], start=True, stop=True), [rk, ("QT", b)], [("ps", sp_)])
                    nt = len(tiles)
                    if nt == 2:
                        o_ap = pT[pk][:, :].rearrange("p (j c) -> p j c", j=2)[:, :, c0:c1]
                        i_ap = pss[sp_][:, :].rearrange("p (j c) -> p j c", j=2)[:, :, c0:c1]
                    else:
                        o_ap, i_ap = pT[pk][:, c0:c1], pss[sp_][:, c0:c1]
                    if kind0 == "o" and h >= 8:
                        S.op("act", lambda e: e.activation(o_ap, i_ap, AF.Exp, bias=negc0[:, h - 8:h - 7]),
                             [("ps", sp_), "negc0"], [("pT", pk)])
                    else:
                        S.op("act", lambda e: e.activation(o_ap, i_ap, AF.Exp), [("ps", sp_)], [("pT", pk)])
                    if kind0 == "k" and kt0 >= qb0:
                        S.op("dve", lambda e: e.tensor_tensor(
                            pT[pk][:, c0:c0 + 128], pT[pk][:, c0:c0 + 128], tri_bf[:], ALU.mult), [("pT", pk), "tri_bf"], [("pT", pk)])

                def back(gi):
                    h, qb0, nq, tiles = groups[gi]
                    b = h % 2
                    pk = gi % 4
                    cb = 0
                    c1 = nq * 128
                    for jx, (kind, kt) in enumerate(tiles):
                        if kind == "o":
                            S.op("pe", lambda e, jx=jx, kt=kt: e.matmul(po[cb][0:65, 0:c1], VAo[b][:, kt, 0:65], pT[pk][:, jx * 512:jx * 512 + c1],
                                                                        start=(kt == 0), stop=False), [("pT", pk), ("VAo", b)], [("po", cb)])
                            continue
                        first = max(kt, qb0) - qb0
                        c0 = first * 128
                        last = kt == qb0 + nq - 1
                        S.op("pe", lambda e, jx=jx, kt=kt, c0=c0, last=last: e.matmul(
                            po[cb][0:65, c0:c1], VA[b][:, kt, 0:65], pT[pk][:, jx * 512 + c0:jx * 512 + c1],
                            start=False, stop=last), [("pT", pk), ("VA", b)], [("po", cb)])
                        if last:
                            x = oi[0] % 2
                            oi[0] += 1
                            S.op("dve", lambda e: e.tensor_copy(xsb[x][0:65, 0:c1], po[cb][0:65, 0:c1]), [("po", cb)], [("xsb", x)])
                            S.op("pe", lambda e: e.matmul(pbc[0:64, 0:c1], sel[0:65, 0:64], xsb[x][0:65, 0:c1], start=True, stop=True),
                                 ["sel", ("xsb", x)], ["pbc"])
                            S.op("dve", lambda e: e.tensor_scalar(rec[0:64, 0:c1], pbc[0:64, 0:c1], 1e-30, None, ALU.max), ["pbc"], ["rec"])
                            S.op("dve", lambda e: e.reciprocal(rec[0:64, 0:c1], rec[0:64, 0:c1]), ["rec"], ["rec"])
                            S.op("dve", lambda e: e.tensor_tensor(oTs[x][0:64, 0:c1], xsb[x][0:64, 0:c1], rec[0:64, 0:c1], ALU.mult),
                                 [("xsb", x), "rec"], [("oTs", x)])
                            S.dma("sp", lambda e: e.dma_start(out=oT_d[h * 64:(h + 1) * 64, qb0 * 128:qb0 * 128 + c1], in_=oTs[x][0:64, 0:c1]),
                                  reads=[("oTs", x)], writes=[("od", h, qb0)])
                    if gi + 1 == len(groups) or groups[gi + 1][0] != h:
                        if h + 2 < 16:
                            load_head(h + 2)

                if "a3" in SKIP:
                    groups = []
                else:
                    load_head(0)
                    load_head(1)
                LA = 2
                for gi in range(len(groups) + LA):
                    if gi < len(groups):
                        front(gi)
                    if gi - LA >= 0:
                        back(gi - LA)
                S.barrier()

            with contextlib.ExitStack() as ps:
                wout = sbt(ps, "awout", [128, 8, D], BF16)
                wst = [sbt(ps, f"wstg{i}", [128, NS], F32) for i in range(3)]
                h4_t = [sbt(ps, f"h{i}", [128, D], F32) for i in range(2)]
                o_t = [sbt(ps, f"o{i}", [128, D], BF16) for i in range(2)]
                oT = [sbt(ps, f"oT{i}", [128, 8, 128], BF16) for i in range(2)]
                tT4 = [pst(ps, f"tT4{i}", [128, 1024], BF16) for i in range(2)]
                pd = [pst(ps, f"pd{i}", [128, 512], F32) for i in range(4)]
                def load4(bb):
                    S.dma("sp", lambda e: e.dma_start(out=h4_t[bb % 2][:], in_=src[bb * 128:(bb + 1) * 128, :]), writes=[("h", bb % 2)])
                    S.dma("sp", lambda e: e.dma_start(out=oT[bb % 2][:, :, :],
                                                      in_=oT_d[:, bb * 128:(bb + 1) * 128].rearrange("(c p) t -> p c t", p=128)),
                          writes=[("oT", bb % 2)])

                load4(0)
                if nblk > 1:
                    load4(1)
                load_weight(wst, wout, "awout", w_out_attn[j], D, D)
                for blk in range(nblk):
                    r0 = blk * 128
                    k = blk % 2
                    if blk >= 1 and blk + 1 < nblk:
                        load4(blk + 1)
                    for dh in range(2):
                        pk = (blk * 2 + dh) % 4
                        for kc in range(8):
                            S.op("pe", lambda e, k=k, kc=kc, dh=dh, pk=pk: e.matmul(
                                pd[pk][:, :], oT[k][:, kc, :], wout[:, kc, dh * 512:(dh + 1) * 512], start=(kc == 0), stop=(kc == 7)),
                                 [("oT", k)] + wr("awout", kc, dh * 512, dh * 512 + 512), [("pd", pk)])
                        S.op("dve", lambda e, k=k, dh=dh, pk=pk: e.tensor_tensor(
                            h4_t[k][:, dh * 512:(dh + 1) * 512], pd[pk][:, :], h4_t[k][:, dh * 512:(dh + 1) * 512], ALU.add),
                             [("pd", pk), ("h", k)], [("h", k)])
                    S.dma("sp", lambda e, k=k, r0=r0: e.dma_start(out=dst[r0:r0 + 128, :], in_=h4_t[k][:]), reads=[("h", k)], writes=[("hrows", blk)])
                S.barrier()

        li = {"attn": 0, "conv": 0}
        for pi, ph in enumerate(layers):
            is_last = pi == len(layers) - 1
            layer = pi // 2
            if ph == "mlp":
                mlp_phase(layer, is_last)
            elif ph == "conv":
                conv_phase(li["conv"], layer, is_last)
                li["conv"] += 1
            else:
                attn_phase(li["attn"], layer, is_last)
                li["attn"] += 1
        S.barrier()
        S.emit()
        print("n_ops", S.n_ops)
    return nc


def host_consts():
    tri = np.triu(np.ones((128, 128), np.float32))
    e127 = np.zeros((128, 128), np.float32)
    e127[127, :] = 1.0
    return {"ident": np.eye(128, dtype=np.float32), "tri": tri, "e127": e127}


def rope_tables(nblk, rank):
    T = nblk * 128
    pos = (np.arange(T, dtype=np.float32) + np.float32(rank * T - PAD)).astype(np.float32)
    inv_freq = (np.float32(10000.0) ** (-np.arange(0, ROPE, 2, dtype=np.float32) / np.float32(ROPE))).astype(np.float32)
    ang = (pos[:, None] * inv_freq[None, :]).astype(np.float32)
    cos = np.cos(ang).astype(np.float32).reshape(nblk, 128, 16).transpose(1, 0, 2)
    sin = np.sin(ang).astype(np.float32).reshape(nblk, 128, 16).transpose(1, 0, 2)
    return np.ascontiguousarray(cos), np.ascontiguousarray(sin)


_WNAMES = ["g_mix", "g_mlp", "w_in_attn", "g_cq", "w_uq", "g_ckv", "w_ukv", "g_q_mla", "g_k_mla", "g_q_fox",
           "g_k_fox", "b_forget", "w_out_attn", "w_in_conv", "conv_w", "w_out_conv", "w_mlp_up", "w_mlp_down"]


def run(inputs, layers=("attn", "mlp", "conv", "mlp", "attn", "mlp", "conv", "mlp"), n_cores=8, debug=False, trace=False):
    x = np.asarray(inputs["x"], np.float32)
    B, SEQ, _ = x.shape
    L = NMETA + SEQ
    nfull = (PAD + L) // 128
    assert nfull * 128 == PAD + L
    nblk = (nfull + 1) // 2
    T = nblk * 128
    nc = build_program(nblk, layers, debug, n_cores)
    consts = host_consts()
    meta = np.asarray(inputs["meta_tokens"], np.float32)
    weights = {n: np.ascontiguousarray(np.asarray(inputs[n], np.float32)) for n in _WNAMES}
    in_maps = []
    for c in range(n_cores):
        b, rank = (c // 2) % B, c % 2
        xp = np.zeros((2 * T, D), np.float32)
        xp[PAD:PAD + NMETA] = meta
        xp[PAD + NMETA:PAD + L] = x[b]
        cos, sin = rope_tables(nblk, rank)
        rowmask = np.ones((128, 1), np.float32)
        if rank == 0:
            rowmask[:PAD] = 0.0
        m = {"x_pad": np.ascontiguousarray(xp[rank * T:(rank + 1) * T]), "cos_t": cos, "sin_t": sin,
             "a1": np.full((128, 1), float(rank), np.float32), "rowmask0": rowmask}
        m.update(weights)
        m.update(consts)
        in_maps.append(m)
    res = run_bass_kernel_spmd(nc, in_maps, core_ids=list(range(n_cores)), **({"trace": True} if trace else {}))
    if trace:
        print("EXEC_TIME_NS", res.exec_time_ns)
    out = np.zeros((B, SEQ, D), np.float32)
    for b in range(min(B, n_cores // 2)):
        full = np.concatenate([res.results[2 * b]["y"], res.results[2 * b + 1]["y"]], axis=0)
        out[b] = full[PAD + NMETA:PAD + L]
    return out


def kernel(**inputs):
    return run(inputs)
```
